# Optimizing a Trainium2 kernel written in Bass

```python
import math
import jax, jax.numpy as jnp
from jax import lax
import numpy as np

D_MODEL = 2048
BATCH = 8
SEQ = 4096
DEPTH = 4

CONV_WIDTH = D_MODEL // 2
CONV_KERNEL = 31
SSM_WIDTH = D_MODEL // 2
SSM_GROUP = 16
SSM_GROUPS = SSM_WIDTH // SSM_GROUP
SSM_STATE = 64
D_FF = 4 * D_MODEL
IN_COLS = 2 * CONV_WIDTH + SSM_WIDTH + 2 * D_MODEL
RMS_EPS = 1e-6
LN_EPS = 1e-5

kernel_name = "hybrid_conformer_s5_gated_block"


def _rmsnorm(x, g):
    xf = x.astype(jnp.float32)
    xf = xf * lax.rsqrt(jnp.mean(xf * xf, axis=-1, keepdims=True) + RMS_EPS)
    return (xf * g.astype(jnp.float32)).astype(x.dtype)


def _layernorm(x, g, b):
    xf = x.astype(jnp.float32)
    mu = jnp.mean(xf, axis=-1, keepdims=True)
    xc = xf - mu
    var = jnp.mean(xc * xc, axis=-1, keepdims=True)
    y = xc * lax.rsqrt(var + LN_EPS) * g.astype(jnp.float32) + b.astype(jnp.float32)
    return y.astype(x.dtype)


def _conformer_conv(a, w_dw, b_dw, ln_g, ln_b):
    val, gate = jnp.split(a, 2, axis=-1)
    u = val * jax.nn.sigmoid(gate)
    u = lax.conv_general_dilated(
        u, w_dw[:, None, :].astype(u.dtype),
        window_strides=(1,),
        padding=((CONV_KERNEL - 1, 0),),
        dimension_numbers=("NWC", "WIO", "NWC"),
        feature_group_count=CONV_WIDTH) + b_dw
    u = _layernorm(u, ln_g, ln_b)
    return jax.nn.silu(u)


def _s5(u, a_re, a_im, log_dt, b_re, b_im, c_re, c_im, d_skip, w_glu):
    f32 = jnp.float32
    bsz, seq_len, _ = u.shape
    uf = u.astype(f32).reshape(bsz, seq_len, SSM_GROUPS, SSM_GROUP)
    lam = lax.complex(a_re.astype(f32), a_im.astype(f32))
    dt = jnp.exp(log_dt.astype(f32))[:, None]
    lam_bar = jnp.exp(lam * dt)
    b = lax.complex(b_re.astype(f32), b_im.astype(f32))
    b_bar = ((lam_bar - 1.0) / lam)[..., None] * b
    bu = lax.complex(jnp.einsum("blgh,gph->blgp", uf, b_bar.real),
                     jnp.einsum("blgh,gph->blgp", uf, b_bar.imag))
    lam_seq = jnp.broadcast_to(lam_bar, bu.shape)

    def combine(e1, e2):
        a1, s1 = e1
        a2, s2 = e2
        return a1 * a2, a2 * s1 + s2

    _, states = lax.associative_scan(combine, (lam_seq, bu), axis=1)
    c = lax.complex(c_re.astype(f32), c_im.astype(f32))
    y = jnp.einsum("blgp,ghp->blgh", states, c).real \
        + d_skip.astype(f32).reshape(SSM_GROUPS, SSM_GROUP) * uf
    y = jax.nn.gelu(y.reshape(bsz, seq_len, SSM_WIDTH)).astype(u.dtype)
    return y * jax.nn.sigmoid(y @ w_glu)


def setup_inputs(seed: int = 0) -> dict:
    key = jax.random.key(seed)
    ks = jax.random.split(key, 24)
    f32 = jnp.float32

    def nrm(k, shape, scale):
        return jax.random.normal(k, shape, f32) * scale

    L = DEPTH
    n = jnp.arange(SSM_STATE, dtype=f32)
    return {
        "x": nrm(ks[0], (BATCH, SEQ, D_MODEL), 1.0),
        "norm_mix": 1.0 + nrm(ks[1], (L, D_MODEL), 0.02),
        "w_in": nrm(ks[2], (L, D_MODEL, IN_COLS), D_MODEL ** -0.5),
        "w_dw": nrm(ks[3], (L, CONV_KERNEL, CONV_WIDTH), CONV_KERNEL ** -0.5),
        "b_dw": nrm(ks[4], (L, CONV_WIDTH), 0.02),
        "ln_g": 1.0 + nrm(ks[5], (L, CONV_WIDTH), 0.02),
        "ln_b": nrm(ks[6], (L, CONV_WIDTH), 0.02),
        "w_conv_out": nrm(ks[7], (L, CONV_WIDTH, D_MODEL), CONV_WIDTH ** -0.5),
        "a_re": -0.5 + nrm(ks[8], (L, SSM_GROUPS, SSM_STATE), 0.01),
        "a_im": math.pi * n + nrm(ks[9], (L, SSM_GROUPS, SSM_STATE), 0.01),
        "log_dt": jax.random.uniform(ks[10], (L, SSM_GROUPS), f32,
                                     math.log(1e-3), math.log(1e-1)),
        "b_re": nrm(ks[11], (L, SSM_GROUPS, SSM_STATE, SSM_GROUP), (2 * SSM_GROUP) ** -0.5),
        "b_im": nrm(ks[12], (L, SSM_GROUPS, SSM_STATE, SSM_GROUP), (2 * SSM_GROUP) ** -0.5),
        "c_re": nrm(ks[13], (L, SSM_GROUPS, SSM_GROUP, SSM_STATE), (2 * SSM_STATE) ** -0.5),
        "c_im": nrm(ks[14], (L, SSM_GROUPS, SSM_GROUP, SSM_STATE), (2 * SSM_STATE) ** -0.5),
        "d_skip": nrm(ks[15], (L, SSM_WIDTH), 1.0),
        "w_glu": nrm(ks[16], (L, SSM_WIDTH, SSM_WIDTH), SSM_WIDTH ** -0.5),
        "w_ssm_out": nrm(ks[17], (L, SSM_WIDTH, D_MODEL), SSM_WIDTH ** -0.5),
        "w_out": nrm(ks[18], (L, D_MODEL, D_MODEL), D_MODEL ** -0.5),
        "norm_mlp": 1.0 + nrm(ks[19], (L, D_MODEL), 0.02),
        "w_ff1": nrm(ks[20], (L, D_MODEL, D_FF), D_MODEL ** -0.5),
        "w_ff2": nrm(ks[21], (L, D_FF, D_MODEL), D_FF ** -0.5),
        "norm_final": 1.0 + nrm(ks[22], (D_MODEL,), 0.02),
    }


def reference(x, norm_mix, w_in, w_dw, b_dw, ln_g, ln_b, w_conv_out,
              a_re, a_im, log_dt, b_re, b_im, c_re, c_im, d_skip, w_glu,
              w_ssm_out, w_out, norm_mlp, w_ff1, w_ff2, norm_final):
    c0 = 2 * CONV_WIDTH
    c1 = c0 + SSM_WIDTH
    for l in range(DEPTH):
        h = _rmsnorm(x, norm_mix[l])
        proj = h @ w_in[l]
        a_conv = proj[..., :c0]
        u_ssm = proj[..., c0:c1]
        g_conv, g_ssm = jnp.split(jax.nn.sigmoid(proj[..., c1:]), 2, axis=-1)
        y_conv = _conformer_conv(a_conv, w_dw[l], b_dw[l], ln_g[l], ln_b[l]) @ w_conv_out[l]
        y_ssm = _s5(u_ssm, a_re[l], a_im[l], log_dt[l], b_re[l], b_im[l],
                    c_re[l], c_im[l], d_skip[l], w_glu[l]) @ w_ssm_out[l]
        x = x + (g_conv * y_conv + g_ssm * y_ssm) @ w_out[l]
        h = _rmsnorm(x, norm_mlp[l])
        x = x + jnp.square(jax.nn.relu(h @ w_ff1[l])) @ w_ff2[l]
    return _rmsnorm(x, norm_final)
```

```python
import math
import numpy as np
import concourse.bass as bass
import concourse.mybir as mybir
from concourse.bass_utils import run_bass_kernel_spmd

F32 = mybir.dt.float32
BF16 = mybir.dt.bfloat16
AF = mybir.ActivationFunctionType
ALU = mybir.AluOpType

D = 2048
SEQ = 4096
DEPTH = 4
CWID = 1024
KW = 31
HIST = KW - 1
NT = 512
DFF = 8192
INC = 7168
EPS_RMS = 1e-6
EPS_LN = 1e-5
NPL = 408
EPOCH = 2000
NSLOT = 8

O_GMIX, O_GMLP, O_BDW, O_LNG, O_LNB, O_WDW, O_DSK, O_ARE, O_AIM, O_LDT = 0, 16, 32, 40, 48, 56, 304, 312, 344, 376


class Sched:
    def __init__(self):
        self.streams = {e: [] for e in ("pe", "act", "dve", "pool", "sp")}
        self.nsig = {e: 0 for e in ("pe", "act", "dve", "pool")}
        self.lastw = {}
        self.readers = {}
        self.known = {e: {} for e in self.streams}
        self.dma_i = {"sp": 0, "pool": 0}
        self.semkeys = set()
        self.mute = False
        self.tag = "init"
        self.pe_tags = []

    def _need(self, reads, writes):
        need = {}

        def add(t):
            if t is None:
                return
            k, v = t
            if need.get(k, 0) < v:
                need[k] = v
        for r in reads:
            add(self.lastw.get(r))
        for w in writes:
            add(self.lastw.get(w))
            for t in self.readers.get(w, {}).items():
                add(t)
        return need

    def _waits(self, eng, need):
        waits = []
        kn = self.known[eng]
        for sk, v in need.items():
            if eng == "pe" and sk[0] == "pe":
                continue
            if sk[0] in self.nsig:
                if any(k2[0] == sk[0] and k2[1] > sk[1] for k2 in kn):
                    continue
            if kn.get(sk, 0) >= v:
                continue
            kn[sk] = v
            waits.append((sk, v))
        return waits

    def _commit(self, tok, reads, writes):
        for r in reads:
            d = self.readers.setdefault(r, {})
            if d.get(tok[0], 0) < tok[1]:
                d[tok[0]] = tok[1]
        for w in writes:
            self.lastw[w] = tok
            self.readers[w] = {}

    def op(self, eng, fn, reads=(), writes=(), sig=True):
        if self.mute:
            return
        if eng == "pe":
            self.pe_tags.append(self.tag)
        need = self._need(reads, writes)
        waits = self._waits(eng, need)
        n = self.nsig[eng]
        sk = (eng, n // EPOCH)
        tok = (sk, n % EPOCH + 1)
        self.semkeys.add(sk)
        self.streams[eng].append((waits, fn, (sk, 1) if sig else None))
        if sig:
            self.nsig[eng] = n + 1
        self._commit(tok, reads, writes)

    def dma(self, q, fn, reads=(), writes=(), extra=()):
        if self.mute:
            return
        i = self.dma_i[q]
        self.dma_i[q] = i + 1
        slot, use = i % NSLOT, i // NSLOT
        sk = ("d" + q, slot)
        self.semkeys.add(sk)
        need = self._need(reads, writes)
        for k_, v_ in extra:
            if need.get(k_, 0) < v_:
                need[k_] = v_
        if use > 0:
            if need.get(sk, 0) < 16 * use:
                need[sk] = 16 * use
        waits = self._waits(q, need)
        self.streams[q].append((waits, fn, (sk, 16)))
        self._commit((sk, 16 * (use + 1)), reads, writes)

    def final_wait(self, eng, toks):
        need = {}
        for k, v in toks:
            if need.get(k, 0) < v:
                need[k] = v
        waits = self._waits(eng, need)
        self.streams[eng].append((waits, None, None))

    def replay(self, eng, handle, sems):
        for waits, fn, inc in self.streams[eng]:
            for sk, v in waits:
                handle.wait_ge(sems[sk], v)
            if fn is None:
                continue
            ins = fn(handle)
            if inc is not None:
                ins.then_inc(sems[inc[0]], inc[1])


def build_nc(n_layers=DEPTH, n_tiles=SEQ // NT, dbg=False, kcut=99, kprep=True):
    nc = bass.Bass("TRN2", target_bir_lowering=False)
    S = Sched()
    L = n_layers
    T = n_tiles * NT

    def din(name, shape, dt=F32):
        return nc.dram_tensor(name, shape, dt, kind="ExternalInput").ap()

    def dscr(name, shape, dt=BF16):
        return nc.dram_tensor(name, shape, dt, kind="Internal").ap()

    xT = din("xT", [D, T])
    w_in = din("w_in", [DEPTH, D, INC])
    w_co = din("w_conv_out", [DEPTH, CWID, D])
    w_gl = din("w_glu", [DEPTH, CWID, CWID])
    w_so = din("w_ssm_out", [DEPTH, CWID, D])
    w_o = din("w_out", [DEPTH, D, D])
    w_f1 = din("w_ff1", [DEPTH, D, DFF])
    w_f2 = din("w_ff2", [DEPTH, DFF, D])
    pl = din("pl", [DEPTH, 128, NPL])
    pb = din("pb", [DEPTH, 128, 4 * 512])
    cst = din("cst", [128, 128 + 128 + 512 + 16])
    outT = nc.dram_tensor("outT", [D, T], F32, kind="ExternalOutput").ap()
    dbg_out = None
    if dbg:
        dbg_out = nc.dram_tensor("dbg", [128, 16 * NT], F32, kind="ExternalOutput").ap()

    wsrc = {"in": (w_in, D, INC), "co": (w_co, CWID, D), "gl": (w_gl, CWID, CWID), "so": (w_so, CWID, D),
            "o": (w_o, D, D), "f1": (w_f1, D, DFF), "f2": (w_f2, DFF, D)}
    wb = {k: dscr("wb_" + k, [L, v[1], v[2]]) for k, v in wsrc.items()}
    zinm = dscr("zinm", [L, 128, 8, 2048])
    lcm = dscr("lcm", [L, 128, 8, 3072])
    dgm = dscr("dgm", [L, 128, 4, 2 * KW * 128])

    import contextlib
    es = contextlib.ExitStack()
    with es:
        ARENA_W = 52100
        arena = es.enter_context(nc.sbuf_tensor("arena", [128, ARENA_W], F32))
        ptr = [0]

        def alloc(words):
            a = ptr[0]
            ptr[0] += words
            assert ptr[0] <= ARENA_W, ptr[0]
            return a

        def vf(a, words):
            return arena[:, a:a + words]

        def vb(a, words):
            return arena[:, a:a + words].bitcast(BF16)

        a_pl = alloc(L * NPL)
        PL = vf(a_pl, L * NPL).rearrange("p (l n) -> p l n", l=L)
        a_c = alloc(128 + 128 + 512 + 16)
        CST = vf(a_c, 784)
        ident = CST[:, 0:128]
        ones_f = CST[:, 128:256]
        bmask = CST[:, 256:768]
        gfin = CST[:, 768:784]
        a_ob = alloc(64)
        ones_b = vb(a_ob, 64)
        a_gs = alloc(L * 32 + 16)
        GS = vf(a_gs, L * 32 + 16)
        a_mu = alloc(L * 128)
        MU = vf(a_mu, L * 128).rearrange("p (l t n) -> p l t n", l=L, t=2)
        a_st = alloc(L * 64)
        ST = vf(a_st, L * 64).rearrange("p (l n) -> p l n", l=L)
        a_hi = alloc(L * 8 * HIST)
        HI = vf(a_hi, L * 8 * HIST).rearrange("p (l c n) -> p l c n", l=L, c=8)
        a_np = alloc(3)
        negpi = vf(a_np, 1)
        epsr = vf(a_np + 1, 1)
        epsl = vf(a_np + 2, 1)
        a_main = ptr[0]
        a_x = alloc(16 * NT)
        xres = vf(a_x, 16 * NT).rearrange("p (c n) -> p c n", c=16)
        a_wb = [alloc(4096) for _ in range(3)]
        wbufs = [vb(a, 4096) for a in a_wb]
        a_h = alloc(4096)
        hbuf = vb(a_h, 4096).rearrange("p (c n) -> p c n", c=16)
        a_R = alloc(8432)
        uc = vb(a_R, 4 * (NT + HIST)).rearrange("p (c n) -> p c n", c=8)
        convo = vf(a_R + 4 * (NT + HIST), 4096).rearrange("p (c n) -> p c n", c=8)
        Z = vf(a_R, 4160).rearrange("p (t g c) -> p t g c", t=2, g=32)
        zbf = vb(a_R + 4160, 2048).rearrange("p (t g c) -> p t g c", t=2, g=32)
        yg = vb(a_R + 4160 + 2048, 2048).rearrange("p (c n) -> p c n", c=8)
        mbuf = vb(a_R, 4096).rearrange("p (c n) -> p c n", c=16)
        abuf = vb(a_R, 8192).rearrange("p (c n) -> p c n", c=32)
        ostage = vf(a_R, 8192).rearrange("p (c n) -> p c n", c=16)
        a_v = alloc(2048)
        vconv = vb(a_v, 2048).rearrange("p (c n) -> p c n", c=8)
        a_u = alloc(2048)
        ubuf = vb(a_u, 2048).rearrange("p (c n) -> p c n", c=8)
        a_zi = [alloc(1024) for _ in range(2)]
        zibuf = [vb(a, 1024).rearrange("p (j t n) -> p j t n", j=8, t=2) for a in a_zi]
        a_ubm = [alloc(1024) for _ in range(2)]
        ubm = [vb(a, 1024).rearrange("p (j r c) -> p j r c", j=8, r=4) for a in a_ubm]
        a_lc = [alloc(1536) for _ in range(2)]
        lcb = [vb(a, 1536) for a in a_lc]
        a_sq = [alloc(256) for _ in range(2)]
        sqb = [vb(a, 256) for a in a_sq]
        a_t = [alloc(512) for _ in range(4)]
        tf = [vf(a, 512) for a in a_t]
        sqf = [tf[2], tf[3]]
        a_tb = [alloc(256) for _ in range(3)]
        tb = [vb(a, 256) for a in a_tb]
        a_s1 = alloc(64)
        a_s2 = alloc(64)
        sc1 = vf(a_s1, 64).rearrange("p (t g) -> p t g", t=2)
        sc2 = vf(a_s2, 64).rearrange("p (t g) -> p t g", t=2)
        a_end = ptr[0]
        ptr[0] = a_main
        a_pb = alloc(2048)
        PB = vf(a_pb, 2048).rearrange("p (t g h) -> p t g h", t=4, g=32)
        sm = [vf(alloc(32), 32) for _ in range(24)]
        smi = vf(alloc(32), 32).bitcast(mybir.dt.int32)
        Pc = [vf(alloc(512), 512).rearrange("p (g h) -> p g h", g=32) for _ in range(4)]
        Wc = [vf(alloc(512), 512).rearrange("p (g h) -> p g h", g=32) for _ in range(4)]
        tmpc = [vf(alloc(512), 512).rearrange("p (g h) -> p g h", g=32) for _ in range(2)]
        Pexp = [vf(alloc(1024), 1024) for _ in range(2)]
        Cexp = [vf(alloc(1024), 1024) for _ in range(2)]
        Wexp = [vf(alloc(1024), 1024) for _ in range(2)]
        zin_sb = vb(alloc(8192), 8192).rearrange("p (q j t n) -> p q j t n", q=8, j=8, t=2)
        lc_sb = vb(alloc(12288), 12288).rearrange("p (q n) -> p q n", q=8)
        dg_sb = vb(alloc(KW * 128), KW * 128).rearrange("p (c k n) -> p c k n", c=2, k=KW)
        assert ptr[0] <= ARENA_W, ptr[0]

        psb = [es.enter_context(nc.psum_tensor("ps%d" % i, [128, 512], F32))[:] for i in range(8)]
        pool_i = [0]

        def nextbank(lo=0, hi=4):
            i = lo + pool_i[0] % (hi - lo)
            pool_i[0] += 1
            return i

        def TT(eng, out, a, b, op, R, W):
            S.op(eng, lambda e: e.tensor_tensor(out=out, in0=a, in1=b, op=op), R, W)

        def TS(eng, out, a, s1, s2, op0, op1, R, W):
            if s2 is None:
                S.op(eng, lambda e: e.tensor_scalar(out=out, in0=a, scalar1=s1, scalar2=None, op0=op0), R, W)
            else:
                S.op(eng, lambda e: e.tensor_scalar(out=out, in0=a, scalar1=s1, scalar2=s2, op0=op0, op1=op1), R, W)

        def STT(out, a, sc, b, op0, op1, R, W):
            S.op("dve", lambda e: e.scalar_tensor_tensor(out=out, in0=a, scalar=sc, in1=b, op0=op0, op1=op1), R, W)

        def ACT(out, a, func, R, W, bias=None, scale=None):
            kw = {}
            if bias is not None:
                kw["bias"] = bias
            if scale is not None:
                kw["scale"] = scale
            S.op("act", lambda e: e.activation(out=out, in_=a, func=func, **kw), R, W)

        def CP(eng, out, a, R, W):
            if eng == "act":
                S.op("act", lambda e: e.copy(out=out, in_=a), R, W)
            else:
                S.op(eng, lambda e: e.tensor_copy(out=out, in_=a), R, W)

        def MM(out, lhsT, rhs, start, stop, R, W, sig=None, tp=None):
            kw = {}
            if tp is not None:
                kw["tile_position"] = tp
            S.op("pe", lambda e: e.matmul(out, lhsT, rhs, start=start, stop=stop, **kw), R, W,
                 sig=True if sig is None else sig)

        S.dma("sp", lambda e: e.dma_start(out=CST, in_=cst[:, :]), [], ["cst"])
        S.dma("sp", lambda e: e.dma_start(out=PL, in_=pl[0:L].rearrange("l p n -> p l n")), [], ["pl"])
        S.op("dve", lambda e: e.memset(negpi, -math.pi), [], ["negpi"])
        S.op("dve", lambda e: e.memset(epsr, D * EPS_RMS), ["negpi"], ["negpi"])
        S.op("dve", lambda e: e.memset(epsl, EPS_LN), ["negpi"], ["negpi"])
        CP("dve", ones_b, ones_f, ["cst"], ["ones_b"])
        S.op("dve", lambda e: e.memset(vf(a_st, L * 64), 0.0), [], ["ST%d" % l for l in range(L)])
        S.op("dve", lambda e: e.memset(vf(a_hi, L * 8 * HIST), 0.0), [], ["HI%d" % l for l in range(L)])
        for l in range(L):
            TS("dve", GS[:, l * 32:l * 32 + 32], PL[:, l, 0:32], math.sqrt(D), None, ALU.mult, None, ["pl"], ["gs"])
        TS("dve", GS[:, L * 32:L * 32 + 16], gfin, math.sqrt(D), None, ALU.mult, None, ["cst"], ["gs"])

        def wpieces(key):
            _, K_, N_ = wsrc[key]
            cw = 1792 if N_ == INC else min(N_, 2048)
            return [(r, c, cw) for r in range(K_ // 128) for c in range(N_ // cw)]

        def cast_items(l):
            items = []
            for key in ("in", "co", "gl", "so", "o", "f1", "f2"):
                src = wsrc[key][0]
                for (r, c, cw) in wpieces(key):
                    items.append((lambda e, key=key, src=src, l=l, r=r, c=c, cw=cw: e.dma_start(
                        out=wb[key][l, r * 128:(r + 1) * 128, c * cw:(c + 1) * cw],
                        in_=src[l, r * 128:(r + 1) * 128, c * cw:(c + 1) * cw]), ("wb", key, l, r, c)))
            return items

        pending_cast = []
        for fn_, res_ in cast_items(0):
            S.dma("pool", fn_, [], [res_])

        def pace_cast(n, extra=()):
            for _ in range(min(n, len(pending_cast))):
                fn_, res_ = pending_cast.pop(0)
                S.dma("pool", fn_, [], [res_], extra=extra)

        cnt = [0]

        def cmul(o_re, o_im, a_re, a_im, b_re, b_im, t0, t1, R, W, eng="dve"):
            big = (t0 is tmpc[0])
            r0, r1 = ("tmpc0", "tmpc1") if big else ("s_q0", "s_q1")
            TT(eng, t0, a_re, b_re, ALU.mult, R, [r0])
            TT(eng, t1, a_im, b_im, ALU.mult, R, [r1])
            TT(eng, o_re, t0, t1, ALU.subtract, [r0, r1], W[0:1])
            TT(eng, t0, a_re, b_im, ALU.mult, R, [r0])
            TT(eng, t1, a_im, b_re, ALU.mult, R, [r1])
            TT(eng, o_im, t0, t1, ALU.add, [r0, r1], W[1:2])

        def bc(t):
            return t.unsqueeze(2).to_broadcast([128, 32, 16])

        S.tag = "prep"
        for l in range(L if kprep else 0):
            are, aim, ldt = PL[:, l, O_ARE:O_ARE + 32], PL[:, l, O_AIM:O_AIM + 32], PL[:, l, O_LDT:O_LDT + 32]
            S.dma("sp", lambda e, l=l: e.dma_start(out=vf(a_pb, 2048), in_=pb[l]), [], ["PB"])
            dt_, xr, xi, mag, sa, ca, sn, cs, lr, li, den, rden, nr, kr, ki, q0, q1, q2, q3 = sm[0:19]
            ACT(dt_, ldt, AF.Exp, ["pl"], ["s_dt"])
            TT("dve", xr, are, dt_, ALU.mult, ["pl", "s_dt"], ["s_xr"])
            TT("dve", xi, aim, dt_, ALU.mult, ["pl", "s_dt"], ["s_xi"])
            ACT(mag, xr, AF.Exp, ["s_xr"], ["s_mag"])
            for (dst, off, nm) in ((sn, 0.0, "s_sn"), (cs, 0.25, "s_cs")):
                TS("dve", sa, xi, 1.0 / (2.0 * math.pi), off, ALU.mult, ALU.add, ["s_xi"], ["s_sa"])
                CP("dve", smi, sa, ["s_sa"], ["s_smi"])
                CP("dve", ca, smi, ["s_smi"], ["s_ca"])
                TT("dve", sa, sa, ca, ALU.subtract, ["s_sa", "s_ca"], ["s_sa"])
                ACT(ca, sa, AF.Sin, ["s_sa"], ["s_ca"], scale=math.pi)
                ACT(q3, sa, AF.Sin, ["s_sa"], ["s_q3"], scale=math.pi / 2.0)
                TT("dve", q3, q3, q3, ALU.mult, ["s_q3"], ["s_q3"])
                TS("dve", q3, q3, -4.0, 2.0, ALU.mult, ALU.add, ["s_q3"], ["s_q3"])
                TT("dve", dst, ca, q3, ALU.mult, ["s_ca", "s_q3"], [nm])
            TT("dve", lr, mag, cs, ALU.mult, ["s_mag", "s_cs"], ["s_lr"])
            TT("dve", li, mag, sn, ALU.mult, ["s_mag", "s_sn"], ["s_li"])
            TT("dve", q0, are, are, ALU.mult, ["pl"], ["s_q0"])
            TT("dve", q1, aim, aim, ALU.mult, ["pl"], ["s_q1"])
            TT("dve", den, q0, q1, ALU.add, ["s_q0", "s_q1"], ["s_den"])
            S.op("dve", lambda e: e.reciprocal(out=rden, in_=den), ["s_den"], ["s_rden"])
            TS("dve", nr, lr, -1.0, None, ALU.add, None, ["s_lr"], ["s_nr"])
            TT("dve", q0, nr, are, ALU.mult, ["s_nr", "pl"], ["s_q0"])
            TT("dve", q1, li, aim, ALU.mult, ["s_li", "pl"], ["s_q1"])
            TT("dve", q2, q0, q1, ALU.add, ["s_q0", "s_q1"], ["s_q2"])
            TT("dve", kr, q2, rden, ALU.mult, ["s_q2", "s_rden"], ["s_kr"])
            TT("dve", q0, li, are, ALU.mult, ["s_li", "pl"], ["s_q0"])
            TT("dve", q1, nr, aim, ALU.mult, ["s_nr", "pl"], ["s_q1"])
            TT("dve", q2, q0, q1, ALU.subtract, ["s_q0", "s_q1"], ["s_q2"])
            TT("dve", ki, q2, rden, ALU.mult, ["s_q2", "s_rden"], ["s_ki"])
            m_re, m_im, n_re, n_im = sm[19:23]
            cmul(m_re, m_im, lr, li, lr, li, q0, q1, ["s_lr", "s_li"], ["s_mre", "s_mim"])
            cmul(n_re, n_im, m_re, m_im, m_re, m_im, q0, q1, ["s_mre", "s_mim"], ["s_nre", "s_nim"])
            cmul(m_re, m_im, n_re, n_im, n_re, n_im, q0, q1, ["s_nre", "s_nim"], ["s_mre", "s_mim"])
            mr = "MU%d" % l
            CP("dve", MU[:, l, 0, 0:32], m_re, ["s_mre"], [mr + "a"])
            CP("dve", MU[:, l, 0, 32:64], m_re, ["s_mre"], [mr + "b"])
            TS("dve", MU[:, l, 1, 0:32], m_im, -1.0, None, ALU.mult, None, ["s_mim"], [mr + "c"])
            CP("dve", MU[:, l, 1, 32:64], m_im, ["s_mim"], [mr + "d"])
            Bre, Bim, Cre, Cim = PB[:, 0], PB[:, 1], PB[:, 2], PB[:, 3]
            cmul(Pc[0], Pc[1], bc(kr), bc(ki), Bre, Bim, tmpc[0], tmpc[1], ["s_kr", "s_ki", "PB"], ["P0r", "P0i"])
            for t_ in range(2):
                S.op("dve", lambda e, t_=t_: e.memset(Cexp[t_], 0.0), [], ["Cexp%d" % t_])
                S.op("dve", lambda e, t_=t_: e.memset(Pexp[t_], 0.0), [], ["Pexp%d" % t_])
                S.op("dve", lambda e, t_=t_: e.memset(Wexp[t_], 0.0), [], ["Wexp%d" % t_])

            def expand(dst, src, R, W, scale=None, eng="dve"):
                dv = dst.rearrange("p (g e h) -> p g e h", g=32, e=2)
                for e_ in range(2):
                    o = dv[64 * e_:64 * e_ + 64, :, e_, :]
                    i = src[64 * e_:64 * e_ + 64, :, :]
                    if scale is None:
                        CP(eng, o, i, R, W)
                    else:
                        TS(eng, o, i, scale, None, ALU.mult, None, R, W)

            expand(Cexp[0], Cre, ["PB"], ["Cexp0"])
            expand(Cexp[1], Cim, ["PB"], ["Cexp1"], scale=-1.0)
            cur = 0
            for k in range(8):
                Pr, Pi = Pc[2 * cur], Pc[2 * cur + 1]
                rn = ["P%dr" % cur, "P%di" % cur]
                expand(Pexp[0], Pr, [rn[0]], ["Pexp0"])
                expand(Pexp[1], Pi, [rn[1]], ["Pexp1"])
                for qh in range(2):
                    bk = nextbank()
                    for qq in range(4):
                        q = qh * 4 + qq
                        o = psb[bk][:, qq * 128:(qq + 1) * 128]
                        MM(o, Pexp[0][:, q * 128:(q + 1) * 128], Cexp[0][:, q * 128:(q + 1) * 128], True, False,
                           ["Pexp0", "Cexp0"], ["ps%d" % bk], sig=False)
                        MM(o, Pexp[1][:, q * 128:(q + 1) * 128], Cexp[1][:, q * 128:(q + 1) * 128], False, True,
                           ["Pexp1", "Cexp1"], ["ps%d" % bk], sig=True)
                    o4 = lc_sb[:, qh * 4:qh * 4 + 4, k * 128:(k + 1) * 128]
                    TT("dve", o4, psb[bk].rearrange("p (q n) -> p q n", q=4),
                       bmask.rearrange("p (q n) -> p q n", q=4), ALU.mult, ["ps%d" % bk, "cst"], ["lc_sb"])
                    for t_ in range(2):
                        bk2 = nextbank()
                        for qq in range(4):
                            q = qh * 4 + qq
                            S.op("pe", lambda e, bk2=bk2, qq=qq, q=q, t_=t_: e.transpose(
                                psb[bk2][:, qq * 128:(qq + 1) * 128], Pexp[t_][:, q * 128:(q + 1) * 128], ident),
                                ["Pexp%d" % t_, "cst"], ["ps%d" % bk2], sig=(qq == 3))
                        CP("act", zin_sb[:, qh * 4:qh * 4 + 4, 7 - k, t_, :],
                           psb[bk2].rearrange("p (q n) -> p q n", q=4), ["ps%d" % bk2], ["zin_sb"])
                if k < 7:
                    nx = 1 - cur
                    cmul(Pc[2 * nx], Pc[2 * nx + 1], bc(lr), bc(li), Pr, Pi, tmpc[0], tmpc[1],
                         ["s_lr", "s_li"] + rn, ["P%dr" % nx, "P%di" % nx])
                    cur = nx
            for q in range(8):
                STT(lc_sb[:, q, 0:128], ident, PL[:, l, O_DSK + q:O_DSK + q + 1], lc_sb[:, q, 0:128], ALU.mult, ALU.add,
                    ["cst", "pl", "lc_sb"], ["lc_sb"])
            wcur = 0
            srcr, srci = Cre, Cim
            srcn = ["PB", "PB"]
            for k in range(1, 9):
                nr_, ni_ = Wc[2 * wcur], Wc[2 * wcur + 1]
                wn = ["W%dr" % wcur, "W%di" % wcur]
                cmul(nr_, ni_, bc(lr), bc(li), srcr, srci, tmpc[0], tmpc[1], ["s_lr", "s_li"] + srcn, wn)
                expand(Wexp[0], nr_, [wn[0]], ["Wexp0"])
                expand(Wexp[1], ni_, [wn[1]], ["Wexp1"], scale=-1.0)
                for t_ in range(2):
                    o = lc_sb[:, :, 1024 + (k - 1) * 256 + t_ * 128:1024 + (k - 1) * 256 + t_ * 128 + 128]
                    CP("act", o, Wexp[t_].rearrange("p (q n) -> p q n", q=8), ["Wexp%d" % t_], ["lc_sb"])
                srcr, srci, srcn = nr_, ni_, wn
                wcur = 1 - wcur
            wdw_l = PL[:, l, O_WDW:O_WDW + 248].rearrange("p (c k) -> p c k", c=8)
            for s4 in range(4):
                for c2 in range(2):
                    TT("dve", dg_sb[:, c2, :, :], ident.unsqueeze(1).to_broadcast([128, KW, 128]),
                       wdw_l[:, 2 * s4 + c2, :].unsqueeze(2).to_broadcast([128, KW, 128]), ALU.mult,
                       ["cst", "pl"], ["dg_sb"])
                S.dma("sp", lambda e, l=l, s4=s4: e.dma_start(out=dgm[l, :, s4, :], in_=dg_sb.rearrange("p c k n -> p (c k n)")),
                      ["dg_sb"], ["dgm%d" % l])
            S.dma("sp", lambda e, l=l: e.dma_start(out=zinm[l], in_=zin_sb.rearrange("p q j t n -> p q (j t n)")),
                  ["zin_sb"], ["zinm%d" % l])
            S.dma("sp", lambda e, l=l: e.dma_start(out=lcm[l], in_=lc_sb), ["lc_sb"], ["lcm%d" % l])

        alltok = []
        for e_ in ("pe", "act", "dve", "pool"):
            n = S.nsig[e_]
            if n > 0:
                alltok.append(((e_, (n - 1) // EPOCH), (n - 1) % EPOCH + 1))
        for q_ in ("sp", "pool"):
            i = S.dma_i[q_]
            for s_ in range(min(i, NSLOT)):
                uses = (i - s_ + NSLOT - 1) // NSLOT
                alltok.append((("d" + q_, s_), 16 * uses))
        for e_ in ("pe", "act", "dve", "sp"):
            S.final_wait(e_, alltok)
        for i_ in range(2):
            S.op("dve", lambda e, i_=i_: e.memset(ubm[i_], 0.0), [], ["ubm0"] + ["ubm_%d_%d" % (i_, r) for r in range(4)])

        def wres(key, l, r0, r1, c0, c1):
            _, K_, N_ = wsrc[key]
            cw = 1792 if N_ == INC else min(N_, 2048)
            return [("wb", key, l, r, c) for r in range(r0 // 128, (r1 + 127) // 128)
                    for c in range(c0 // cw, (c1 - 1) // cw + 1)]

        slab_i = [0]

        def load_slab(key, l, r0, nk, c0, ncol):
            i = slab_i[0] % 3
            slab_i[0] += 1
            if pending_cast:
                pace_cast(5, extra=list(S.readers.get("wbuf%d" % i, {}).items()))
            v = wbufs[i][:, 0:nk * ncol].rearrange("p (k n) -> p k n", k=nk)
            srcap = wb[key][l, r0:r0 + nk * 128, c0:c0 + ncol].rearrange("(k p) n -> p k n", p=128)
            S.dma("sp", lambda e: e.dma_start(out=v, in_=srcap), wres(key, l, r0, r0 + nk * 128, c0, c0 + ncol),
                  ["wbuf%d" % i])
            return v, "wbuf%d" % i

        def rmsnorm_to_h(goff):
            bk = nextbank()
            for c in range(16):
                s = sqb[c % 2]
                ACT(s, xres[:, c, :], AF.Square, [("x", c)], ["sqb%d" % (c % 2)])
                MM(psb[bk], ones_b, s, c == 0, c == 15, ["sqb%d" % (c % 2), "ones_b"], ["ps%d" % bk])
            ACT(tf[0], psb[bk], AF.Sqrt, ["ps%d" % bk, "negpi"], ["tf0"], bias=epsr, scale=1.0)
            S.op("dve", lambda e: e.reciprocal(out=tf[0], in_=tf[0]), ["tf0"], ["tf0"])
            return goff

        def dump(ap_src, R):
            pass

        last_out_tok = []
        for ti in range(n_tiles):
            t0 = ti * NT
            S.dma("sp", lambda e, t0=t0: e.dma_start(out=xres, in_=xT[:, t0:t0 + NT].rearrange("(c p) n -> p c n", p=128)),
                  [], [("x", c) for c in range(16)])
            for l in range(L):
                g0 = l * 32
                if ti == 0:
                    pace_cast(len(pending_cast))
                    if l + 1 < L:
                        pending_cast.extend(cast_items(l + 1))
                S.mute = kcut < 1
                S.tag = "ph1"
                rmsnorm_to_h(g0)
                for c in range(16):
                    STT(hbuf[:, c, :], xres[:, c, :], GS[:, g0 + c:g0 + c + 1], tf[0], ALU.mult, ALU.mult,
                        [("x", c), "gs", "tf0"], [("h", c)])
                HR = [("h", c) for c in range(16)]
                S.mute = kcut < 2
                S.tag = "ph2"
                for s in (2, 3, 0, 1):
                    wv, wr = load_slab("in", l, 0, 16, s * 512, 512)
                    bks = [nextbank() for _ in range(4)]
                    if s == 2:
                        for kc in range(16):
                            for mi in range(4):
                                MM(psb[bks[mi]], wv[:, kc, mi * 128:(mi + 1) * 128], hbuf[:, kc, :], kc == 0, kc == 15,
                                   [wr, ("h", kc)], ["ps%d" % bks[mi]], sig=(kc == 15))
                    for mi in range(4):
                        bk = bks[mi]
                        for kc in range(16 if s != 2 else 0):
                            MM(psb[bk], wv[:, kc, mi * 128:(mi + 1) * 128], hbuf[:, kc, :], kc == 0, kc == 15,
                               [wr, ("h", kc)], ["ps%d" % bk], sig=(kc == 15))
                        cc = (s % 2) * 4 + mi
                        if s >= 2:
                            ACT(vconv[:, cc, :], psb[bk], AF.Sigmoid, ["ps%d" % bk], [("v", cc)])
                        else:
                            TT("dve", uc[:, cc, HIST:HIST + NT], psb[bk], vconv[:, cc, :], ALU.mult,
                               ["ps%d" % bk, ("v", cc)], [("R", "uc", cc)])
                S.mute = kcut < 3
                S.tag = "ph3"
                for s in (4, 5):
                    wv, wr = load_slab("in", l, 0, 16, s * 512, 512)
                    for mi in range(4):
                        bk = nextbank()
                        for kc in range(16):
                            MM(psb[bk], wv[:, kc, mi * 128:(mi + 1) * 128], hbuf[:, kc, :], kc == 0, kc == 15,
                               [wr, ("h", kc)], ["ps%d" % bk], sig=(kc == 15))
                        q = (s - 4) * 4 + mi
                        CP("act", ubuf[:, q, :].rearrange("p (j c) -> p c j", j=8), psb[bk].rearrange("p (c j) -> p c j", j=8), ["ps%d" % bk], [("u", q)])
                S.mute = kcut < 4
                S.tag = "ph4"
                CP("dve", uc[:, :, 0:HIST], HI[:, l, :, :], ["HI%d" % l], [("R", "uc", cc) for cc in range(8)])
                for s4 in range(4):
                    i = slab_i[0] % 3
                    slab_i[0] += 1
                    dgv = wbufs[i][:, 0:2 * KW * 128].rearrange("p (c k n) -> p c k n", c=2, k=KW)
                    S.dma("sp", lambda e, l=l, s4=s4, i=i: e.dma_start(out=wbufs[i][:, 0:2 * KW * 128], in_=dgm[l, :, s4, :]),
                          ["dgm%d" % l], ["wbuf%d" % i])
                    for c2 in range(2):
                        cc = 2 * s4 + c2
                        bk = nextbank()
                        for k in range(KW):
                            MM(psb[bk], dgv[:, c2, k, :], uc[:, cc, k:k + NT], k == 0, k == KW - 1,
                               ["wbuf%d" % i, ("R", "uc", cc)], ["ps%d" % bk], sig=(k == KW - 1))
                        ACT(convo[:, cc, :], psb[bk], AF.Identity, ["ps%d" % bk, "pl"], [("R", "cv", cc)],
                            bias=PL[:, l, O_BDW + cc:O_BDW + cc + 1], scale=1.0)
                CP("dve", HI[:, l, :, :], uc[:, :, NT:NT + HIST], [("R", "uc", cc) for cc in range(8)], ["HI%d" % l])
                S.mute = kcut < 5
                S.tag = "ph5"
                bm, bs = nextbank(), nextbank()
                for cc in range(8):
                    MM(psb[bm], ones_f, convo[:, cc, :], cc == 0, cc == 7, [("R", "cv", cc), "cst"], ["ps%d" % bm])
                for cc in range(8):
                    s = sqf[cc % 2]
                    ACT(s, convo[:, cc, :], AF.Square, [("R", "cv", cc)], ["tf%d" % (2 + cc % 2)])
                    MM(psb[bs], ones_f, s, cc == 0, cc == 7, ["tf%d" % (2 + cc % 2), "cst"], ["ps%d" % bs])
                TS("dve", tf[1], psb[bm], 1.0 / CWID, None, ALU.mult, None, ["ps%d" % bm], ["tf1"])
                TT("dve", tf[2], tf[1], tf[1], ALU.mult, ["tf1"], ["tf2"])
                STT(tf[3], psb[bs], 1.0 / CWID, tf[2], ALU.mult, ALU.subtract, ["ps%d" % bs, "tf2"], ["tf3"])
                ACT(tf[2], tf[3], AF.Sqrt, ["tf3", "negpi"], ["tf2"], bias=epsl, scale=1.0)
                S.op("dve", lambda e: e.reciprocal(out=tf[2], in_=tf[2]), ["tf2"], ["tf2"])
                STT(tf[3], tf[1], -1.0, tf[2], ALU.mult, ALU.mult, ["tf1", "tf2"], ["tf3"])
                for cc in range(8):
                    TT("dve", convo[:, cc, :], convo[:, cc, :], tf[2], ALU.mult, [("R", "cv", cc), "tf2"], [("R", "cv", cc)])
                    TT("dve", convo[:, cc, :], convo[:, cc, :], tf[3], ALU.add, [("R", "cv", cc), "tf3"], [("R", "cv", cc)])
                    ACT(vconv[:, cc, :], convo[:, cc, :], AF.Silu, [("R", "cv", cc), "pl"], [("v", cc)],
                        bias=PL[:, l, O_LNB + cc:O_LNB + cc + 1], scale=PL[:, l, O_LNG + cc:O_LNG + cc + 1])
                S.mute = kcut < 6
                S.tag = "ph6"
                RZ = [("R", "uc", c) for c in range(8)] + [("R", "cv", c) for c in range(8)]
                for q in range(8):
                    zb = zibuf[q % 2]
                    S.dma("sp", lambda e, l=l, q=q, zb=zb: e.dma_start(out=zb.rearrange("p j t n -> p (j t n)"), in_=zinm[l, :, q, :]),
                          ["zinm%d" % l], ["zib%d" % (q % 2)])
                    uq = ubuf[:, q, :].rearrange("p (j c) -> p j c", j=8)
                    um = ubm[q % 2]
                    for r in range(4):
                        CP("act", um[32 * r:32 * r + 32, :, r, :], uq[32 * r:32 * r + 32, :, :], [("u", q), "ubm0"], ["ubm_%d_%d" % (q % 2, r)])
                    bk = 4 + (q % 2)
                    zv = psb[bk].rearrange("p (t r c) -> p t r c", t=2, r=4)
                    for t_ in range(2):
                        for j in range(8):
                            MM(psb[bk][:, t_ * 256:(t_ + 1) * 256], zb[:, j, t_, :], um[:, j, :, :].rearrange("p r c -> p (r c)"),
                               j == 0, j == 7, ["zib%d" % (q % 2)] + ["ubm_%d_%d" % (q % 2, r) for r in range(4)], ["ps%d" % bk],
                               sig=(j == 7))
                    for t_ in range(2):
                        first = (q == 0 and t_ == 0)
                        CP("act" if t_ == 0 else "dve", Z[:, t_, 4 * q:4 * q + 4, 1:65], zv[:, t_, :, :],
                           ["ps%d" % bk] + (RZ if first else []), ["Z"] + (RZ if first else []))
                S.mute = kcut < 7
                S.tag = "ph7"
                CP("dve", Z[:, :, :, 0], ST[:, l, :].rearrange("p (t g) -> p t g", t=2), ["ST%d" % l, "Z"], ["Z"])
                Am = MU[:, l, 0, :].rearrange("p (t g) -> p t g", t=2)
                Bmm = MU[:, l, 1, :].rearrange("p (t g) -> p t g", t=2)
                MR = ["MU%d%s" % (l, x) for x in "abcd"]
                for c in range(64):
                    TT("dve", sc1, Am, Z[:, :, :, c], ALU.mult, ["Z"] + MR, ["sc1"])
                    TT("dve", sc2, Bmm, Z[:, ::-1, :, c], ALU.mult, ["Z"] + MR, ["sc2"])
                    TT("dve", sc1, sc1, sc2, ALU.add, ["sc1", "sc2"], ["sc1"])
                    TT("dve", Z[:, :, :, c + 1], Z[:, :, :, c + 1], sc1, ALU.add, ["Z", "sc1"], ["Z"])
                CP("dve", ST[:, l, :].rearrange("p (t g) -> p t g", t=2), Z[:, :, :, 64], ["Z"], ["ST%d" % l])
                CP("act", zbf, Z[:, :, :, 0:64], ["Z"], ["zbf"])
                S.mute = kcut < 8
                S.tag = "ph8"
                for q in range(8):
                    lb = lcb[q % 2]
                    S.dma("sp", lambda e, l=l, q=q, lb=lb: e.dma_start(out=lb, in_=lcm[l, :, q, :]), ["lcm%d" % l], ["lcb%d" % (q % 2)])
                    bk = nextbank()
                    yp = psb[bk]
                    ypv = yp.rearrange("p (j c) -> p j c", j=8)
                    MM(yp, lb[:, 0:128], ubuf[:, q, :], True, False, ["lcb%d" % (q % 2), ("u", q)], ["ps%d" % bk], sig=False)
                    for k in range(1, 8):
                        MM(yp[:, k * 64:512], lb[:, k * 128:(k + 1) * 128], ubuf[:, q, 0:(8 - k) * 64], False, False,
                           ["lcb%d" % (q % 2), ("u", q)], ["ps%d" % bk], sig=False)
                    for r in range(4):
                        for j in range(8):
                            for t_ in range(2):
                                o = 1024 + j * 256 + t_ * 128 + r * 32
                                last = (r == 3 and j == 7 and t_ == 1)
                                MM(ypv[32 * r:32 * r + 32, j, :], lb[:, o:o + 32], zbf[:, t_, 4 * q + r, :], False, (j == 7 and t_ == 1),
                                   ["lcb%d" % (q % 2), "zbf"], ["ps%d" % bk], sig=last, tp=(0, 32 * r))
                    ACT(tf[0], yp, AF.Square, ["ps%d" % bk], ["tf0"])
                    TS("dve", tf[0], tf[0], 0.044715, 1.0, ALU.mult, ALU.add, ["tf0"], ["tf0"])
                    TT("dve", tf[0], tf[0], yp, ALU.mult, ["tf0", "ps%d" % bk], ["tf0"])
                    ACT(tf[1], tf[0], AF.Sigmoid, ["tf0"], ["tf1"], scale=2.0 * math.sqrt(2.0 / math.pi))
                    TT("dve", yg[:, q, :].rearrange("p (c j) -> p c j", j=8), tf[1].rearrange("p (j c) -> p c j", j=8), yp.rearrange("p (j c) -> p c j", j=8), ALU.mult, ["tf1", "ps%d" % bk], [("yg", q)])
                S.mute = kcut < 9
                S.tag = "ph9"
                wv, wr = load_slab("gl", l, 0, 8, 0, 1024)
                for mt in range(8):
                    bk = nextbank()
                    for kc in range(8):
                        MM(psb[bk], wv[:, kc, mt * 128:(mt + 1) * 128], yg[:, kc, :], kc == 0, kc == 7,
                           [wr, ("yg", kc)], ["ps%d" % bk], sig=(kc == 7))
                    tbi = mt % 3
                    ACT(tb[tbi], psb[bk], AF.Sigmoid, ["ps%d" % bk], ["tb%d" % tbi])
                    TT("dve", ubuf[:, mt, :], yg[:, mt, :], tb[tbi], ALU.mult, [("yg", mt), "tb%d" % tbi], [("u", mt)])
                S.mute = kcut < 10
                S.tag = "ph10"
                RY = [("yg", q) for q in range(8)] + ["Z", "zbf"]
                for half in range(2):
                    wv, wr = load_slab("co", l, 0, 8, half * 1024, 1024)
                    for g4 in range(2):
                        gv, gr = load_slab("in", l, 0, 16, 3072 + (half * 2 + g4) * 512, 512)
                        for mi in range(4):
                            mt = half * 8 + g4 * 4 + mi
                            bkg = nextbank()
                            for kc in range(16):
                                MM(psb[bkg], gv[:, kc, mi * 128:(mi + 1) * 128], hbuf[:, kc, :], kc == 0, kc == 15,
                                   [gr, ("h", kc)], ["ps%d" % bkg], sig=(kc == 15))
                            tbi = mt % 3
                            ACT(tb[tbi], psb[bkg], AF.Sigmoid, ["ps%d" % bkg], ["tb%d" % tbi])
                            bk = nextbank()
                            for kc in range(8):
                                MM(psb[bk], wv[:, kc, (g4 * 4 + mi) * 128:(g4 * 4 + mi + 1) * 128], vconv[:, kc, :], kc == 0, kc == 7,
                                   [wr, ("v", kc)], ["ps%d" % bk], sig=(kc == 7))
                            extra = RY + RZ if mt == 0 else []
                            TT("dve", mbuf[:, mt, :], psb[bk], tb[tbi], ALU.mult, ["ps%d" % bk, "tb%d" % tbi] + extra, [("m", mt)] + extra)
                S.mute = kcut < 11
                S.tag = "ph11"
                for half in range(2):
                    wv, wr = load_slab("so", l, 0, 8, half * 1024, 1024)
                    for g4 in range(2):
                        gv, gr = load_slab("in", l, 0, 16, 5120 + (half * 2 + g4) * 512, 512)
                        for mi in range(4):
                            mt = half * 8 + g4 * 4 + mi
                            bkg = nextbank()
                            for kc in range(16):
                                MM(psb[bkg], gv[:, kc, mi * 128:(mi + 1) * 128], hbuf[:, kc, :], kc == 0, kc == 15,
                                   [gr, ("h", kc)], ["ps%d" % bkg], sig=(kc == 15))
                            tbi = mt % 3
                            ACT(tb[tbi], psb[bkg], AF.Sigmoid, ["ps%d" % bkg], ["tb%d" % tbi])
                            bk = nextbank()
                            for kc in range(8):
                                MM(psb[bk], wv[:, kc, (g4 * 4 + mi) * 128:(g4 * 4 + mi + 1) * 128], ubuf[:, kc, :], kc == 0, kc == 7,
                                   [wr, ("u", kc)], ["ps%d" % bk], sig=(kc == 7))
                            TT("dve", tf[mt % 2], psb[bk], tb[tbi], ALU.mult, ["ps%d" % bk, "tb%d" % tbi], ["tf%d" % (mt % 2)])
                            TT("dve", mbuf[:, mt, :], mbuf[:, mt, :], tf[mt % 2], ALU.add, [("m", mt), "tf%d" % (mt % 2)], [("m", mt)])
                S.mute = kcut < 12
                S.tag = "ph12"
                for s in range(4):
                    wv, wr = load_slab("o", l, 0, 16, s * 512, 512)
                    for mi in range(4):
                        mt = s * 4 + mi
                        bk = nextbank()
                        for kc in range(16):
                            MM(psb[bk], wv[:, kc, mi * 128:(mi + 1) * 128], mbuf[:, kc, :], kc == 0, kc == 15,
                               [wr, ("m", kc)], ["ps%d" % bk], sig=(kc == 15))
                        TT("dve", xres[:, mt, :], xres[:, mt, :], psb[bk], ALU.add, [("x", mt), "ps%d" % bk], [("x", mt)])
                S.mute = kcut < 13
                S.tag = "ph13"
                rmsnorm_to_h(g0 + 16)
                for c in range(16):
                    STT(hbuf[:, c, :], xres[:, c, :], GS[:, g0 + 16 + c:g0 + 17 + c], tf[0], ALU.mult, ALU.mult,
                        [("x", c), "gs", "tf0"], [("h", c)])
                RM = [("m", c) for c in range(16)]
                for half in range(2):
                    for s in range(8):
                        wv, wr = load_slab("f1", l, 0, 16, half * 4096 + s * 512, 512)
                        bks = [nextbank() for _ in range(4)]
                        kco = (half == 0 and s == 0)
                        if kco:
                            for kc in range(16):
                                for mi in range(4):
                                    MM(psb[bks[mi]], wv[:, kc, mi * 128:(mi + 1) * 128], hbuf[:, kc, :], kc == 0, kc == 15,
                                       [wr, ("h", kc)], ["ps%d" % bks[mi]], sig=(kc == 15))
                        for mi in range(4):
                            bk = bks[mi]
                            for kc in range(0 if kco else 16):
                                MM(psb[bk], wv[:, kc, mi * 128:(mi + 1) * 128], hbuf[:, kc, :], kc == 0, kc == 15,
                                   [wr, ("h", kc)], ["ps%d" % bk], sig=(kc == 15))
                            ai = s * 4 + mi
                            tbi = ai % 3
                            ACT(tb[tbi], psb[bk], AF.Relu, ["ps%d" % bk], ["tb%d" % tbi])
                            extra = RM if (ai == 0) else []
                            TT("dve", abuf[:, ai, :], tb[tbi], tb[tbi], ALU.mult, ["tb%d" % tbi] + extra, [("a", ai)] + extra)
                    for cg in range(4):
                        bset = 4 if cg % 2 == 0 else 0
                        for kq in range(2):
                            wv, wr = load_slab("f2", l, half * 4096 + kq * 2048, 16, cg * 512, 512)
                            for mi in range(4):
                                bk = bset + mi
                                for kc in range(16):
                                    st = (kq == 0 and kc == 0)
                                    sp_ = (kq == 1 and kc == 15)
                                    MM(psb[bk], wv[:, kc, mi * 128:(mi + 1) * 128], abuf[:, kq * 16 + kc, :], st, sp_,
                                       [wr, ("a", kq * 16 + kc)], ["ps%d" % bk], sig=(kc == 15))
                        for mi in range(4):
                            mt = cg * 4 + mi
                            bk = bset + mi
                            TT("dve", xres[:, mt, :], xres[:, mt, :], psb[bk], ALU.add, [("x", mt), "ps%d" % bk], [("x", mt)])
                RA = [("a", i) for i in range(32)]
                S.op("dve", lambda e: e.memset(sc1, 0.0), RA + ["sc1"], RA + RZ + ["sc1"])
            S.mute = False
            S.tag = "final"
            rmsnorm_to_h(L * 32)
            for c in range(16):
                STT(ostage[:, c, :], xres[:, c, :], GS[:, L * 32 + c:L * 32 + c + 1], tf[0], ALU.mult, ALU.mult,
                    [("x", c), "gs", "tf0"] + (RZ if c == 0 else []), [("os", c)] + (RZ if c == 0 else []))
            S.dma("sp", lambda e, t0=t0: e.dma_start(out=outT[:, t0:t0 + NT].rearrange("(c p) n -> p c n", p=128), in_=ostage),
                  [("os", c) for c in range(16)], ["outdma"] + RZ)
            last_out_tok.append(S.lastw["outdma"])
        S.final_wait("sp", last_out_tok)

        sems = {}
        for sk in sorted(S.semkeys, key=str):
            sems[sk] = es.enter_context(nc.semaphore("s_%s_%d" % (sk[0], sk[1])))
        with nc.Block() as block:
            @block.tensor
            def _(e):
                S.replay("pe", e, sems)

            @block.scalar
            def _(e):
                S.replay("act", e, sems)

            @block.vector
            def _(e):
                S.replay("dve", e, sems)

            @block.gpsimd
            def _(e):
                S.replay("pool", e, sems)

            @block.sync
            def _(e):
                S.replay("sp", e, sems)
    nc._sched = S
    return nc


def host_pack(inp):
    f = np.float32
    L = DEPTH
    pl = np.zeros((L, 128, NPL), f)
    pbk = np.zeros((L, 128, 4, 32, 16), f)

    def col(v, n):
        return np.ascontiguousarray(v.reshape(n, 128).T)

    for l in range(L):
        pl[l, :, O_GMIX:O_GMIX + 16] = col(inp["norm_mix"][l], 16)
        pl[l, :, O_GMLP:O_GMLP + 16] = col(inp["norm_mlp"][l], 16)
        pl[l, :, O_BDW:O_BDW + 8] = col(inp["b_dw"][l], 8)
        pl[l, :, O_LNG:O_LNG + 8] = col(inp["ln_g"][l], 8)
        pl[l, :, O_LNB:O_LNB + 8] = col(inp["ln_b"][l], 8)
        wd = inp["w_dw"][l]
        pl[l, :, O_WDW:O_WDW + 248] = wd.T.reshape(8, 128, KW).transpose(1, 0, 2).reshape(128, 248)
        pl[l, :, O_DSK:O_DSK + 8] = col(inp["d_skip"][l], 8)

        def gp(a):
            return a.reshape(32, 2, 64).transpose(1, 2, 0).reshape(128, 32)
        pl[l, :, O_ARE:O_ARE + 32] = gp(inp["a_re"][l])
        pl[l, :, O_AIM:O_AIM + 32] = gp(inp["a_im"][l])
        pl[l, :, O_LDT:O_LDT + 32] = gp(np.repeat(inp["log_dt"][l][:, None], 64, axis=1))
        for i, nm in enumerate(("b_re", "b_im")):
            pbk[l, :, i] = inp[nm][l].reshape(32, 2, 64, 16).transpose(1, 2, 0, 3).reshape(128, 32, 16)
        for i, nm in enumerate(("c_re", "c_im")):
            pbk[l, :, 2 + i] = inp[nm][l].reshape(32, 2, 16, 64).transpose(1, 3, 0, 2).reshape(128, 32, 16)
    cst = np.zeros((128, 784), f)
    cst[:, 0:128] = np.eye(128, dtype=f)
    cst[:, 128:256] = 1.0
    blk = np.kron(np.eye(8, dtype=f), np.ones((16, 16), f))
    cst[:, 256:768] = np.tile(blk, (1, 4))
    cst[:, 768:784] = col(inp["norm_final"], 16)
    return pl, pbk.reshape(L, 128, 2048), cst


_NC_CACHE = {}


def kernel(**inputs):
    inp = {k: np.asarray(v) for k, v in inputs.items()}
    x = inp["x"]
    B = x.shape[0]
    pl, pbk, cst = host_pack(inp)
    if "nc" not in _NC_CACHE:
        _NC_CACHE["nc"] = build_nc()
    nc = _NC_CACHE["nc"]
    shared = {"w_in": inp["w_in"], "w_conv_out": inp["w_conv_out"], "w_glu": inp["w_glu"], "w_ssm_out": inp["w_ssm_out"],
              "w_out": inp["w_out"], "w_ff1": inp["w_ff1"], "w_ff2": inp["w_ff2"], "pl": pl, "pb": pbk, "cst": cst}
    in_maps = []
    for b in range(B):
        m = dict(shared)
        m["xT"] = np.ascontiguousarray(x[b].T)
        in_maps.append(m)
    res = run_bass_kernel_spmd(nc, in_maps, core_ids=list(range(B)))
    out = np.stack([np.ascontiguousarray(r["outT"].T) for r in res.results], axis=0)
    return out.astype(np.float32)
```

```python
import math
import numpy as np
import concourse.bass as bass
import concourse.mybir as mybir
from concourse.bass_utils import run_bass_kernel_spmd

F32 = mybir.dt.float32
BF16 = mybir.dt.bfloat16
AF = mybir.ActivationFunctionType
ALU = mybir.AluOpType

D = 2048
SEQ = 4096
DEPTH = 4
CWID = 1024
KW = 31
HIST = KW - 1
NT = 512
DFF = 8192
INC = 7168
EPS_RMS = 1e-6
EPS_LN = 1e-5
NPL = 160
EPOCH = 2000
NSLOT = 8

O_GMIX, O_GMLP, O_BDW, O_LNG, O_LNB, O_DSK, O_ARE, O_AIM, O_LDT = 0, 16, 32, 40, 48, 56, 64, 96, 128


class Sched:
    def __init__(self):
        self.streams = {e: [] for e in ("pe", "act", "dve", "pool", "sp")}
        self.nsig = {e: 0 for e in ("pe", "act", "dve", "pool")}
        self.lastw = {}
        self.readers = {}
        self.known = {e: {} for e in self.streams}
        self.dma_i = {"sp": 0, "pool": 0}
        self.semkeys = set()
        self.mute = False
        self.tag = "init"
        self.pe_tags = []

    def _need(self, reads, writes):
        need = {}

        def add(t):
            if t is None:
                return
            k, v = t
            if need.get(k, 0) < v:
                need[k] = v
        for r in reads:
            add(self.lastw.get(r))
        for w in writes:
            add(self.lastw.get(w))
            for t in self.readers.get(w, {}).items():
                add(t)
        return need

    def _waits(self, eng, need):
        waits = []
        kn = self.known[eng]
        for sk, v in need.items():
            if eng == "pe" and sk[0] == "pe":
                continue
            if sk[0] in self.nsig:
                if any(k2[0] == sk[0] and k2[1] > sk[1] for k2 in kn):
                    continue
            if kn.get(sk, 0) >= v:
                continue
            kn[sk] = v
            waits.append((sk, v))
        return waits

    def _commit(self, tok, reads, writes):
        for r in reads:
            d = self.readers.setdefault(r, {})
            if d.get(tok[0], 0) < tok[1]:
                d[tok[0]] = tok[1]
        for w in writes:
            self.lastw[w] = tok
            self.readers[w] = {}

    def op(self, eng, fn, reads=(), writes=(), sig=True):
        if self.mute:
            return
        if eng == "pe":
            self.pe_tags.append(self.tag)
        need = self._need(reads, writes)
        waits = self._waits(eng, need)
        n = self.nsig[eng]
        sk = (eng, n // EPOCH)
        tok = (sk, n % EPOCH + 1)
        self.semkeys.add(sk)
        self.streams[eng].append((waits, fn, (sk, 1) if sig else None))
        if sig:
            self.nsig[eng] = n + 1
        self._commit(tok, reads, writes)

    def dma(self, q, fn, reads=(), writes=(), extra=()):
        if self.mute:
            return
        i = self.dma_i[q]
        self.dma_i[q] = i + 1
        slot, use = i % NSLOT, i // NSLOT
        sk = ("d" + q, slot)
        self.semkeys.add(sk)
        need = self._need(reads, writes)
        for k_, v_ in extra:
            if need.get(k_, 0) < v_:
                need[k_] = v_
        if use > 0:
            if need.get(sk, 0) < 16 * use:
                need[sk] = 16 * use
        waits = self._waits(q, need)
        self.streams[q].append((waits, fn, (sk, 16)))
        self._commit((sk, 16 * (use + 1)), reads, writes)

    def final_wait(self, eng, toks):
        need = {}
        for k, v in toks:
            if need.get(k, 0) < v:
                need[k] = v
        waits = self._waits(eng, need)
        self.streams[eng].append((waits, None, None))

    def replay(self, eng, handle, sems):
        for waits, fn, inc in self.streams[eng]:
            for sk, v in waits:
                handle.wait_ge(sems[sk], v)
            if fn is None:
                continue
            ins = fn(handle)
            if inc is not None:
                ins.then_inc(sems[inc[0]], inc[1])


def build_nc(n_layers=DEPTH, n_tiles=SEQ // NT, dbg=False, kcut=99, kprep=True):
    nc = bass.Bass("TRN2", target_bir_lowering=False)
    S = Sched()
    L = n_layers
    T = n_tiles * NT

    def din(name, shape, dt=F32):
        return nc.dram_tensor(name, shape, dt, kind="ExternalInput").ap()

    def dscr(name, shape, dt=BF16):
        return nc.dram_tensor(name, shape, dt, kind="Internal").ap()

    xT = din("xT", [D, T])
    w_in = din("w_in", [DEPTH, D, INC])
    w_co = din("w_conv_out", [DEPTH, CWID, D])
    w_gl = din("w_glu", [DEPTH, CWID, CWID])
    w_so = din("w_ssm_out", [DEPTH, CWID, D])
    w_o = din("w_out", [DEPTH, D, D])
    w_f1 = din("w_ff1", [DEPTH, D, DFF])
    w_f2 = din("w_ff2", [DEPTH, DFF, D])
    pl = din("pl", [DEPTH, 128, NPL])
    pb = din("pb", [DEPTH, 128, 4 * 512])
    wd = din("wd", [DEPTH, 128, 8 * KW])
    cst = din("cst", [128, 128 + 128 + 512 + 16])
    outT = nc.dram_tensor("outT", [D, T], F32, kind="ExternalOutput").ap()
    dbg_out = None
    if dbg:
        dbg_out = nc.dram_tensor("dbg", [128, 16 * NT], F32, kind="ExternalOutput").ap()

    wsrc = {"in": (w_in, D, INC), "co": (w_co, CWID, D), "gl": (w_gl, CWID, CWID), "so": (w_so, CWID, D),
            "o": (w_o, D, D), "f1": (w_f1, D, DFF), "f2": (w_f2, DFF, D)}
    wb = {k: dscr("wb_" + k, [L, v[1], v[2]]) for k, v in wsrc.items()}
    zinm = dscr("zinm", [L, 128, 8, 2048])
    lcm = dscr("lcm", [L, 128, 8, 3072])
    dgm = dscr("dgm", [L, 128, 4, 2 * KW * 128])

    import contextlib
    es = contextlib.ExitStack()
    with es:
        ARENA_W = 52100
        arena = es.enter_context(nc.sbuf_tensor("arena", [128, ARENA_W], F32))
        ptr = [0]

        def alloc(words):
            a = ptr[0]
            ptr[0] += words
            assert ptr[0] <= ARENA_W, ptr[0]
            return a

        def vf(a, words):
            return arena[:, a:a + words]

        def vb(a, words):
            return arena[:, a:a + words].bitcast(BF16)

        a_pl = alloc(L * NPL)
        PL = vf(a_pl, L * NPL).rearrange("p (l n) -> p l n", l=L)
        a_c = alloc(128 + 128 + 512 + 16)
        CST = vf(a_c, 784)
        ident = CST[:, 0:128]
        ones_f = CST[:, 128:256]
        bmask = CST[:, 256:768]
        gfin = CST[:, 768:784]
        a_ob = alloc(64)
        ones_b = vb(a_ob, 64)
        a_gs = alloc(L * 32 + 16)
        GS = vf(a_gs, L * 32 + 16)
        a_mu = alloc(L * 128)
        MU = vf(a_mu, L * 128).rearrange("p (l t n) -> p l t n", l=L, t=2)
        a_st = alloc(L * 64)
        ST = vf(a_st, L * 64).rearrange("p (l n) -> p l n", l=L)
        a_hi = alloc(L * 8 * HIST)
        HI = vf(a_hi, L * 8 * HIST).rearrange("p (l c n) -> p l c n", l=L, c=8)
        a_np = alloc(3)
        negpi = vf(a_np, 1)
        epsr = vf(a_np + 1, 1)
        epsl = vf(a_np + 2, 1)
        a_main = ptr[0]
        a_x = alloc(16 * NT)
        xres = vf(a_x, 16 * NT).rearrange("p (c n) -> p c n", c=16)
        a_wb = [alloc(4096) for _ in range(3)]
        wbufs = [vb(a, 4096) for a in a_wb]
        a_h = alloc(4096)
        hbuf = vb(a_h, 4096).rearrange("p (c n) -> p c n", c=16)
        a_R = alloc(8432)
        uc = vb(a_R, 4 * (NT + HIST)).rearrange("p (c n) -> p c n", c=8)
        convo = vf(a_R + 4 * (NT + HIST), 4096).rearrange("p (c n) -> p c n", c=8)
        Z = vf(a_R, 4160).rearrange("p (t g c) -> p t g c", t=2, g=32)
        zbf = vb(a_R + 4160, 2048).rearrange("p (t g c) -> p t g c", t=2, g=32)
        yg = vb(a_R + 4160 + 2048, 2048).rearrange("p (c n) -> p c n", c=8)
        mbuf = vb(a_R, 4096).rearrange("p (c n) -> p c n", c=16)
        abuf = vb(a_R, 8192).rearrange("p (c n) -> p c n", c=32)
        ostage = vf(a_R, 8192).rearrange("p (c n) -> p c n", c=16)
        a_v = alloc(2048)
        vconv = vb(a_v, 2048).rearrange("p (c n) -> p c n", c=8)
        a_u = alloc(2048)
        ubuf = vb(a_u, 2048).rearrange("p (c n) -> p c n", c=8)
        a_zi = [alloc(1024) for _ in range(2)]
        zibuf = [vb(a, 1024).rearrange("p (j t n) -> p j t n", j=8, t=2) for a in a_zi]
        a_ubm = [alloc(1024) for _ in range(2)]
        ubm = [vb(a, 1024).rearrange("p (j r c) -> p j r c", j=8, r=4) for a in a_ubm]
        a_lc = [alloc(1536) for _ in range(2)]
        lcb = [vb(a, 1536) for a in a_lc]
        a_sq = [alloc(256) for _ in range(2)]
        sqb = [vb(a, 256) for a in a_sq]
        a_t = [alloc(512) for _ in range(4)]
        tf = [vf(a, 512) for a in a_t]
        sqf = [tf[2], tf[3]]
        a_tb = [alloc(256) for _ in range(4)]
        tb = [vb(a, 256) for a in a_tb]
        a_s1 = alloc(64)
        a_s2 = alloc(64)
        sc1 = vf(a_s1, 64).rearrange("p (t g) -> p t g", t=2)
        sc2 = vf(a_s2, 64).rearrange("p (t g) -> p t g", t=2)
        a_end = ptr[0]
        ptr[0] = a_main
        a_pb = alloc(2048)
        PB = vf(a_pb, 2048).rearrange("p (t g h) -> p t g h", t=4, g=32)
        sm = [vf(alloc(32), 32) for _ in range(24)]
        smi = vf(alloc(32), 32).bitcast(mybir.dt.int32)
        Pc = [vf(alloc(512), 512).rearrange("p (g h) -> p g h", g=32) for _ in range(4)]
        Wc = [vf(alloc(512), 512).rearrange("p (g h) -> p g h", g=32) for _ in range(4)]
        tmpc = [vf(alloc(512), 512).rearrange("p (g h) -> p g h", g=32) for _ in range(2)]
        Pexp = [vf(alloc(1024), 1024) for _ in range(2)]
        Cexp = [vf(alloc(1024), 1024) for _ in range(2)]
        Wexp = [vf(alloc(1024), 1024) for _ in range(2)]
        zin_sb = vb(alloc(8192), 8192).rearrange("p (q j t n) -> p q j t n", q=8, j=8, t=2)
        lc_sb = vb(alloc(12288), 12288).rearrange("p (q n) -> p q n", q=8)
        dg_sb = vb(alloc(KW * 128), KW * 128).rearrange("p (c k n) -> p c k n", c=2, k=KW)
        WD = vf(alloc(8 * KW), 8 * KW)
        assert ptr[0] <= ARENA_W, ptr[0]

        psb = [es.enter_context(nc.psum_tensor("ps%d" % i, [128, 512], F32))[:] for i in range(8)]
        pool_i = [0]

        def nextbank(lo=0, hi=4):
            i = lo + pool_i[0] % (hi - lo)
            pool_i[0] += 1
            return i

        def TT(eng, out, a, b, op, R, W):
            S.op(eng, lambda e: e.tensor_tensor(out=out, in0=a, in1=b, op=op), R, W)

        def TS(eng, out, a, s1, s2, op0, op1, R, W):
            if s2 is None:
                S.op(eng, lambda e: e.tensor_scalar(out=out, in0=a, scalar1=s1, scalar2=None, op0=op0), R, W)
            else:
                S.op(eng, lambda e: e.tensor_scalar(out=out, in0=a, scalar1=s1, scalar2=s2, op0=op0, op1=op1), R, W)

        def STT(out, a, sc, b, op0, op1, R, W):
            S.op("dve", lambda e: e.scalar_tensor_tensor(out=out, in0=a, scalar=sc, in1=b, op0=op0, op1=op1), R, W)

        def ACT(out, a, func, R, W, bias=None, scale=None):
            kw = {}
            if bias is not None:
                kw["bias"] = bias
            if scale is not None:
                kw["scale"] = scale
            S.op("act", lambda e: e.activation(out=out, in_=a, func=func, **kw), R, W)

        def CP(eng, out, a, R, W):
            if eng == "act":
                S.op("act", lambda e: e.copy(out=out, in_=a), R, W)
            else:
                S.op(eng, lambda e: e.tensor_copy(out=out, in_=a), R, W)

        def MM(out, lhsT, rhs, start, stop, R, W, sig=None, tp=None):
            kw = {}
            if tp is not None:
                kw["tile_position"] = tp
            S.op("pe", lambda e: e.matmul(out, lhsT, rhs, start=start, stop=stop, **kw), R, W,
                 sig=True if sig is None else sig)

        S.dma("sp", lambda e: e.dma_start(out=CST, in_=cst[:, :]), [], ["cst"])
        S.dma("sp", lambda e: e.dma_start(out=PL, in_=pl[0:L].rearrange("l p n -> p l n")), [], ["pl"])
        S.op("dve", lambda e: e.memset(negpi, -math.pi), [], ["negpi"])
        S.op("dve", lambda e: e.memset(epsr, D * EPS_RMS), ["negpi"], ["negpi"])
        S.op("dve", lambda e: e.memset(epsl, EPS_LN), ["negpi"], ["negpi"])
        CP("dve", ones_b, ones_f, ["cst"], ["ones_b"])
        S.op("dve", lambda e: e.memset(vf(a_st, L * 64), 0.0), [], ["ST%d" % l for l in range(L)])
        S.op("dve", lambda e: e.memset(vf(a_hi, L * 8 * HIST), 0.0), [], ["HI%d" % l for l in range(L)])
        for l in range(L):
            TS("dve", GS[:, l * 32:l * 32 + 32], PL[:, l, 0:32], math.sqrt(D), None, ALU.mult, None, ["pl"], ["gs"])
        TS("dve", GS[:, L * 32:L * 32 + 16], gfin, math.sqrt(D), None, ALU.mult, None, ["cst"], ["gs"])

        def wpieces(key):
            _, K_, N_ = wsrc[key]
            cw = 1792 if N_ == INC else min(N_, 2048)
            return [(r, c, cw) for r in range(K_ // 128) for c in range(N_ // cw)]

        def cast_items(l):
            items = []
            for key in ("in", "co", "gl", "so", "o", "f1", "f2"):
                src = wsrc[key][0]
                for (r, c, cw) in wpieces(key):
                    items.append((lambda e, key=key, src=src, l=l, r=r, c=c, cw=cw: e.dma_start(
                        out=wb[key][l, r * 128:(r + 1) * 128, c * cw:(c + 1) * cw],
                        in_=src[l, r * 128:(r + 1) * 128, c * cw:(c + 1) * cw]), ("wb", key, l, r, c)))
            return items

        pending_cast = []
        for fn_, res_ in cast_items(0):
            S.dma("pool", fn_, [], [res_])

        def pace_cast(n, extra=()):
            for _ in range(min(n, len(pending_cast))):
                fn_, res_ = pending_cast.pop(0)
                S.dma("pool", fn_, [], [res_], extra=extra)

        cnt = [0]

        def cmul(o_re, o_im, a_re, a_im, b_re, b_im, t0, t1, R, W, eng="dve"):
            big = (t0 is tmpc[0])
            r0, r1 = ("tmpc0", "tmpc1") if big else ("s_q0", "s_q1")
            TT(eng, t0, a_re, b_re, ALU.mult, R, [r0])
            TT(eng, t1, a_im, b_im, ALU.mult, R, [r1])
            TT(eng, o_re, t0, t1, ALU.subtract, [r0, r1], W[0:1])
            TT(eng, t0, a_re, b_im, ALU.mult, R, [r0])
            TT(eng, t1, a_im, b_re, ALU.mult, R, [r1])
            TT(eng, o_im, t0, t1, ALU.add, [r0, r1], W[1:2])

        def bc(t):
            return t.unsqueeze(2).to_broadcast([128, 32, 16])

        S.tag = "prep"
        for l in range(L if kprep else 0):
            are, aim, ldt = PL[:, l, O_ARE:O_ARE + 32], PL[:, l, O_AIM:O_AIM + 32], PL[:, l, O_LDT:O_LDT + 32]
            S.dma("sp", lambda e, l=l: e.dma_start(out=vf(a_pb, 2048), in_=pb[l]), [], ["PB"])
            dt_, xr, xi, mag, sa, ca, sn, cs, lr, li, den, rden, nr, kr, ki, q0, q1, q2, q3 = sm[0:19]
            ACT(dt_, ldt, AF.Exp, ["pl"], ["s_dt"])
            TT("dve", xr, are, dt_, ALU.mult, ["pl", "s_dt"], ["s_xr"])
            TT("dve", xi, aim, dt_, ALU.mult, ["pl", "s_dt"], ["s_xi"])
            ACT(mag, xr, AF.Exp, ["s_xr"], ["s_mag"])
            for (dst, off, nm) in ((sn, 0.0, "s_sn"), (cs, 0.25, "s_cs")):
                TS("dve", sa, xi, 1.0 / (2.0 * math.pi), off, ALU.mult, ALU.add, ["s_xi"], ["s_sa"])
                CP("dve", smi, sa, ["s_sa"], ["s_smi"])
                CP("dve", ca, smi, ["s_smi"], ["s_ca"])
                TT("dve", sa, sa, ca, ALU.subtract, ["s_sa", "s_ca"], ["s_sa"])
                ACT(ca, sa, AF.Sin, ["s_sa"], ["s_ca"], scale=math.pi)
                ACT(q3, sa, AF.Sin, ["s_sa"], ["s_q3"], scale=math.pi / 2.0)
                TT("dve", q3, q3, q3, ALU.mult, ["s_q3"], ["s_q3"])
                TS("dve", q3, q3, -4.0, 2.0, ALU.mult, ALU.add, ["s_q3"], ["s_q3"])
                TT("dve", dst, ca, q3, ALU.mult, ["s_ca", "s_q3"], [nm])
            TT("dve", lr, mag, cs, ALU.mult, ["s_mag", "s_cs"], ["s_lr"])
            TT("dve", li, mag, sn, ALU.mult, ["s_mag", "s_sn"], ["s_li"])
            TT("dve", q0, are, are, ALU.mult, ["pl"], ["s_q0"])
            TT("dve", q1, aim, aim, ALU.mult, ["pl"], ["s_q1"])
            TT("dve", den, q0, q1, ALU.add, ["s_q0", "s_q1"], ["s_den"])
            S.op("dve", lambda e: e.reciprocal(out=rden, in_=den), ["s_den"], ["s_rden"])
            TS("dve", nr, lr, -1.0, None, ALU.add, None, ["s_lr"], ["s_nr"])
            TT("dve", q0, nr, are, ALU.mult, ["s_nr", "pl"], ["s_q0"])
            TT("dve", q1, li, aim, ALU.mult, ["s_li", "pl"], ["s_q1"])
            TT("dve", q2, q0, q1, ALU.add, ["s_q0", "s_q1"], ["s_q2"])
            TT("dve", kr, q2, rden, ALU.mult, ["s_q2", "s_rden"], ["s_kr"])
            TT("dve", q0, li, are, ALU.mult, ["s_li", "pl"], ["s_q0"])
            TT("dve", q1, nr, aim, ALU.mult, ["s_nr", "pl"], ["s_q1"])
            TT("dve", q2, q0, q1, ALU.subtract, ["s_q0", "s_q1"], ["s_q2"])
            TT("dve", ki, q2, rden, ALU.mult, ["s_q2", "s_rden"], ["s_ki"])
            m_re, m_im, n_re, n_im = sm[19:23]
            cmul(m_re, m_im, lr, li, lr, li, q0, q1, ["s_lr", "s_li"], ["s_mre", "s_mim"])
            cmul(n_re, n_im, m_re, m_im, m_re, m_im, q0, q1, ["s_mre", "s_mim"], ["s_nre", "s_nim"])
            cmul(m_re, m_im, n_re, n_im, n_re, n_im, q0, q1, ["s_nre", "s_nim"], ["s_mre", "s_mim"])
            mr = "MU%d" % l
            CP("dve", MU[:, l, 0, 0:32], m_re, ["s_mre"], [mr + "a"])
            CP("dve", MU[:, l, 0, 32:64], m_re, ["s_mre"], [mr + "b"])
            TS("dve", MU[:, l, 1, 0:32], m_im, -1.0, None, ALU.mult, None, ["s_mim"], [mr + "c"])
            CP("dve", MU[:, l, 1, 32:64], m_im, ["s_mim"], [mr + "d"])
            Bre, Bim, Cre, Cim = PB[:, 0], PB[:, 1], PB[:, 2], PB[:, 3]
            cmul(Pc[0], Pc[1], bc(kr), bc(ki), Bre, Bim, tmpc[0], tmpc[1], ["s_kr", "s_ki", "PB"], ["P0r", "P0i"])
            for t_ in range(2):
                S.op("dve", lambda e, t_=t_: e.memset(Cexp[t_], 0.0), [], ["Cexp%d" % t_])
                S.op("dve", lambda e, t_=t_: e.memset(Pexp[t_], 0.0), [], ["Pexp%d" % t_])
                S.op("dve", lambda e, t_=t_: e.memset(Wexp[t_], 0.0), [], ["Wexp%d" % t_])

            def expand(dst, src, R, W, scale=None, eng="dve"):
                dv = dst.rearrange("p (g e h) -> p g e h", g=32, e=2)
                for e_ in range(2):
                    o = dv[64 * e_:64 * e_ + 64, :, e_, :]
                    i = src[64 * e_:64 * e_ + 64, :, :]
                    if scale is None:
                        CP(eng, o, i, R, W)
                    else:
                        TS(eng, o, i, scale, None, ALU.mult, None, R, W)

            expand(Cexp[0], Cre, ["PB"], ["Cexp0"])
            expand(Cexp[1], Cim, ["PB"], ["Cexp1"], scale=-1.0)
            cur = 0
            for k in range(8):
                Pr, Pi = Pc[2 * cur], Pc[2 * cur + 1]
                rn = ["P%dr" % cur, "P%di" % cur]
                expand(Pexp[0], Pr, [rn[0]], ["Pexp0"])
                expand(Pexp[1], Pi, [rn[1]], ["Pexp1"])
                for qh in range(2):
                    bk = nextbank()
                    for qq in range(4):
                        q = qh * 4 + qq
                        o = psb[bk][:, qq * 128:(qq + 1) * 128]
                        MM(o, Pexp[0][:, q * 128:(q + 1) * 128], Cexp[0][:, q * 128:(q + 1) * 128], True, False,
                           ["Pexp0", "Cexp0"], ["ps%d" % bk], sig=False)
                        MM(o, Pexp[1][:, q * 128:(q + 1) * 128], Cexp[1][:, q * 128:(q + 1) * 128], False, True,
                           ["Pexp1", "Cexp1"], ["ps%d" % bk], sig=True)
                    o4 = lc_sb[:, qh * 4:qh * 4 + 4, k * 128:(k + 1) * 128]
                    TT("dve", o4, psb[bk].rearrange("p (q n) -> p q n", q=4),
                       bmask.rearrange("p (q n) -> p q n", q=4), ALU.mult, ["ps%d" % bk, "cst"], ["lc_sb"])
                    for t_ in range(2):
                        bk2 = nextbank()
                        for qq in range(4):
                            q = qh * 4 + qq
                            S.op("pe", lambda e, bk2=bk2, qq=qq, q=q, t_=t_: e.transpose(
                                psb[bk2][:, qq * 128:(qq + 1) * 128], Pexp[t_][:, q * 128:(q + 1) * 128], ident),
                                ["Pexp%d" % t_, "cst"], ["ps%d" % bk2], sig=(qq == 3))
                        CP("act", zin_sb[:, qh * 4:qh * 4 + 4, 7 - k, t_, :],
                           psb[bk2].rearrange("p (q n) -> p q n", q=4), ["ps%d" % bk2], ["zin_sb"])
                if k < 7:
                    nx = 1 - cur
                    cmul(Pc[2 * nx], Pc[2 * nx + 1], bc(lr), bc(li), Pr, Pi, tmpc[0], tmpc[1],
                         ["s_lr", "s_li"] + rn, ["P%dr" % nx, "P%di" % nx])
                    cur = nx
            for q in range(8):
                STT(lc_sb[:, q, 0:128], ident, PL[:, l, O_DSK + q:O_DSK + q + 1], lc_sb[:, q, 0:128], ALU.mult, ALU.add,
                    ["cst", "pl", "lc_sb"], ["lc_sb"])
            wcur = 0
            srcr, srci = Cre, Cim
            srcn = ["PB", "PB"]
            for k in range(1, 9):
                nr_, ni_ = Wc[2 * wcur], Wc[2 * wcur + 1]
                wn = ["W%dr" % wcur, "W%di" % wcur]
                cmul(nr_, ni_, bc(lr), bc(li), srcr, srci, tmpc[0], tmpc[1], ["s_lr", "s_li"] + srcn, wn)
                expand(Wexp[0], nr_, [wn[0]], ["Wexp0"])
                expand(Wexp[1], ni_, [wn[1]], ["Wexp1"], scale=-1.0)
                for t_ in range(2):
                    o = lc_sb[:, :, 1024 + (k - 1) * 256 + t_ * 128:1024 + (k - 1) * 256 + t_ * 128 + 128]
                    CP("act", o, Wexp[t_].rearrange("p (q n) -> p q n", q=8), ["Wexp%d" % t_], ["lc_sb"])
                srcr, srci, srcn = nr_, ni_, wn
                wcur = 1 - wcur
            S.dma("sp", lambda e, l=l: e.dma_start(out=WD, in_=wd[l]), [], ["WD"])
            wdw_l = WD.rearrange("p (c k) -> p c k", c=8)
            for s4 in range(4):
                for c2 in range(2):
                    TT("dve", dg_sb[:, c2, :, :], ident.unsqueeze(1).to_broadcast([128, KW, 128]),
                       wdw_l[:, 2 * s4 + c2, :].unsqueeze(2).to_broadcast([128, KW, 128]), ALU.mult,
                       ["cst", "WD"], ["dg_sb"])
                S.dma("sp", lambda e, l=l, s4=s4: e.dma_start(out=dgm[l, :, s4, :], in_=dg_sb.rearrange("p c k n -> p (c k n)")),
                      ["dg_sb"], ["dgm%d" % l])
            S.dma("sp", lambda e, l=l: e.dma_start(out=zinm[l], in_=zin_sb.rearrange("p q j t n -> p q (j t n)")),
                  ["zin_sb"], ["zinm%d" % l])
            S.dma("sp", lambda e, l=l: e.dma_start(out=lcm[l], in_=lc_sb), ["lc_sb"], ["lcm%d" % l])

        alltok = []
        for e_ in ("pe", "act", "dve", "pool"):
            n = S.nsig[e_]
            if n > 0:
                alltok.append(((e_, (n - 1) // EPOCH), (n - 1) % EPOCH + 1))
        for q_ in ("sp", "pool"):
            i = S.dma_i[q_]
            for s_ in range(min(i, NSLOT)):
                uses = (i - s_ + NSLOT - 1) // NSLOT
                alltok.append((("d" + q_, s_), 16 * uses))
        for e_ in ("pe", "act", "dve", "sp"):
            S.final_wait(e_, alltok)
        for i_ in range(2):
            S.op("dve", lambda e, i_=i_: e.memset(ubm[i_], 0.0), [], ["ubm0"] + ["ubm_%d_%d" % (i_, r) for r in range(4)])

        def wres(key, l, r0, r1, c0, c1):
            _, K_, N_ = wsrc[key]
            cw = 1792 if N_ == INC else min(N_, 2048)
            return [("wb", key, l, r, c) for r in range(r0 // 128, (r1 + 127) // 128)
                    for c in range(c0 // cw, (c1 - 1) // cw + 1)]

        slab_i = [0]

        def load_slab(key, l, r0, nk, c0, ncol):
            i = slab_i[0] % 3
            slab_i[0] += 1
            if pending_cast:
                pace_cast(5, extra=list(S.readers.get("wbuf%d" % i, {}).items()))
            v = wbufs[i][:, 0:nk * ncol].rearrange("p (k n) -> p k n", k=nk)
            srcap = wb[key][l, r0:r0 + nk * 128, c0:c0 + ncol].rearrange("(k p) n -> p k n", p=128)
            S.dma("sp", lambda e: e.dma_start(out=v, in_=srcap), wres(key, l, r0, r0 + nk * 128, c0, c0 + ncol),
                  ["wbuf%d" % i])
            return v, "wbuf%d" % i

        def rmsnorm_to_h(goff):
            bk = nextbank()
            for c in range(16):
                s = sqb[c % 2]
                ACT(s, xres[:, c, :], AF.Square, [("x", c)], ["sqb%d" % (c % 2)])
                MM(psb[bk], ones_b, s, c == 0, c == 15, ["sqb%d" % (c % 2), "ones_b"], ["ps%d" % bk])
            ACT(tf[0], psb[bk], AF.Sqrt, ["ps%d" % bk, "negpi"], ["tf0"], bias=epsr, scale=1.0)
            S.op("dve", lambda e: e.reciprocal(out=tf[0], in_=tf[0]), ["tf0"], ["tf0"])
            return goff

        def dump(ap_src, R):
            pass

        last_out_tok = []
        for ti in range(n_tiles):
            t0 = ti * NT
            S.dma("sp", lambda e, t0=t0: e.dma_start(out=xres, in_=xT[:, t0:t0 + NT].rearrange("(c p) n -> p c n", p=128)),
                  [], [("x", c) for c in range(16)])
            for l in range(L):
                g0 = l * 32
                if ti == 0:
                    pace_cast(len(pending_cast))
                    if l + 1 < L:
                        pending_cast.extend(cast_items(l + 1))
                S.mute = kcut < 1
                S.tag = "ph1"
                rmsnorm_to_h(g0)
                for c in range(16):
                    STT(hbuf[:, c, :], xres[:, c, :], GS[:, g0 + c:g0 + c + 1], tf[0], ALU.mult, ALU.mult,
                        [("x", c), "gs", "tf0"], [("h", c)])
                HR = [("h", c) for c in range(16)]
                S.mute = kcut < 2
                S.tag = "ph2"
                for s in (2, 3, 0, 1):
                    wv, wr = load_slab("in", l, 0, 16, s * 512, 512)
                    bks = [nextbank() for _ in range(4)]
                    if s == 2:
                        for kc in range(16):
                            for mi in range(4):
                                MM(psb[bks[mi]], wv[:, kc, mi * 128:(mi + 1) * 128], hbuf[:, kc, :], kc == 0, kc == 15,
                                   [wr, ("h", kc)], ["ps%d" % bks[mi]], sig=(kc == 15))
                    for mi in range(4):
                        bk = bks[mi]
                        for kc in range(16 if s != 2 else 0):
                            MM(psb[bk], wv[:, kc, mi * 128:(mi + 1) * 128], hbuf[:, kc, :], kc == 0, kc == 15,
                               [wr, ("h", kc)], ["ps%d" % bk], sig=(kc == 15))
                        cc = (s % 2) * 4 + mi
                        if s >= 2:
                            ACT(vconv[:, cc, :], psb[bk], AF.Sigmoid, ["ps%d" % bk], [("v", cc)])
                        else:
                            TT("dve", uc[:, cc, HIST:HIST + NT], psb[bk], vconv[:, cc, :], ALU.mult,
                               ["ps%d" % bk, ("v", cc)], [("R", "uc", cc)])
                S.mute = kcut < 3
                S.tag = "ph3"
                for s in (4, 5):
                    wv, wr = load_slab("in", l, 0, 16, s * 512, 512)
                    for mi in range(4):
                        bk = nextbank()
                        for kc in range(16):
                            MM(psb[bk], wv[:, kc, mi * 128:(mi + 1) * 128], hbuf[:, kc, :], kc == 0, kc == 15,
                               [wr, ("h", kc)], ["ps%d" % bk], sig=(kc == 15))
                        q = (s - 4) * 4 + mi
                        CP("act", ubuf[:, q, :].rearrange("p (j c) -> p c j", j=8), psb[bk].rearrange("p (c j) -> p c j", j=8), ["ps%d" % bk], [("u", q)])
                S.mute = kcut < 4
                S.tag = "ph4"
                CP("dve", uc[:, :, 0:HIST], HI[:, l, :, :], ["HI%d" % l], [("R", "uc", cc) for cc in range(8)])
                for s4 in range(4):
                    i = slab_i[0] % 3
                    slab_i[0] += 1
                    dgv = wbufs[i][:, 0:2 * KW * 128].rearrange("p (c k n) -> p c k n", c=2, k=KW)
                    S.dma("sp", lambda e, l=l, s4=s4, i=i: e.dma_start(out=wbufs[i][:, 0:2 * KW * 128], in_=dgm[l, :, s4, :]),
                          ["dgm%d" % l], ["wbuf%d" % i])
                    for c2 in range(2):
                        cc = 2 * s4 + c2
                        bk = nextbank()
                        for k in range(KW):
                            MM(psb[bk], dgv[:, c2, k, :], uc[:, cc, k:k + NT], k == 0, k == KW - 1,
                               ["wbuf%d" % i, ("R", "uc", cc)], ["ps%d" % bk], sig=(k == KW - 1))
                        ACT(convo[:, cc, :], psb[bk], AF.Identity, ["ps%d" % bk, "pl"], [("R", "cv", cc)],
                            bias=PL[:, l, O_BDW + cc:O_BDW + cc + 1], scale=1.0)
                CP("dve", HI[:, l, :, :], uc[:, :, NT:NT + HIST], [("R", "uc", cc) for cc in range(8)], ["HI%d" % l])
                S.mute = kcut < 5
                S.tag = "ph5"
                bm, bs = nextbank(), nextbank()
                for cc in range(8):
                    MM(psb[bm], ones_f, convo[:, cc, :], cc == 0, cc == 7, [("R", "cv", cc), "cst"], ["ps%d" % bm])
                for cc in range(8):
                    s = sqf[cc % 2]
                    ACT(s, convo[:, cc, :], AF.Square, [("R", "cv", cc)], ["tf%d" % (2 + cc % 2)])
                    MM(psb[bs], ones_f, s, cc == 0, cc == 7, ["tf%d" % (2 + cc % 2), "cst"], ["ps%d" % bs])
                TS("dve", tf[1], psb[bm], 1.0 / CWID, None, ALU.mult, None, ["ps%d" % bm], ["tf1"])
                TT("dve", tf[2], tf[1], tf[1], ALU.mult, ["tf1"], ["tf2"])
                STT(tf[3], psb[bs], 1.0 / CWID, tf[2], ALU.mult, ALU.subtract, ["ps%d" % bs, "tf2"], ["tf3"])
                ACT(tf[2], tf[3], AF.Sqrt, ["tf3", "negpi"], ["tf2"], bias=epsl, scale=1.0)
                S.op("dve", lambda e: e.reciprocal(out=tf[2], in_=tf[2]), ["tf2"], ["tf2"])
                STT(tf[3], tf[1], -1.0, tf[2], ALU.mult, ALU.mult, ["tf1", "tf2"], ["tf3"])
                for cc in range(8):
                    TT("dve", convo[:, cc, :], convo[:, cc, :], tf[2], ALU.mult, [("R", "cv", cc), "tf2"], [("R", "cv", cc)])
                    TT("dve", convo[:, cc, :], convo[:, cc, :], tf[3], ALU.add, [("R", "cv", cc), "tf3"], [("R", "cv", cc)])
                    ACT(vconv[:, cc, :], convo[:, cc, :], AF.Silu, [("R", "cv", cc), "pl"], [("v", cc)],
                        bias=PL[:, l, O_LNB + cc:O_LNB + cc + 1], scale=PL[:, l, O_LNG + cc:O_LNG + cc + 1])
                S.mute = kcut < 6
                S.tag = "ph6"
                RZ = [("R", "uc", c) for c in range(8)] + [("R", "cv", c) for c in range(8)]
                for q in range(8):
                    zb = zibuf[q % 2]
                    S.dma("sp", lambda e, l=l, q=q, zb=zb: e.dma_start(out=zb.rearrange("p j t n -> p (j t n)"), in_=zinm[l, :, q, :]),
                          ["zinm%d" % l], ["zib%d" % (q % 2)])
                    uq = ubuf[:, q, :].rearrange("p (j c) -> p j c", j=8)
                    um = ubm[q % 2]
                    for r in range(4):
                        CP("act", um[32 * r:32 * r + 32, :, r, :], uq[32 * r:32 * r + 32, :, :], [("u", q), "ubm0"], ["ubm_%d_%d" % (q % 2, r)])
                    bk = 4 + (q % 2)
                    zv = psb[bk].rearrange("p (t r c) -> p t r c", t=2, r=4)
                    for t_ in range(2):
                        for j in range(8):
                            MM(psb[bk][:, t_ * 256:(t_ + 1) * 256], zb[:, j, t_, :], um[:, j, :, :].rearrange("p r c -> p (r c)"),
                               j == 0, j == 7, ["zib%d" % (q % 2)] + ["ubm_%d_%d" % (q % 2, r) for r in range(4)], ["ps%d" % bk],
                               sig=(j == 7))
                    for t_ in range(2):
                        first = (q == 0 and t_ == 0)
                        CP("act" if t_ == 0 else "dve", Z[:, t_, 4 * q:4 * q + 4, 1:65], zv[:, t_, :, :],
                           ["ps%d" % bk] + (RZ if first else []), ["Z"] + (RZ if first else []))
                S.mute = kcut < 7
                S.tag = "ph7"
                CP("dve", Z[:, :, :, 0], ST[:, l, :].rearrange("p (t g) -> p t g", t=2), ["ST%d" % l, "Z"], ["Z"])
                Am = MU[:, l, 0, :].rearrange("p (t g) -> p t g", t=2)
                Bmm = MU[:, l, 1, :].rearrange("p (t g) -> p t g", t=2)
                MR = ["MU%d%s" % (l, x) for x in "abcd"]
                for c in range(64):
                    TT("dve", sc1, Am, Z[:, :, :, c], ALU.mult, ["Z"] + MR, ["sc1"])
                    TT("dve", sc2, Bmm, Z[:, ::-1, :, c], ALU.mult, ["Z"] + MR, ["sc2"])
                    TT("dve", sc1, sc1, sc2, ALU.add, ["sc1", "sc2"], ["sc1"])
                    TT("dve", Z[:, :, :, c + 1], Z[:, :, :, c + 1], sc1, ALU.add, ["Z", "sc1"], ["Z"])
                CP("dve", ST[:, l, :].rearrange("p (t g) -> p t g", t=2), Z[:, :, :, 64], ["Z"], ["ST%d" % l])
                CP("act", zbf, Z[:, :, :, 0:64], ["Z"], ["zbf"])
                S.mute = kcut < 8
                S.tag = "ph8"
                for q in range(8):
                    lb = lcb[q % 2]
                    S.dma("sp", lambda e, l=l, q=q, lb=lb: e.dma_start(out=lb, in_=lcm[l, :, q, :]), ["lcm%d" % l], ["lcb%d" % (q % 2)])
                    bk = nextbank()
                    yp = psb[bk]
                    ypv = yp.rearrange("p (j c) -> p j c", j=8)
                    MM(yp, lb[:, 0:128], ubuf[:, q, :], True, False, ["lcb%d" % (q % 2), ("u", q)], ["ps%d" % bk], sig=False)
                    for k in range(1, 8):
                        MM(yp[:, k * 64:512], lb[:, k * 128:(k + 1) * 128], ubuf[:, q, 0:(8 - k) * 64], False, False,
                           ["lcb%d" % (q % 2), ("u", q)], ["ps%d" % bk], sig=False)
                    for r in range(4):
                        for j in range(8):
                            for t_ in range(2):
                                o = 1024 + j * 256 + t_ * 128 + r * 32
                                last = (r == 3 and j == 7 and t_ == 1)
                                MM(ypv[32 * r:32 * r + 32, j, :], lb[:, o:o + 32], zbf[:, t_, 4 * q + r, :], False, (j == 7 and t_ == 1),
                                   ["lcb%d" % (q % 2), "zbf"], ["ps%d" % bk], sig=last, tp=(0, 32 * r))
                    ACT(tf[0], yp, AF.Square, ["ps%d" % bk], ["tf0"])
                    TS("dve", tf[0], tf[0], 0.044715, 1.0, ALU.mult, ALU.add, ["tf0"], ["tf0"])
                    TT("dve", tf[0], tf[0], yp, ALU.mult, ["tf0", "ps%d" % bk], ["tf0"])
                    ACT(tf[1], tf[0], AF.Sigmoid, ["tf0"], ["tf1"], scale=2.0 * math.sqrt(2.0 / math.pi))
                    TT("dve", yg[:, q, :].rearrange("p (c j) -> p c j", j=8), tf[1].rearrange("p (j c) -> p c j", j=8), yp.rearrange("p (j c) -> p c j", j=8), ALU.mult, ["tf1", "ps%d" % bk], [("yg", q)])
                S.mute = kcut < 9
                S.tag = "ph9"
                wv, wr = load_slab("gl", l, 0, 8, 0, 1024)
                for mt in range(8):
                    bk = nextbank()
                    for kc in range(8):
                        MM(psb[bk], wv[:, kc, mt * 128:(mt + 1) * 128], yg[:, kc, :], kc == 0, kc == 7,
                           [wr, ("yg", kc)], ["ps%d" % bk], sig=(kc == 7))
                    tbi = mt % 3
                    ACT(tb[tbi], psb[bk], AF.Sigmoid, ["ps%d" % bk], ["tb%d" % tbi])
                    TT("dve", ubuf[:, mt, :], yg[:, mt, :], tb[tbi], ALU.mult, [("yg", mt), "tb%d" % tbi], [("u", mt)])
                S.mute = kcut < 10
                S.tag = "ph10"
                RY = [("yg", q) for q in range(8)] + ["Z", "zbf"]
                for grp in range(4):
                    gv, gr = load_slab("in", l, 0, 16, 3072 + grp * 512, 512)
                    wv, wr = load_slab("co", l, 0, 8, grp * 512, 512)
                    for mi in range(4):
                        bkg = nextbank()
                        for kc in range(16):
                            MM(psb[bkg], gv[:, kc, mi * 128:(mi + 1) * 128], hbuf[:, kc, :], kc == 0, kc == 15,
                               [gr, ("h", kc)], ["ps%d" % bkg], sig=(kc == 15))
                        ACT(tb[mi], psb[bkg], AF.Sigmoid, ["ps%d" % bkg], ["tb%d" % mi])
                    for mi in range(4):
                        mt = grp * 4 + mi
                        bk = nextbank()
                        for kc in range(8):
                            MM(psb[bk], wv[:, kc, mi * 128:(mi + 1) * 128], vconv[:, kc, :], kc == 0, kc == 7,
                               [wr, ("v", kc)], ["ps%d" % bk], sig=(kc == 7))
                        extra = RY + RZ if mt == 0 else []
                        TT("dve", mbuf[:, mt, :], psb[bk], tb[mi], ALU.mult, ["ps%d" % bk, "tb%d" % mi] + extra, [("m", mt)] + extra)
                S.mute = kcut < 11
                S.tag = "ph11"
                for grp in range(4):
                    gv, gr = load_slab("in", l, 0, 16, 5120 + grp * 512, 512)
                    wv, wr = load_slab("so", l, 0, 8, grp * 512, 512)
                    for mi in range(4):
                        bkg = nextbank()
                        for kc in range(16):
                            MM(psb[bkg], gv[:, kc, mi * 128:(mi + 1) * 128], hbuf[:, kc, :], kc == 0, kc == 15,
                               [gr, ("h", kc)], ["ps%d" % bkg], sig=(kc == 15))
                        ACT(tb[mi], psb[bkg], AF.Sigmoid, ["ps%d" % bkg], ["tb%d" % mi])
                    for mi in range(4):
                        mt = grp * 4 + mi
                        bk = nextbank()
                        for kc in range(8):
                            MM(psb[bk], wv[:, kc, mi * 128:(mi + 1) * 128], ubuf[:, kc, :], kc == 0, kc == 7,
                               [wr, ("u", kc)], ["ps%d" % bk], sig=(kc == 7))
                        TT("dve", tf[mt % 2], psb[bk], tb[mi], ALU.mult, ["ps%d" % bk, "tb%d" % mi], ["tf%d" % (mt % 2)])
                        TT("dve", mbuf[:, mt, :], mbuf[:, mt, :], tf[mt % 2], ALU.add, [("m", mt), "tf%d" % (mt % 2)], [("m", mt)])
                S.mute = kcut < 12
                S.tag = "ph12"
                for s in range(4):
                    wv, wr = load_slab("o", l, 0, 16, s * 512, 512)
                    for mi in range(4):
                        mt = s * 4 + mi
                        bk = nextbank()
                        for kc in range(16):
                            MM(psb[bk], wv[:, kc, mi * 128:(mi + 1) * 128], mbuf[:, kc, :], kc == 0, kc == 15,
                               [wr, ("m", kc)], ["ps%d" % bk], sig=(kc == 15))
                        TT("dve", xres[:, mt, :], xres[:, mt, :], psb[bk], ALU.add, [("x", mt), "ps%d" % bk], [("x", mt)])
                S.mute = kcut < 13
                S.tag = "ph13"
                rmsnorm_to_h(g0 + 16)
                for c in range(16):
                    STT(hbuf[:, c, :], xres[:, c, :], GS[:, g0 + 16 + c:g0 + 17 + c], tf[0], ALU.mult, ALU.mult,
                        [("x", c), "gs", "tf0"], [("h", c)])
                RM = [("m", c) for c in range(16)]
                for half in range(2):
                    for s in range(8):
                        wv, wr = load_slab("f1", l, 0, 16, half * 4096 + s * 512, 512)
                        bks = [nextbank() for _ in range(4)]
                        kco = (half == 0 and s == 0)
                        if kco:
                            for kc in range(16):
                                for mi in range(4):
                                    MM(psb[bks[mi]], wv[:, kc, mi * 128:(mi + 1) * 128], hbuf[:, kc, :], kc == 0, kc == 15,
                                       [wr, ("h", kc)], ["ps%d" % bks[mi]], sig=(kc == 15))
                        for mi in range(4):
                            bk = bks[mi]
                            for kc in range(0 if kco else 16):
                                MM(psb[bk], wv[:, kc, mi * 128:(mi + 1) * 128], hbuf[:, kc, :], kc == 0, kc == 15,
                                   [wr, ("h", kc)], ["ps%d" % bk], sig=(kc == 15))
                            ai = s * 4 + mi
                            tbi = ai % 3
                            ACT(tb[tbi], psb[bk], AF.Relu, ["ps%d" % bk], ["tb%d" % tbi])
                            extra = RM if (ai == 0) else []
                            TT("dve", abuf[:, ai, :], tb[tbi], tb[tbi], ALU.mult, ["tb%d" % tbi] + extra, [("a", ai)] + extra)
                    for cg in range(4):
                        bset = 4 if cg % 2 == 0 else 0
                        for kq in range(2):
                            wv, wr = load_slab("f2", l, half * 4096 + kq * 2048, 16, cg * 512, 512)
                            for mi in range(4):
                                bk = bset + mi
                                for kc in range(16):
                                    st = (kq == 0 and kc == 0)
                                    sp_ = (kq == 1 and kc == 15)
                                    MM(psb[bk], wv[:, kc, mi * 128:(mi + 1) * 128], abuf[:, kq * 16 + kc, :], st, sp_,
                                       [wr, ("a", kq * 16 + kc)], ["ps%d" % bk], sig=(kc == 15))
                        for mi in range(4):
                            mt = cg * 4 + mi
                            bk = bset + mi
                            TT("dve", xres[:, mt, :], xres[:, mt, :], psb[bk], ALU.add, [("x", mt), "ps%d" % bk], [("x", mt)])
                RA = [("a", i) for i in range(32)]
                S.op("dve", lambda e: e.memset(sc1, 0.0), RA + ["sc1"], RA + RZ + ["sc1"])
            S.mute = False
            S.tag = "final"
            rmsnorm_to_h(L * 32)
            for c in range(16):
                STT(ostage[:, c, :], xres[:, c, :], GS[:, L * 32 + c:L * 32 + c + 1], tf[0], ALU.mult, ALU.mult,
                    [("x", c), "gs", "tf0"] + (RZ if c == 0 else []), [("os", c)] + (RZ if c == 0 else []))
            S.dma("sp", lambda e, t0=t0: e.dma_start(out=outT[:, t0:t0 + NT].rearrange("(c p) n -> p c n", p=128), in_=ostage),
                  [("os", c) for c in range(16)], ["outdma"] + RZ)
            last_out_tok.append(S.lastw["outdma"])
        S.final_wait("sp", last_out_tok)

        sems = {}
        for sk in sorted(S.semkeys, key=str):
            sems[sk] = es.enter_context(nc.semaphore("s_%s_%d" % (sk[0], sk[1])))
        with nc.Block() as block:
            @block.tensor
            def _(e):
                S.replay("pe", e, sems)

            @block.scalar
            def _(e):
                S.replay("act", e, sems)

            @block.vector
            def _(e):
                S.replay("dve", e, sems)

            @block.gpsimd
            def _(e):
                S.replay("pool", e, sems)

            @block.sync
            def _(e):
                S.replay("sp", e, sems)
    nc._sched = S
    return nc


def host_pack(inp):
    f = np.float32
    L = DEPTH
    pl = np.zeros((L, 128, NPL), f)
    pbk = np.zeros((L, 128, 4, 32, 16), f)
    wdl = np.zeros((L, 128, 8 * KW), f)

    def col(v, n):
        return np.ascontiguousarray(v.reshape(n, 128).T)

    for l in range(L):
        pl[l, :, O_GMIX:O_GMIX + 16] = col(inp["norm_mix"][l], 16)
        pl[l, :, O_GMLP:O_GMLP + 16] = col(inp["norm_mlp"][l], 16)
        pl[l, :, O_BDW:O_BDW + 8] = col(inp["b_dw"][l], 8)
        pl[l, :, O_LNG:O_LNG + 8] = col(inp["ln_g"][l], 8)
        pl[l, :, O_LNB:O_LNB + 8] = col(inp["ln_b"][l], 8)
        wdl[l] = inp["w_dw"][l].T.reshape(8, 128, KW).transpose(1, 0, 2).reshape(128, 248)
        pl[l, :, O_DSK:O_DSK + 8] = col(inp["d_skip"][l], 8)

        def gp(a):
            return a.reshape(32, 2, 64).transpose(1, 2, 0).reshape(128, 32)
        pl[l, :, O_ARE:O_ARE + 32] = gp(inp["a_re"][l])
        pl[l, :, O_AIM:O_AIM + 32] = gp(inp["a_im"][l])
        pl[l, :, O_LDT:O_LDT + 32] = gp(np.repeat(inp["log_dt"][l][:, None], 64, axis=1))
        for i, nm in enumerate(("b_re", "b_im")):
            pbk[l, :, i] = inp[nm][l].reshape(32, 2, 64, 16).transpose(1, 2, 0, 3).reshape(128, 32, 16)
        for i, nm in enumerate(("c_re", "c_im")):
            pbk[l, :, 2 + i] = inp[nm][l].reshape(32, 2, 16, 64).transpose(1, 3, 0, 2).reshape(128, 32, 16)
    cst = np.zeros((128, 784), f)
    cst[:, 0:128] = np.eye(128, dtype=f)
    cst[:, 128:256] = 1.0
    blk = np.kron(np.eye(8, dtype=f), np.ones((16, 16), f))
    cst[:, 256:768] = np.tile(blk, (1, 4))
    cst[:, 768:784] = col(inp["norm_final"], 16)
    return pl, pbk.reshape(L, 128, 2048), cst, wdl


def make_shared(inp):
    pl, pbk, cst, wdl = host_pack(inp)
    return {"w_in": inp["w_in"], "w_conv_out": inp["w_conv_out"], "w_glu": inp["w_glu"], "w_ssm_out": inp["w_ssm_out"],
            "w_out": inp["w_out"], "w_ff1": inp["w_ff1"], "w_ff2": inp["w_ff2"], "pl": pl, "pb": pbk, "cst": cst, "wd": wdl}


_NC_CACHE = {}


def kernel(**inputs):
    inp = {k: np.asarray(v) for k, v in inputs.items()}
    x = inp["x"]
    B = x.shape[0]
    shared = make_shared(inp)
    if "nc" not in _NC_CACHE:
        _NC_CACHE["nc"] = build_nc()
    nc = _NC_CACHE["nc"]
    in_maps = []
    for b in range(B):
        m = dict(shared)
        m["xT"] = np.ascontiguousarray(x[b].T)
        in_maps.append(m)
    res = run_bass_kernel_spmd(nc, in_maps, core_ids=list(range(B)))
    out = np.stack([np.ascontiguousarray(r["outT"].T) for r in res.results], axis=0)
    return out.astype(np.float32)
```

```python
import math
import numpy as np
import concourse.bass as bass
import concourse.mybir as mybir
from concourse.bass_utils import run_bass_kernel_spmd

F32 = mybir.dt.float32
BF16 = mybir.dt.bfloat16
AF = mybir.ActivationFunctionType
ALU = mybir.AluOpType

D = 2048
SEQ = 4096
DEPTH = 4
CWID = 1024
KW = 31
HIST = KW - 1
NT = 512
DFF = 8192
INC = 7168
EPS_RMS = 1e-6
EPS_LN = 1e-5
NPL = 160
EPOCH = 2000
NSLOT = 8

O_GMIX, O_GMLP, O_BDW, O_LNG, O_LNB, O_DSK, O_ARE, O_AIM, O_LDT = 0, 16, 32, 40, 48, 56, 64, 96, 128


class Sched:
    def __init__(self):
        self.streams = {e: [] for e in ("pe", "act", "dve", "pool", "sp")}
        self.nsig = {e: 0 for e in ("pe", "act", "dve", "pool")}
        self.lastw = {}
        self.readers = {}
        self.known = {e: {} for e in self.streams}
        self.dma_i = {"sp": 0, "pool": 0}
        self.semkeys = set()
        self.mute = False
        self.tag = "init"
        self.pe_tags = []

    def _need(self, reads, writes):
        need = {}

        def add(t):
            if t is None:
                return
            k, v = t
            if need.get(k, 0) < v:
                need[k] = v
        for r in reads:
            add(self.lastw.get(r))
        for w in writes:
            add(self.lastw.get(w))
            for t in self.readers.get(w, {}).items():
                add(t)
        return need

    def _waits(self, eng, need):
        waits = []
        kn = self.known[eng]
        for sk, v in need.items():
            if eng == "pe" and sk[0] == "pe":
                continue
            if sk[0] in self.nsig:
                if any(k2[0] == sk[0] and k2[1] > sk[1] for k2 in kn):
                    continue
            if kn.get(sk, 0) >= v:
                continue
            kn[sk] = v
            waits.append((sk, v))
        return waits

    def _commit(self, tok, reads, writes):
        for r in reads:
            d = self.readers.setdefault(r, {})
            if d.get(tok[0], 0) < tok[1]:
                d[tok[0]] = tok[1]
        for w in writes:
            self.lastw[w] = tok
            self.readers[w] = {}

    def op(self, eng, fn, reads=(), writes=(), sig=True):
        if self.mute:
            return
        if eng == "pe":
            self.pe_tags.append(self.tag)
        need = self._need(reads, writes)
        waits = self._waits(eng, need)
        n = self.nsig[eng]
        sk = (eng, n // EPOCH)
        tok = (sk, n % EPOCH + 1)
        self.semkeys.add(sk)
        self.streams[eng].append((waits, fn, (sk, 1) if sig else None))
        if sig:
            self.nsig[eng] = n + 1
        self._commit(tok, reads, writes)

    def dma(self, q, fn, reads=(), writes=(), extra=()):
        if self.mute:
            return
        i = self.dma_i[q]
        self.dma_i[q] = i + 1
        slot, use = i % NSLOT, i // NSLOT
        sk = ("d" + q, slot)
        self.semkeys.add(sk)
        need = self._need(reads, writes)
        for k_, v_ in extra:
            if need.get(k_, 0) < v_:
                need[k_] = v_
        if use > 0:
            if need.get(sk, 0) < 16 * use:
                need[sk] = 16 * use
        waits = self._waits(q, need)
        self.streams[q].append((waits, fn, (sk, 16)))
        self._commit((sk, 16 * (use + 1)), reads, writes)

    def final_wait(self, eng, toks):
        need = {}
        for k, v in toks:
            if need.get(k, 0) < v:
                need[k] = v
        waits = self._waits(eng, need)
        self.streams[eng].append((waits, None, None))

    def replay(self, eng, handle, sems):
        for waits, fn, inc in self.streams[eng]:
            for sk, v in waits:
                handle.wait_ge(sems[sk], v)
            if fn is None:
                continue
            ins = fn(handle)
            if inc is not None:
                ins.then_inc(sems[inc[0]], inc[1])


def build_nc(n_layers=DEPTH, n_tiles=SEQ // NT, dbg=False, kcut=99, kprep=True):
    nc = bass.Bass("TRN2", target_bir_lowering=False)
    S = Sched()
    L = n_layers
    T = n_tiles * NT

    def din(name, shape, dt=F32):
        return nc.dram_tensor(name, shape, dt, kind="ExternalInput").ap()

    def dscr(name, shape, dt=BF16):
        return nc.dram_tensor(name, shape, dt, kind="Internal").ap()

    xT = din("xT", [D, T])
    w_in = din("w_in", [DEPTH, D, INC])
    w_co = din("w_conv_out", [DEPTH, CWID, D])
    w_gl = din("w_glu", [DEPTH, CWID, CWID])
    w_so = din("w_ssm_out", [DEPTH, CWID, D])
    w_o = din("w_out", [DEPTH, D, D])
    w_f1 = din("w_ff1", [DEPTH, D, DFF])
    w_f2 = din("w_ff2", [DEPTH, DFF, D])
    pl = din("pl", [DEPTH, 128, NPL])
    pb = din("pb", [DEPTH, 128, 4 * 512])
    wd = din("wd", [DEPTH, 128, 8 * KW])
    cst = din("cst", [128, 128 + 128 + 512 + 16])
    outT = nc.dram_tensor("outT", [D, T], F32, kind="ExternalOutput").ap()
    dbg_out = None
    if dbg:
        dbg_out = nc.dram_tensor("dbg", [128, 16 * NT], F32, kind="ExternalOutput").ap()

    wsrc = {"in": (w_in, D, INC), "co": (w_co, CWID, D), "gl": (w_gl, CWID, CWID), "so": (w_so, CWID, D),
            "o": (w_o, D, D), "f1": (w_f1, D, DFF), "f2": (w_f2, DFF, D)}
    wb = {k: dscr("wb_" + k, [L, v[1], v[2]]) for k, v in wsrc.items()}
    zinm = dscr("zinm", [L, 128, 8, 2048])
    lcm = dscr("lcm", [L, 128, 8, 3072])
    dgm = dscr("dgm", [L, 128, 4, 2 * KW * 128])

    import contextlib
    es = contextlib.ExitStack()
    with es:
        ARENA_W = 52100
        arena = es.enter_context(nc.sbuf_tensor("arena", [128, ARENA_W], F32))
        ptr = [0]

        def alloc(words):
            a = ptr[0]
            ptr[0] += words
            assert ptr[0] <= ARENA_W, ptr[0]
            return a

        def vf(a, words):
            return arena[:, a:a + words]

        def vb(a, words):
            return arena[:, a:a + words].bitcast(BF16)

        a_pl = alloc(L * NPL)
        PL = vf(a_pl, L * NPL).rearrange("p (l n) -> p l n", l=L)
        a_c = alloc(128 + 128 + 512 + 16)
        CST = vf(a_c, 784)
        ident = CST[:, 0:128]
        ones_f = CST[:, 128:256]
        bmask = CST[:, 256:768]
        gfin = CST[:, 768:784]
        a_ob = alloc(64)
        ones_b = vb(a_ob, 64)
        a_gs = alloc(L * 32 + 16)
        GS = vf(a_gs, L * 32 + 16)
        a_mu = alloc(L * 128)
        MU = vf(a_mu, L * 128).rearrange("p (l t n) -> p l t n", l=L, t=2)
        a_st = alloc(L * 64)
        ST = vf(a_st, L * 64).rearrange("p (l n) -> p l n", l=L)
        a_hi = alloc(L * 8 * HIST)
        HI = vf(a_hi, L * 8 * HIST).rearrange("p (l c n) -> p l c n", l=L, c=8)
        a_np = alloc(3)
        negpi = vf(a_np, 1)
        epsr = vf(a_np + 1, 1)
        epsl = vf(a_np + 2, 1)
        a_main = ptr[0]
        a_x = alloc(16 * NT)
        xres = vf(a_x, 16 * NT).rearrange("p (c n) -> p c n", c=16)
        a_wb = [alloc(4096) for _ in range(3)]
        wbufs = [vb(a, 4096) for a in a_wb]
        a_h = alloc(4096)
        hbuf = vb(a_h, 4096).rearrange("p (c n) -> p c n", c=16)
        a_R = alloc(8432)
        uc = vb(a_R, 4 * (NT + HIST)).rearrange("p (c n) -> p c n", c=8)
        convo = vf(a_R + 4 * (NT + HIST), 4096).rearrange("p (c n) -> p c n", c=8)
        Z = vf(a_R, 4160).rearrange("p (t g c) -> p t g c", t=2, g=32)
        zbf = vb(a_R + 4160, 2048).rearrange("p (t g c) -> p t g c", t=2, g=32)
        yg = vb(a_R + 4160 + 2048, 2048).rearrange("p (c n) -> p c n", c=8)
        mbuf = vb(a_R, 4096).rearrange("p (c n) -> p c n", c=16)
        abuf = vb(a_R, 8192).rearrange("p (c n) -> p c n", c=32)
        ostage = vf(a_R, 8192).rearrange("p (c n) -> p c n", c=16)
        a_v = alloc(2048)
        vconv = vb(a_v, 2048).rearrange("p (c n) -> p c n", c=8)
        a_u = alloc(2048)
        ubuf = vb(a_u, 2048).rearrange("p (c n) -> p c n", c=8)
        a_zi = [alloc(1024) for _ in range(2)]
        zibuf = [vb(a, 1024).rearrange("p (j t n) -> p j t n", j=8, t=2) for a in a_zi]
        a_ubm = [alloc(1024) for _ in range(2)]
        ubm = [vb(a, 1024).rearrange("p (j r c) -> p j r c", j=8, r=4) for a in a_ubm]
        a_lc = [alloc(1536) for _ in range(2)]
        lcb = [vb(a, 1536) for a in a_lc]
        a_sq = [alloc(256) for _ in range(2)]
        sqb = [vb(a, 256) for a in a_sq]
        a_t = [alloc(512) for _ in range(4)]
        tf = [vf(a, 512) for a in a_t]
        sqf = [tf[2], tf[3]]
        a_tb = [alloc(256) for _ in range(4)]
        tb = [vb(a, 256) for a in a_tb]
        a_s1 = alloc(64)
        a_s2 = alloc(64)
        sc1 = vf(a_s1, 64).rearrange("p (t g) -> p t g", t=2)
        sc2 = vf(a_s2, 64).rearrange("p (t g) -> p t g", t=2)
        a_end = ptr[0]
        ptr[0] = a_main
        a_pb = alloc(2048)
        PB = vf(a_pb, 2048).rearrange("p (t g h) -> p t g h", t=4, g=32)
        sm = [vf(alloc(32), 32) for _ in range(24)]
        smi = vf(alloc(32), 32).bitcast(mybir.dt.int32)
        Pc = [vf(alloc(512), 512).rearrange("p (g h) -> p g h", g=32) for _ in range(4)]
        Wc = [vf(alloc(512), 512).rearrange("p (g h) -> p g h", g=32) for _ in range(4)]
        tmpc = [vf(alloc(512), 512).rearrange("p (g h) -> p g h", g=32) for _ in range(2)]
        Pexp = [vf(alloc(1024), 1024) for _ in range(2)]
        Cexp = [vf(alloc(1024), 1024) for _ in range(2)]
        Wexp = [vf(alloc(1024), 1024) for _ in range(2)]
        zin_sb = vb(alloc(8192), 8192).rearrange("p (q j t n) -> p q j t n", q=8, j=8, t=2)
        lc_sb = vb(alloc(12288), 12288).rearrange("p (q n) -> p q n", q=8)
        dg_sb = vb(alloc(KW * 128), KW * 128).rearrange("p (c k n) -> p c k n", c=2, k=KW)
        WD = vf(alloc(8 * KW), 8 * KW)
        assert ptr[0] <= ARENA_W, ptr[0]

        psb = [es.enter_context(nc.psum_tensor("ps%d" % i, [128, 512], F32))[:] for i in range(8)]
        pool_i = [0]

        def nextbank(lo=0, hi=4):
            i = lo + pool_i[0] % (hi - lo)
            pool_i[0] += 1
            return i

        def TT(eng, out, a, b, op, R, W):
            S.op(eng, lambda e: e.tensor_tensor(out=out, in0=a, in1=b, op=op), R, W)

        def TS(eng, out, a, s1, s2, op0, op1, R, W):
            if s2 is None:
                S.op(eng, lambda e: e.tensor_scalar(out=out, in0=a, scalar1=s1, scalar2=None, op0=op0), R, W)
            else:
                S.op(eng, lambda e: e.tensor_scalar(out=out, in0=a, scalar1=s1, scalar2=s2, op0=op0, op1=op1), R, W)

        def STT(out, a, sc, b, op0, op1, R, W):
            S.op("dve", lambda e: e.scalar_tensor_tensor(out=out, in0=a, scalar=sc, in1=b, op0=op0, op1=op1), R, W)

        def ACT(out, a, func, R, W, bias=None, scale=None):
            kw = {}
            if bias is not None:
                kw["bias"] = bias
            if scale is not None:
                kw["scale"] = scale
            S.op("act", lambda e: e.activation(out=out, in_=a, func=func, **kw), R, W)

        def CP(eng, out, a, R, W):
            if eng == "act":
                S.op("act", lambda e: e.copy(out=out, in_=a), R, W)
            else:
                S.op(eng, lambda e: e.tensor_copy(out=out, in_=a), R, W)

        def MM(out, lhsT, rhs, start, stop, R, W, sig=None, tp=None):
            kw = {}
            if tp is not None:
                kw["tile_position"] = tp
            S.op("pe", lambda e: e.matmul(out, lhsT, rhs, start=start, stop=stop, **kw), R, W,
                 sig=True if sig is None else sig)

        S.dma("sp", lambda e: e.dma_start(out=CST, in_=cst[:, :]), [], ["cst"])
        S.dma("sp", lambda e: e.dma_start(out=PL, in_=pl[0:L].rearrange("l p n -> p l n")), [], ["pl"])
        S.op("dve", lambda e: e.memset(negpi, -math.pi), [], ["negpi"])
        S.op("dve", lambda e: e.memset(epsr, D * EPS_RMS), ["negpi"], ["negpi"])
        S.op("dve", lambda e: e.memset(epsl, EPS_LN), ["negpi"], ["negpi"])
        CP("dve", ones_b, ones_f, ["cst"], ["ones_b"])
        S.op("dve", lambda e: e.memset(vf(a_st, L * 64), 0.0), [], ["ST%d" % l for l in range(L)])
        S.op("dve", lambda e: e.memset(vf(a_hi, L * 8 * HIST), 0.0), [], ["HI%d" % l for l in range(L)])
        for l in range(L):
            TS("dve", GS[:, l * 32:l * 32 + 32], PL[:, l, 0:32], math.sqrt(D), None, ALU.mult, None, ["pl"], ["gs"])
        TS("dve", GS[:, L * 32:L * 32 + 16], gfin, math.sqrt(D), None, ALU.mult, None, ["cst"], ["gs"])

        def wpieces(key):
            _, K_, N_ = wsrc[key]
            cw = 1792 if N_ == INC else min(N_, 2048)
            return [(r, c, cw) for r in range(K_ // 128) for c in range(N_ // cw)]

        def cast_items(l):
            items = []
            for key in ("in", "co", "gl", "so", "o", "f1", "f2"):
                src = wsrc[key][0]
                for (r, c, cw) in wpieces(key):
                    items.append((lambda e, key=key, src=src, l=l, r=r, c=c, cw=cw: e.dma_start(
                        out=wb[key][l, r * 128:(r + 1) * 128, c * cw:(c + 1) * cw],
                        in_=src[l, r * 128:(r + 1) * 128, c * cw:(c + 1) * cw]), ("wb", key, l, r, c)))
            return items

        pending_cast = []
        for fn_, res_ in cast_items(0):
            S.dma("pool", fn_, [], [res_])

        def pace_cast(n, extra=()):
            for _ in range(min(n, len(pending_cast))):
                fn_, res_ = pending_cast.pop(0)
                S.dma("pool", fn_, [], [res_], extra=extra)

        cnt = [0]

        def cmul(o_re, o_im, a_re, a_im, b_re, b_im, t0, t1, R, W, eng="dve"):
            big = (t0 is tmpc[0])
            r0, r1 = ("tmpc0", "tmpc1") if big else ("s_q0", "s_q1")
            TT(eng, t0, a_re, b_re, ALU.mult, R, [r0])
            TT(eng, t1, a_im, b_im, ALU.mult, R, [r1])
            TT(eng, o_re, t0, t1, ALU.subtract, [r0, r1], W[0:1])
            TT(eng, t0, a_re, b_im, ALU.mult, R, [r0])
            TT(eng, t1, a_im, b_re, ALU.mult, R, [r1])
            TT(eng, o_im, t0, t1, ALU.add, [r0, r1], W[1:2])

        def bc(t):
            return t.unsqueeze(2).to_broadcast([128, 32, 16])

        S.tag = "prep"
        for l in range(L if kprep else 0):
            are, aim, ldt = PL[:, l, O_ARE:O_ARE + 32], PL[:, l, O_AIM:O_AIM + 32], PL[:, l, O_LDT:O_LDT + 32]
            S.dma("sp", lambda e, l=l: e.dma_start(out=vf(a_pb, 2048), in_=pb[l]), [], ["PB"])
            dt_, xr, xi, mag, sa, ca, sn, cs, lr, li, den, rden, nr, kr, ki, q0, q1, q2, q3 = sm[0:19]
            ACT(dt_, ldt, AF.Exp, ["pl"], ["s_dt"])
            TT("dve", xr, are, dt_, ALU.mult, ["pl", "s_dt"], ["s_xr"])
            TT("dve", xi, aim, dt_, ALU.mult, ["pl", "s_dt"], ["s_xi"])
            ACT(mag, xr, AF.Exp, ["s_xr"], ["s_mag"])
            for (dst, off, nm) in ((sn, 0.0, "s_sn"), (cs, 0.25, "s_cs")):
                TS("dve", sa, xi, 1.0 / (2.0 * math.pi), off, ALU.mult, ALU.add, ["s_xi"], ["s_sa"])
                CP("dve", smi, sa, ["s_sa"], ["s_smi"])
                CP("dve", ca, smi, ["s_smi"], ["s_ca"])
                TT("dve", sa, sa, ca, ALU.subtract, ["s_sa", "s_ca"], ["s_sa"])
                ACT(ca, sa, AF.Sin, ["s_sa"], ["s_ca"], scale=math.pi)
                ACT(q3, sa, AF.Sin, ["s_sa"], ["s_q3"], scale=math.pi / 2.0)
                TT("dve", q3, q3, q3, ALU.mult, ["s_q3"], ["s_q3"])
                TS("dve", q3, q3, -4.0, 2.0, ALU.mult, ALU.add, ["s_q3"], ["s_q3"])
                TT("dve", dst, ca, q3, ALU.mult, ["s_ca", "s_q3"], [nm])
            TT("dve", lr, mag, cs, ALU.mult, ["s_mag", "s_cs"], ["s_lr"])
            TT("dve", li, mag, sn, ALU.mult, ["s_mag", "s_sn"], ["s_li"])
            TT("dve", q0, are, are, ALU.mult, ["pl"], ["s_q0"])
            TT("dve", q1, aim, aim, ALU.mult, ["pl"], ["s_q1"])
            TT("dve", den, q0, q1, ALU.add, ["s_q0", "s_q1"], ["s_den"])
            S.op("dve", lambda e: e.reciprocal(out=rden, in_=den), ["s_den"], ["s_rden"])
            TS("dve", nr, lr, -1.0, None, ALU.add, None, ["s_lr"], ["s_nr"])
            TT("dve", q0, nr, are, ALU.mult, ["s_nr", "pl"], ["s_q0"])
            TT("dve", q1, li, aim, ALU.mult, ["s_li", "pl"], ["s_q1"])
            TT("dve", q2, q0, q1, ALU.add, ["s_q0", "s_q1"], ["s_q2"])
            TT("dve", kr, q2, rden, ALU.mult, ["s_q2", "s_rden"], ["s_kr"])
            TT("dve", q0, li, are, ALU.mult, ["s_li", "pl"], ["s_q0"])
            TT("dve", q1, nr, aim, ALU.mult, ["s_nr", "pl"], ["s_q1"])
            TT("dve", q2, q0, q1, ALU.subtract, ["s_q0", "s_q1"], ["s_q2"])
            TT("dve", ki, q2, rden, ALU.mult, ["s_q2", "s_rden"], ["s_ki"])
            m_re, m_im, n_re, n_im = sm[19:23]
            cmul(m_re, m_im, lr, li, lr, li, q0, q1, ["s_lr", "s_li"], ["s_mre", "s_mim"])
            cmul(n_re, n_im, m_re, m_im, m_re, m_im, q0, q1, ["s_mre", "s_mim"], ["s_nre", "s_nim"])
            cmul(m_re, m_im, n_re, n_im, n_re, n_im, q0, q1, ["s_nre", "s_nim"], ["s_mre", "s_mim"])
            mr = "MU%d" % l
            CP("dve", MU[:, l, 0, 0:32], m_re, ["s_mre"], [mr + "a"])
            CP("dve", MU[:, l, 0, 32:64], m_re, ["s_mre"], [mr + "b"])
            TS("dve", MU[:, l, 1, 0:32], m_im, -1.0, None, ALU.mult, None, ["s_mim"], [mr + "c"])
            CP("dve", MU[:, l, 1, 32:64], m_im, ["s_mim"], [mr + "d"])
            Bre, Bim, Cre, Cim = PB[:, 0], PB[:, 1], PB[:, 2], PB[:, 3]
            cmul(Pc[0], Pc[1], bc(kr), bc(ki), Bre, Bim, tmpc[0], tmpc[1], ["s_kr", "s_ki", "PB"], ["P0r", "P0i"])
            for t_ in range(2):
                S.op("dve", lambda e, t_=t_: e.memset(Cexp[t_], 0.0), [], ["Cexp%d" % t_])
                S.op("dve", lambda e, t_=t_: e.memset(Pexp[t_], 0.0), [], ["Pexp%d" % t_])
                S.op("dve", lambda e, t_=t_: e.memset(Wexp[t_], 0.0), [], ["Wexp%d" % t_])

            def expand(dst, src, R, W, scale=None, eng="dve"):
                dv = dst.rearrange("p (g e h) -> p g e h", g=32, e=2)
                for e_ in range(2):
                    o = dv[64 * e_:64 * e_ + 64, :, e_, :]
                    i = src[64 * e_:64 * e_ + 64, :, :]
                    if scale is None:
                        CP(eng, o, i, R, W)
                    else:
                        TS(eng, o, i, scale, None, ALU.mult, None, R, W)

            expand(Cexp[0], Cre, ["PB"], ["Cexp0"])
            expand(Cexp[1], Cim, ["PB"], ["Cexp1"], scale=-1.0)
            cur = 0
            for k in range(8):
                Pr, Pi = Pc[2 * cur], Pc[2 * cur + 1]
                rn = ["P%dr" % cur, "P%di" % cur]
                expand(Pexp[0], Pr, [rn[0]], ["Pexp0"])
                expand(Pexp[1], Pi, [rn[1]], ["Pexp1"])
                for qh in range(2):
                    bk = nextbank()
                    for qq in range(4):
                        q = qh * 4 + qq
                        o = psb[bk][:, qq * 128:(qq + 1) * 128]
                        MM(o, Pexp[0][:, q * 128:(q + 1) * 128], Cexp[0][:, q * 128:(q + 1) * 128], True, False,
                           ["Pexp0", "Cexp0"], ["ps%d" % bk], sig=False)
                        MM(o, Pexp[1][:, q * 128:(q + 1) * 128], Cexp[1][:, q * 128:(q + 1) * 128], False, True,
                           ["Pexp1", "Cexp1"], ["ps%d" % bk], sig=True)
                    o4 = lc_sb[:, qh * 4:qh * 4 + 4, k * 128:(k + 1) * 128]
                    TT("dve", o4, psb[bk].rearrange("p (q n) -> p q n", q=4),
                       bmask.rearrange("p (q n) -> p q n", q=4), ALU.mult, ["ps%d" % bk, "cst"], ["lc_sb"])
                    for t_ in range(2):
                        bk2 = nextbank()
                        for qq in range(4):
                            q = qh * 4 + qq
                            S.op("pe", lambda e, bk2=bk2, qq=qq, q=q, t_=t_: e.transpose(
                                psb[bk2][:, qq * 128:(qq + 1) * 128], Pexp[t_][:, q * 128:(q + 1) * 128], ident),
                                ["Pexp%d" % t_, "cst"], ["ps%d" % bk2], sig=(qq == 3))
                        CP("act", zin_sb[:, qh * 4:qh * 4 + 4, 7 - k, t_, :],
                           psb[bk2].rearrange("p (q n) -> p q n", q=4), ["ps%d" % bk2], ["zin_sb"])
                if k < 7:
                    nx = 1 - cur
                    cmul(Pc[2 * nx], Pc[2 * nx + 1], bc(lr), bc(li), Pr, Pi, tmpc[0], tmpc[1],
                         ["s_lr", "s_li"] + rn, ["P%dr" % nx, "P%di" % nx])
                    cur = nx
            for q in range(8):
                STT(lc_sb[:, q, 0:128], ident, PL[:, l, O_DSK + q:O_DSK + q + 1], lc_sb[:, q, 0:128], ALU.mult, ALU.add,
                    ["cst", "pl", "lc_sb"], ["lc_sb"])
            wcur = 0
            srcr, srci = Cre, Cim
            srcn = ["PB", "PB"]
            for k in range(1, 9):
                nr_, ni_ = Wc[2 * wcur], Wc[2 * wcur + 1]
                wn = ["W%dr" % wcur, "W%di" % wcur]
                cmul(nr_, ni_, bc(lr), bc(li), srcr, srci, tmpc[0], tmpc[1], ["s_lr", "s_li"] + srcn, wn)
                expand(Wexp[0], nr_, [wn[0]], ["Wexp0"])
                expand(Wexp[1], ni_, [wn[1]], ["Wexp1"], scale=-1.0)
                for t_ in range(2):
                    o = lc_sb[:, :, 1024 + (k - 1) * 256 + t_ * 128:1024 + (k - 1) * 256 + t_ * 128 + 128]
                    CP("act", o, Wexp[t_].rearrange("p (q n) -> p q n", q=8), ["Wexp%d" % t_], ["lc_sb"])
                srcr, srci, srcn = nr_, ni_, wn
                wcur = 1 - wcur
            S.dma("sp", lambda e, l=l: e.dma_start(out=WD, in_=wd[l]), [], ["WD"])
            wdw_l = WD.rearrange("p (c k) -> p c k", c=8)
            for s4 in range(4):
                for c2 in range(2):
                    TT("dve", dg_sb[:, c2, :, :], ident.unsqueeze(1).to_broadcast([128, KW, 128]),
                       wdw_l[:, 2 * s4 + c2, :].unsqueeze(2).to_broadcast([128, KW, 128]), ALU.mult,
                       ["cst", "WD"], ["dg_sb"])
                S.dma("sp", lambda e, l=l, s4=s4: e.dma_start(out=dgm[l, :, s4, :], in_=dg_sb.rearrange("p c k n -> p (c k n)")),
                      ["dg_sb"], ["dgm%d" % l])
            S.dma("sp", lambda e, l=l: e.dma_start(out=zinm[l], in_=zin_sb.rearrange("p q j t n -> p q (j t n)")),
                  ["zin_sb"], ["zinm%d" % l])
            S.dma("sp", lambda e, l=l: e.dma_start(out=lcm[l], in_=lc_sb), ["lc_sb"], ["lcm%d" % l])

        alltok = []
        for e_ in ("pe", "act", "dve", "pool"):
            n = S.nsig[e_]
            if n > 0:
                alltok.append(((e_, (n - 1) // EPOCH), (n - 1) % EPOCH + 1))
        for q_ in ("sp", "pool"):
            i = S.dma_i[q_]
            for s_ in range(min(i, NSLOT)):
                uses = (i - s_ + NSLOT - 1) // NSLOT
                alltok.append((("d" + q_, s_), 16 * uses))
        for e_ in ("pe", "act", "dve", "sp"):
            S.final_wait(e_, alltok)
        for i_ in range(2):
            S.op("dve", lambda e, i_=i_: e.memset(ubm[i_], 0.0), [], ["ubm0"] + ["ubm_%d_%d" % (i_, r) for r in range(4)])

        def wres(key, l, r0, r1, c0, c1):
            _, K_, N_ = wsrc[key]
            cw = 1792 if N_ == INC else min(N_, 2048)
            return [("wb", key, l, r, c) for r in range(r0 // 128, (r1 + 127) // 128)
                    for c in range(c0 // cw, (c1 - 1) // cw + 1)]

        slab_i = [0]

        def load_slab(key, l, r0, nk, c0, ncol):
            i = slab_i[0] % 3
            slab_i[0] += 1
            if pending_cast:
                pace_cast(5, extra=list(S.readers.get("wbuf%d" % i, {}).items()))
            v = wbufs[i][:, 0:nk * ncol].rearrange("p (k n) -> p k n", k=nk)
            srcap = wb[key][l, r0:r0 + nk * 128, c0:c0 + ncol].rearrange("(k p) n -> p k n", p=128)
            S.dma("sp", lambda e: e.dma_start(out=v, in_=srcap), wres(key, l, r0, r0 + nk * 128, c0, c0 + ncol),
                  ["wbuf%d" % i])
            return v, "wbuf%d" % i

        def rmsnorm_to_h(goff):
            bk = nextbank()
            for c in range(16):
                s = sqb[c % 2]
                ACT(s, xres[:, c, :], AF.Square, [("x", c)], ["sqb%d" % (c % 2)])
                MM(psb[bk], ones_b, s, c == 0, c == 15, ["sqb%d" % (c % 2), "ones_b"], ["ps%d" % bk])
            ACT(tf[0], psb[bk], AF.Sqrt, ["ps%d" % bk, "negpi"], ["tf0"], bias=epsr, scale=1.0)
            S.op("dve", lambda e: e.reciprocal(out=tf[0], in_=tf[0]), ["tf0"], ["tf0"])
            return goff

        def dump(ap_src, R):
            pass

        last_out_tok = []
        for ti in range(n_tiles):
            t0 = ti * NT
            S.dma("sp", lambda e, t0=t0: e.dma_start(out=xres, in_=xT[:, t0:t0 + NT].rearrange("(c p) n -> p c n", p=128)),
                  [], [("x", c) for c in range(16)])
            for l in range(L):
                g0 = l * 32
                if ti == 0:
                    pace_cast(len(pending_cast))
                    if l + 1 < L:
                        pending_cast.extend(cast_items(l + 1))
                S.mute = kcut < 1
                S.tag = "ph1"
                rmsnorm_to_h(g0)
                for c in range(16):
                    STT(hbuf[:, c, :], xres[:, c, :], GS[:, g0 + c:g0 + c + 1], tf[0], ALU.mult, ALU.mult,
                        [("x", c), "gs", "tf0"], [("h", c)])
                HR = [("h", c) for c in range(16)]
                S.mute = kcut < 2
                S.tag = "ph2"
                for s in (2, 3, 0, 1):
                    wv, wr = load_slab("in", l, 0, 16, s * 512, 512)
                    bks = [nextbank() for _ in range(4)]
                    if s == 2:
                        for kc in range(16):
                            for mi in range(4):
                                MM(psb[bks[mi]], wv[:, kc, mi * 128:(mi + 1) * 128], hbuf[:, kc, :], kc == 0, kc == 15,
                                   [wr, ("h", kc)], ["ps%d" % bks[mi]], sig=(kc == 15))
                    for mi in range(4):
                        bk = bks[mi]
                        for kc in range(16 if s != 2 else 0):
                            MM(psb[bk], wv[:, kc, mi * 128:(mi + 1) * 128], hbuf[:, kc, :], kc == 0, kc == 15,
                               [wr, ("h", kc)], ["ps%d" % bk], sig=(kc == 15))
                        cc = (s % 2) * 4 + mi
                        if s >= 2:
                            ACT(vconv[:, cc, :], psb[bk], AF.Sigmoid, ["ps%d" % bk], [("v", cc)])
                        else:
                            TT("dve", uc[:, cc, HIST:HIST + NT], psb[bk], vconv[:, cc, :], ALU.mult,
                               ["ps%d" % bk, ("v", cc)], [("R", "uc", cc)])
                S.mute = kcut < 3
                S.tag = "ph3"
                for s in (4, 5):
                    wv, wr = load_slab("in", l, 0, 16, s * 512, 512)
                    for mi in range(4):
                        bk = nextbank()
                        for kc in range(16):
                            MM(psb[bk], wv[:, kc, mi * 128:(mi + 1) * 128], hbuf[:, kc, :], kc == 0, kc == 15,
                               [wr, ("h", kc)], ["ps%d" % bk], sig=(kc == 15))
                        q = (s - 4) * 4 + mi
                        CP("act", ubuf[:, q, :].rearrange("p (j c) -> p c j", j=8), psb[bk].rearrange("p (c j) -> p c j", j=8), ["ps%d" % bk], [("u", q)])
                S.mute = kcut < 4
                S.tag = "ph4"
                CP("dve", uc[:, :, 0:HIST], HI[:, l, :, :], ["HI%d" % l], [("R", "uc", cc) for cc in range(8)])
                for s4 in range(4):
                    i = slab_i[0] % 3
                    slab_i[0] += 1
                    dgv = wbufs[i][:, 0:2 * KW * 128].rearrange("p (c k n) -> p c k n", c=2, k=KW)
                    S.dma("sp", lambda e, l=l, s4=s4, i=i: e.dma_start(out=wbufs[i][:, 0:2 * KW * 128], in_=dgm[l, :, s4, :]),
                          ["dgm%d" % l], ["wbuf%d" % i])
                    for c2 in range(2):
                        cc = 2 * s4 + c2
                        bk = nextbank()
                        for k in range(KW):
                            MM(psb[bk], dgv[:, c2, k, :], uc[:, cc, k:k + NT], k == 0, k == KW - 1,
                               ["wbuf%d" % i, ("R", "uc", cc)], ["ps%d" % bk], sig=(k == KW - 1))
                        ACT(convo[:, cc, :], psb[bk], AF.Identity, ["ps%d" % bk, "pl"], [("R", "cv", cc)],
                            bias=PL[:, l, O_BDW + cc:O_BDW + cc + 1], scale=1.0)
                CP("dve", HI[:, l, :, :], uc[:, :, NT:NT + HIST], [("R", "uc", cc) for cc in range(8)], ["HI%d" % l])
                S.mute = kcut < 5
                S.tag = "ph5"
                bm, bs = nextbank(), nextbank()
                for cc in range(8):
                    MM(psb[bm], ones_f, convo[:, cc, :], cc == 0, cc == 7, [("R", "cv", cc), "cst"], ["ps%d" % bm])
                for cc in range(8):
                    s = sqf[cc % 2]
                    ACT(s, convo[:, cc, :], AF.Square, [("R", "cv", cc)], ["tf%d" % (2 + cc % 2)])
                    MM(psb[bs], ones_f, s, cc == 0, cc == 7, ["tf%d" % (2 + cc % 2), "cst"], ["ps%d" % bs])
                TS("dve", tf[1], psb[bm], 1.0 / CWID, None, ALU.mult, None, ["ps%d" % bm], ["tf1"])
                TT("dve", tf[2], tf[1], tf[1], ALU.mult, ["tf1"], ["tf2"])
                STT(tf[3], psb[bs], 1.0 / CWID, tf[2], ALU.mult, ALU.subtract, ["ps%d" % bs, "tf2"], ["tf3"])
                ACT(tf[2], tf[3], AF.Sqrt, ["tf3", "negpi"], ["tf2"], bias=epsl, scale=1.0)
                S.op("dve", lambda e: e.reciprocal(out=tf[2], in_=tf[2]), ["tf2"], ["tf2"])
                STT(tf[3], tf[1], -1.0, tf[2], ALU.mult, ALU.mult, ["tf1", "tf2"], ["tf3"])
                for cc in range(8):
                    TT("dve", convo[:, cc, :], convo[:, cc, :], tf[2], ALU.mult, [("R", "cv", cc), "tf2"], [("R", "cv", cc)])
                    TT("dve", convo[:, cc, :], convo[:, cc, :], tf[3], ALU.add, [("R", "cv", cc), "tf3"], [("R", "cv", cc)])
                    ACT(vconv[:, cc, :], convo[:, cc, :], AF.Silu, [("R", "cv", cc), "pl"], [("v", cc)],
                        bias=PL[:, l, O_LNB + cc:O_LNB + cc + 1], scale=PL[:, l, O_LNG + cc:O_LNG + cc + 1])
                S.mute = kcut < 6
                S.tag = "ph6"
                RZ = [("R", "uc", c) for c in range(8)] + [("R", "cv", c) for c in range(8)]
                for q in range(8):
                    zb = zibuf[q % 2]
                    S.dma("sp", lambda e, l=l, q=q, zb=zb: e.dma_start(out=zb.rearrange("p j t n -> p (j t n)"), in_=zinm[l, :, q, :]),
                          ["zinm%d" % l], ["zib%d" % (q % 2)])
                    uq = ubuf[:, q, :].rearrange("p (j c) -> p j c", j=8)
                    um = ubm[q % 2]
                    for r in range(4):
                        CP("act", um[32 * r:32 * r + 32, :, r, :], uq[32 * r:32 * r + 32, :, :], [("u", q), "ubm0"], ["ubm_%d_%d" % (q % 2, r)])
                    bk = 4 + (q % 2)
                    zv = psb[bk].rearrange("p (t r c) -> p t r c", t=2, r=4)
                    for t_ in range(2):
                        for j in range(8):
                            MM(psb[bk][:, t_ * 256:(t_ + 1) * 256], zb[:, j, t_, :], um[:, j, :, :].rearrange("p r c -> p (r c)"),
                               j == 0, j == 7, ["zib%d" % (q % 2)] + ["ubm_%d_%d" % (q % 2, r) for r in range(4)], ["ps%d" % bk],
                               sig=(j == 7))
                    for t_ in range(2):
                        first = (q == 0 and t_ == 0)
                        CP("act" if t_ == 0 else "dve", Z[:, t_, 4 * q:4 * q + 4, 1:65], zv[:, t_, :, :],
                           ["ps%d" % bk] + (RZ if first else []), ["Z"] + (RZ if first else []))
                S.mute = kcut < 7
                S.tag = "ph7"
                CP("dve", Z[:, :, :, 0], ST[:, l, :].rearrange("p (t g) -> p t g", t=2), ["ST%d" % l, "Z"], ["Z"])
                Am = MU[:, l, 0, :].rearrange("p (t g) -> p t g", t=2)
                Bmm = MU[:, l, 1, :].rearrange("p (t g) -> p t g", t=2)
                MR = ["MU%d%s" % (l, x) for x in "abcd"]
                for c in range(64):
                    TT("dve", sc1, Am, Z[:, :, :, c], ALU.mult, ["Z"] + MR, ["sc1"])
                    TT("dve", sc2, Bmm, Z[:, ::-1, :, c], ALU.mult, ["Z"] + MR, ["sc2"])
                    TT("dve", Z[:, :, :, c + 1], Z[:, :, :, c + 1], sc1, ALU.add, ["Z", "sc1"], ["Z"])
                    TT("dve", Z[:, :, :, c + 1], Z[:, :, :, c + 1], sc2, ALU.add, ["Z", "sc2"], ["Z"])
                CP("dve", ST[:, l, :].rearrange("p (t g) -> p t g", t=2), Z[:, :, :, 64], ["Z"], ["ST%d" % l])
                CP("act", zbf, Z[:, :, :, 0:64], ["Z"], ["zbf"])
                S.mute = kcut < 8
                S.tag = "ph8"
                for q in range(8):
                    lb = lcb[q % 2]
                    S.dma("sp", lambda e, l=l, q=q, lb=lb: e.dma_start(out=lb, in_=lcm[l, :, q, :]), ["lcm%d" % l], ["lcb%d" % (q % 2)])
                    bk = nextbank()
                    yp = psb[bk]
                    ypv = yp.rearrange("p (j c) -> p j c", j=8)
                    MM(yp, lb[:, 0:128], ubuf[:, q, :], True, False, ["lcb%d" % (q % 2), ("u", q)], ["ps%d" % bk], sig=False)
                    for k in range(1, 8):
                        MM(yp[:, k * 64:512], lb[:, k * 128:(k + 1) * 128], ubuf[:, q, 0:(8 - k) * 64], False, False,
                           ["lcb%d" % (q % 2), ("u", q)], ["ps%d" % bk], sig=False)
                    for r in range(4):
                        for j in range(8):
                            for t_ in range(2):
                                o = 1024 + j * 256 + t_ * 128 + r * 32
                                last = (r == 3 and j == 7 and t_ == 1)
                                MM(ypv[32 * r:32 * r + 32, j, :], lb[:, o:o + 32], zbf[:, t_, 4 * q + r, :], False, (j == 7 and t_ == 1),
                                   ["lcb%d" % (q % 2), "zbf"], ["ps%d" % bk], sig=last, tp=(0, 32 * r))
                    ACT(tf[0], yp, AF.Square, ["ps%d" % bk], ["tf0"])
                    TS("dve", tf[0], tf[0], 0.044715, 1.0, ALU.mult, ALU.add, ["tf0"], ["tf0"])
                    TT("dve", tf[0], tf[0], yp, ALU.mult, ["tf0", "ps%d" % bk], ["tf0"])
                    ACT(tf[1], tf[0], AF.Sigmoid, ["tf0"], ["tf1"], scale=2.0 * math.sqrt(2.0 / math.pi))
                    TT("dve", yg[:, q, :].rearrange("p (c j) -> p c j", j=8), tf[1].rearrange("p (j c) -> p c j", j=8), yp.rearrange("p (j c) -> p c j", j=8), ALU.mult, ["tf1", "ps%d" % bk], [("yg", q)])
                S.mute = kcut < 9
                S.tag = "ph9"
                wv, wr = load_slab("gl", l, 0, 8, 0, 1024)
                for mt in range(8):
                    bk = nextbank()
                    for kc in range(8):
                        MM(psb[bk], wv[:, kc, mt * 128:(mt + 1) * 128], yg[:, kc, :], kc == 0, kc == 7,
                           [wr, ("yg", kc)], ["ps%d" % bk], sig=(kc == 7))
                    tbi = mt % 3
                    ACT(tb[tbi], psb[bk], AF.Sigmoid, ["ps%d" % bk], ["tb%d" % tbi])
                    TT("dve", ubuf[:, mt, :], yg[:, mt, :], tb[tbi], ALU.mult, [("yg", mt), "tb%d" % tbi], [("u", mt)])
                S.mute = kcut < 10
                S.tag = "ph10"
                RY = [("yg", q) for q in range(8)] + ["Z", "zbf"]
                for grp in range(4):
                    gv, gr = load_slab("in", l, 0, 16, 3072 + grp * 512, 512)
                    wv, wr = load_slab("co", l, 0, 8, grp * 512, 512)
                    for mi in range(4):
                        bkg = nextbank()
                        for kc in range(16):
                            MM(psb[bkg], gv[:, kc, mi * 128:(mi + 1) * 128], hbuf[:, kc, :], kc == 0, kc == 15,
                               [gr, ("h", kc)], ["ps%d" % bkg], sig=(kc == 15))
                        ACT(tb[mi], psb[bkg], AF.Sigmoid, ["ps%d" % bkg], ["tb%d" % mi])
                    for mi in range(4):
                        mt = grp * 4 + mi
                        bk = nextbank()
                        for kc in range(8):
                            MM(psb[bk], wv[:, kc, mi * 128:(mi + 1) * 128], vconv[:, kc, :], kc == 0, kc == 7,
                               [wr, ("v", kc)], ["ps%d" % bk], sig=(kc == 7))
                        extra = RY + RZ if mt == 0 else []
                        TT("dve", mbuf[:, mt, :], psb[bk], tb[mi], ALU.mult, ["ps%d" % bk, "tb%d" % mi] + extra, [("m", mt)] + extra)
                S.mute = kcut < 11
                S.tag = "ph11"
                for grp in range(4):
                    gv, gr = load_slab("in", l, 0, 16, 5120 + grp * 512, 512)
                    wv, wr = load_slab("so", l, 0, 8, grp * 512, 512)
                    for mi in range(4):
                        bkg = nextbank()
                        for kc in range(16):
                            MM(psb[bkg], gv[:, kc, mi * 128:(mi + 1) * 128], hbuf[:, kc, :], kc == 0, kc == 15,
                               [gr, ("h", kc)], ["ps%d" % bkg], sig=(kc == 15))
                        ACT(tb[mi], psb[bkg], AF.Sigmoid, ["ps%d" % bkg], ["tb%d" % mi])
                    for mi in range(4):
                        mt = grp * 4 + mi
                        bk = nextbank()
                        for kc in range(8):
                            MM(psb[bk], wv[:, kc, mi * 128:(mi + 1) * 128], ubuf[:, kc, :], kc == 0, kc == 7,
                               [wr, ("u", kc)], ["ps%d" % bk], sig=(kc == 7))
                        TT("dve", tf[mt % 2], psb[bk], tb[mi], ALU.mult, ["ps%d" % bk, "tb%d" % mi], ["tf%d" % (mt % 2)])
                        TT("dve", mbuf[:, mt, :], mbuf[:, mt, :], tf[mt % 2], ALU.add, [("m", mt), "tf%d" % (mt % 2)], [("m", mt)])
                S.mute = kcut < 12
                S.tag = "ph12"
                for s in range(4):
                    wv, wr = load_slab("o", l, 0, 16, s * 512, 512)
                    for mi in range(4):
                        mt = s * 4 + mi
                        bk = nextbank()
                        for kc in range(16):
                            MM(psb[bk], wv[:, kc, mi * 128:(mi + 1) * 128], mbuf[:, kc, :], kc == 0, kc == 15,
                               [wr, ("m", kc)], ["ps%d" % bk], sig=(kc == 15))
                        TT("dve", xres[:, mt, :], xres[:, mt, :], psb[bk], ALU.add, [("x", mt), "ps%d" % bk], [("x", mt)])
                S.mute = kcut < 13
                S.tag = "ph13"
                rmsnorm_to_h(g0 + 16)
                for c in range(16):
                    STT(hbuf[:, c, :], xres[:, c, :], GS[:, g0 + 16 + c:g0 + 17 + c], tf[0], ALU.mult, ALU.mult,
                        [("x", c), "gs", "tf0"], [("h", c)])
                RM = [("m", c) for c in range(16)]
                for half in range(2):
                    for s in range(8):
                        wv, wr = load_slab("f1", l, 0, 16, half * 4096 + s * 512, 512)
                        bks = [nextbank() for _ in range(4)]
                        kco = (half == 0 and s == 0)
                        if kco:
                            for kc in range(16):
                                for mi in range(4):
                                    MM(psb[bks[mi]], wv[:, kc, mi * 128:(mi + 1) * 128], hbuf[:, kc, :], kc == 0, kc == 15,
                                       [wr, ("h", kc)], ["ps%d" % bks[mi]], sig=(kc == 15))
                        for mi in range(4):
                            bk = bks[mi]
                            for kc in range(0 if kco else 16):
                                MM(psb[bk], wv[:, kc, mi * 128:(mi + 1) * 128], hbuf[:, kc, :], kc == 0, kc == 15,
                                   [wr, ("h", kc)], ["ps%d" % bk], sig=(kc == 15))
                            ai = s * 4 + mi
                            tbi = ai % 3
                            ACT(tb[tbi], psb[bk], AF.Relu, ["ps%d" % bk], ["tb%d" % tbi])
                            extra = RM if (ai == 0) else []
                            TT("dve", abuf[:, ai, :], tb[tbi], tb[tbi], ALU.mult, ["tb%d" % tbi] + extra, [("a", ai)] + extra)
                    for cg in range(4):
                        bset = 4 if cg % 2 == 0 else 0
                        for kq in range(2):
                            wv, wr = load_slab("f2", l, half * 4096 + kq * 2048, 16, cg * 512, 512)
                            for mi in range(4):
                                bk = bset + mi
                                for kc in range(16):
                                    st = (kq == 0 and kc == 0)
                                    sp_ = (kq == 1 and kc == 15)
                                    MM(psb[bk], wv[:, kc, mi * 128:(mi + 1) * 128], abuf[:, kq * 16 + kc, :], st, sp_,
                                       [wr, ("a", kq * 16 + kc)], ["ps%d" % bk], sig=(kc == 15))
                        for mi in range(4):
                            mt = cg * 4 + mi
                            bk = bset + mi
                            TT("dve", xres[:, mt, :], xres[:, mt, :], psb[bk], ALU.add, [("x", mt), "ps%d" % bk], [("x", mt)])
                RA = [("a", i) for i in range(32)]
                S.op("dve", lambda e: e.memset(sc1, 0.0), RA + ["sc1"], RA + RZ + ["sc1"])
            S.mute = False
            S.tag = "final"
            rmsnorm_to_h(L * 32)
            for c in range(16):
                STT(ostage[:, c, :], xres[:, c, :], GS[:, L * 32 + c:L * 32 + c + 1], tf[0], ALU.mult, ALU.mult,
                    [("x", c), "gs", "tf0"] + (RZ if c == 0 else []), [("os", c)] + (RZ if c == 0 else []))
            S.dma("sp", lambda e, t0=t0: e.dma_start(out=outT[:, t0:t0 + NT].rearrange("(c p) n -> p c n", p=128), in_=ostage),
                  [("os", c) for c in range(16)], ["outdma"] + RZ)
            last_out_tok.append(S.lastw["outdma"])
        S.final_wait("sp", last_out_tok)

        sems = {}
        for sk in sorted(S.semkeys, key=str):
            sems[sk] = es.enter_context(nc.semaphore("s_%s_%d" % (sk[0], sk[1])))
        with nc.Block() as block:
            @block.tensor
            def _(e):
                S.replay("pe", e, sems)

            @block.scalar
            def _(e):
                S.replay("act", e, sems)

            @block.vector
            def _(e):
                S.replay("dve", e, sems)

            @block.gpsimd
            def _(e):
                S.replay("pool", e, sems)

            @block.sync
            def _(e):
                S.replay("sp", e, sems)
    nc._sched = S
    return nc


def host_pack(inp):
    f = np.float32
    L = DEPTH
    pl = np.zeros((L, 128, NPL), f)
    pbk = np.zeros((L, 128, 4, 32, 16), f)
    wdl = np.zeros((L, 128, 8 * KW), f)

    def col(v, n):
        return np.ascontiguousarray(v.reshape(n, 128).T)

    for l in range(L):
        pl[l, :, O_GMIX:O_GMIX + 16] = col(inp["norm_mix"][l], 16)
        pl[l, :, O_GMLP:O_GMLP + 16] = col(inp["norm_mlp"][l], 16)
        pl[l, :, O_BDW:O_BDW + 8] = col(inp["b_dw"][l], 8)
        pl[l, :, O_LNG:O_LNG + 8] = col(inp["ln_g"][l], 8)
        pl[l, :, O_LNB:O_LNB + 8] = col(inp["ln_b"][l], 8)
        wdl[l] = inp["w_dw"][l].T.reshape(8, 128, KW).transpose(1, 0, 2).reshape(128, 248)
        pl[l, :, O_DSK:O_DSK + 8] = col(inp["d_skip"][l], 8)

        def gp(a):
            return a.reshape(32, 2, 64).transpose(1, 2, 0).reshape(128, 32)
        pl[l, :, O_ARE:O_ARE + 32] = gp(inp["a_re"][l])
        pl[l, :, O_AIM:O_AIM + 32] = gp(inp["a_im"][l])
        pl[l, :, O_LDT:O_LDT + 32] = gp(np.repeat(inp["log_dt"][l][:, None], 64, axis=1))
        for i, nm in enumerate(("b_re", "b_im")):
            pbk[l, :, i] = inp[nm][l].reshape(32, 2, 64, 16).transpose(1, 2, 0, 3).reshape(128, 32, 16)
        for i, nm in enumerate(("c_re", "c_im")):
            pbk[l, :, 2 + i] = inp[nm][l].reshape(32, 2, 16, 64).transpose(1, 3, 0, 2).reshape(128, 32, 16)
    cst = np.zeros((128, 784), f)
    cst[:, 0:128] = np.eye(128, dtype=f)
    cst[:, 128:256] = 1.0
    blk = np.kron(np.eye(8, dtype=f), np.ones((16, 16), f))
    cst[:, 256:768] = np.tile(blk, (1, 4))
    cst[:, 768:784] = col(inp["norm_final"], 16)
    return pl, pbk.reshape(L, 128, 2048), cst, wdl


def make_shared(inp):
    pl, pbk, cst, wdl = host_pack(inp)
    return {"w_in": inp["w_in"], "w_conv_out": inp["w_conv_out"], "w_glu": inp["w_glu"], "w_ssm_out": inp["w_ssm_out"],
            "w_out": inp["w_out"], "w_ff1": inp["w_ff1"], "w_ff2": inp["w_ff2"], "pl": pl, "pb": pbk, "cst": cst, "wd": wdl}


_NC_CACHE = {}


def kernel(**inputs):
    inp = {k: np.asarray(v) for k, v in inputs.items()}
    x = inp["x"]
    B = x.shape[0]
    shared = make_shared(inp)
    if "nc" not in _NC_CACHE:
        _NC_CACHE["nc"] = build_nc()
    nc = _NC_CACHE["nc"]
    in_maps = []
    for b in range(B):
        m = dict(shared)
        m["xT"] = np.ascontiguousarray(x[b].T)
        in_maps.append(m)
    res = run_bass_kernel_spmd(nc, in_maps, core_ids=list(range(B)))
    out = np.stack([np.ascontiguousarray(r["outT"].T) for r in res.results], axis=0)
    return out.astype(np.float32)
```

```python
import math
import numpy as np
import concourse.bass as bass
import concourse.mybir as mybir
from concourse.bass_utils import run_bass_kernel_spmd

F32 = mybir.dt.float32
BF16 = mybir.dt.bfloat16
AF = mybir.ActivationFunctionType
ALU = mybir.AluOpType

D = 2048
SEQ = 4096
DEPTH = 4
CWID = 1024
KW = 31
HIST = KW - 1
NT = 512
DFF = 8192
INC = 7168
EPS_RMS = 1e-6
EPS_LN = 1e-5
NPL = 160
EPOCH = 2000
NSLOT = 8

O_GMIX, O_GMLP, O_BDW, O_LNG, O_LNB, O_DSK, O_ARE, O_AIM, O_LDT = 0, 16, 32, 40, 48, 56, 64, 96, 128


class Sched:
    def __init__(self):
        self.streams = {e: [] for e in ("pe", "act", "dve", "pool", "sp")}
        self.nsig = {e: 0 for e in ("pe", "act", "dve", "pool")}
        self.lastw = {}
        self.readers = {}
        self.known = {e: {} for e in self.streams}
        self.dma_i = {"sp": 0, "pool": 0}
        self.semkeys = set()
        self.mute = False
        self.tag = "init"
        self.pe_tags = []

    def _need(self, reads, writes):
        need = {}

        def add(t):
            if t is None:
                return
            k, v = t
            if need.get(k, 0) < v:
                need[k] = v
        for r in reads:
            add(self.lastw.get(r))
        for w in writes:
            add(self.lastw.get(w))
            for t in self.readers.get(w, {}).items():
                add(t)
        return need

    def _waits(self, eng, need):
        waits = []
        kn = self.known[eng]
        for sk, v in need.items():
            if eng == "pe" and sk[0] == "pe":
                continue
            if sk[0] in self.nsig:
                if any(k2[0] == sk[0] and k2[1] > sk[1] for k2 in kn):
                    continue
            if kn.get(sk, 0) >= v:
                continue
            kn[sk] = v
            waits.append((sk, v))
        return waits

    def _commit(self, tok, reads, writes):
        for r in reads:
            d = self.readers.setdefault(r, {})
            if d.get(tok[0], 0) < tok[1]:
                d[tok[0]] = tok[1]
        for w in writes:
            self.lastw[w] = tok
            self.readers[w] = {}

    def op(self, eng, fn, reads=(), writes=(), sig=True):
        if self.mute:
            return
        if eng == "pe":
            self.pe_tags.append(self.tag)
        need = self._need(reads, writes)
        waits = self._waits(eng, need)
        n = self.nsig[eng]
        sk = (eng, n // EPOCH)
        tok = (sk, n % EPOCH + 1)
        self.semkeys.add(sk)
        self.streams[eng].append((waits, fn, (sk, 1) if sig else None))
        if sig:
            self.nsig[eng] = n + 1
        self._commit(tok, reads, writes)

    def dma(self, q, fn, reads=(), writes=(), extra=()):
        if self.mute:
            return
        i = self.dma_i[q]
        self.dma_i[q] = i + 1
        slot, use = i % NSLOT, i // NSLOT
        sk = ("d" + q, slot)
        self.semkeys.add(sk)
        need = self._need(reads, writes)
        for k_, v_ in extra:
            if need.get(k_, 0) < v_:
                need[k_] = v_
        if use > 0:
            if need.get(sk, 0) < 16 * use:
                need[sk] = 16 * use
        waits = self._waits(q, need)
        self.streams[q].append((waits, fn, (sk, 16)))
        self._commit((sk, 16 * (use + 1)), reads, writes)

    def final_wait(self, eng, toks):
        need = {}
        for k, v in toks:
            if need.get(k, 0) < v:
                need[k] = v
        waits = self._waits(eng, need)
        self.streams[eng].append((waits, None, None))

    def replay(self, eng, handle, sems):
        for waits, fn, inc in self.streams[eng]:
            for sk, v in waits:
                handle.wait_ge(sems[sk], v)
            if fn is None:
                continue
            ins = fn(handle)
            if inc is not None:
                ins.then_inc(sems[inc[0]], inc[1])


def build_nc(n_layers=DEPTH, n_tiles=SEQ // NT, dbg=False, kcut=99, kprep=True):
    nc = bass.Bass("TRN2", target_bir_lowering=False)
    S = Sched()
    L = n_layers
    T = n_tiles * NT

    def din(name, shape, dt=F32):
        return nc.dram_tensor(name, shape, dt, kind="ExternalInput").ap()

    def dscr(name, shape, dt=BF16):
        return nc.dram_tensor(name, shape, dt, kind="Internal").ap()

    xT = din("xT", [D, T])
    w_in = din("w_in", [DEPTH, D, INC])
    w_co = din("w_conv_out", [DEPTH, CWID, D])
    w_gl = din("w_glu", [DEPTH, CWID, CWID])
    w_so = din("w_ssm_out", [DEPTH, CWID, D])
    w_o = din("w_out", [DEPTH, D, D])
    w_f1 = din("w_ff1", [DEPTH, D, DFF])
    w_f2 = din("w_ff2", [DEPTH, DFF, D])
    pl = din("pl", [DEPTH, 128, NPL])
    pb = din("pb", [DEPTH, 128, 4 * 512])
    wd = din("wd", [DEPTH, 128, 8 * KW])
    cst = din("cst", [128, 128 + 128 + 512 + 16])
    outT = nc.dram_tensor("outT", [D, T], F32, kind="ExternalOutput").ap()
    dbg_out = None
    if dbg:
        dbg_out = nc.dram_tensor("dbg", [128, 16 * NT], F32, kind="ExternalOutput").ap()

    wsrc = {"in": (w_in, D, INC), "co": (w_co, CWID, D), "gl": (w_gl, CWID, CWID), "so": (w_so, CWID, D),
            "o": (w_o, D, D), "f1": (w_f1, D, DFF), "f2": (w_f2, DFF, D)}
    wb = {k: dscr("wb_" + k, [L, v[1], v[2]]) for k, v in wsrc.items()}
    zinm = dscr("zinm", [L, 128, 8, 2048])
    lcm = dscr("lcm", [L, 128, 8, 3072])
    dgm = dscr("dgm", [L, 128, 4, 2 * KW * 128])
    pwtab = nc.dram_tensor("pwtab", [L, 128, 1024], F32, kind="Internal").ap()

    import contextlib
    es = contextlib.ExitStack()
    with es:
        ARENA_W = 52600
        arena = es.enter_context(nc.sbuf_tensor("arena", [128, ARENA_W], F32))
        ptr = [0]

        def alloc(words):
            a = ptr[0]
            ptr[0] += words
            assert ptr[0] <= ARENA_W, ptr[0]
            return a

        def vf(a, words):
            return arena[:, a:a + words]

        def vb(a, words):
            return arena[:, a:a + words].bitcast(BF16)

        a_pl = alloc(L * NPL)
        PL = vf(a_pl, L * NPL).rearrange("p (l n) -> p l n", l=L)
        a_c = alloc(128 + 128 + 512 + 16)
        CST = vf(a_c, 784)
        ident = CST[:, 0:128]
        ones_f = CST[:, 128:256]
        bmask = CST[:, 256:768]
        gfin = CST[:, 768:784]
        a_ob = alloc(64)
        ones_b = vb(a_ob, 64)
        a_gs = alloc(L * 32 + 16)
        GS = vf(a_gs, L * 32 + 16)
        a_mu = alloc(L * 128)
        MU = vf(a_mu, L * 128).rearrange("p (l t n) -> p l t n", l=L, t=2)
        a_st = alloc(L * 64)
        ST = vf(a_st, L * 64).rearrange("p (l n) -> p l n", l=L)
        a_hi = alloc(L * 8 * HIST)
        HI = vf(a_hi, L * 8 * HIST).rearrange("p (l c n) -> p l c n", l=L, c=8)
        a_np = alloc(3)
        negpi = vf(a_np, 1)
        epsr = vf(a_np + 1, 1)
        epsl = vf(a_np + 2, 1)
        a_main = ptr[0]
        a_x = alloc(16 * NT)
        xres = vf(a_x, 16 * NT).rearrange("p (c n) -> p c n", c=16)
        a_wb = [alloc(4096) for _ in range(3)]
        wbufs = [vb(a, 4096) for a in a_wb]
        a_h = alloc(4096)
        hbuf = vb(a_h, 4096).rearrange("p (c n) -> p c n", c=16)
        a_R = alloc(8432)
        uc = vb(a_R, 4 * (NT + HIST)).rearrange("p (c n) -> p c n", c=8)
        convo = vf(a_R + 4 * (NT + HIST), 4096).rearrange("p (c n) -> p c n", c=8)
        Z = vf(a_R, 4160).rearrange("p (t g c) -> p t g c", t=2, g=32)
        zbf = vb(a_R + 4160, 2048).rearrange("p (t g c) -> p t g c", t=2, g=32)
        yg = vb(a_R + 4160 + 2048, 2048).rearrange("p (c n) -> p c n", c=8)
        mbuf = vb(a_R, 4096).rearrange("p (c n) -> p c n", c=16)
        abuf = vb(a_R, 8192).rearrange("p (c n) -> p c n", c=32)
        ostage = vf(a_R, 8192).rearrange("p (c n) -> p c n", c=16)
        a_v = alloc(2048)
        vconv = vb(a_v, 2048).rearrange("p (c n) -> p c n", c=8)
        a_u = alloc(2048)
        ubuf = vb(a_u, 2048).rearrange("p (c n) -> p c n", c=8)
        a_zi = [alloc(1024) for _ in range(2)]
        zibuf = [vb(a, 1024).rearrange("p (j t n) -> p j t n", j=8, t=2) for a in a_zi]
        a_ubm = [alloc(1024) for _ in range(2)]
        ubm = [vb(a, 1024).rearrange("p (j r c) -> p j r c", j=8, r=4) for a in a_ubm]
        a_lc = [alloc(1536) for _ in range(2)]
        lcb = [vb(a, 1536) for a in a_lc]
        a_sq = [alloc(256) for _ in range(2)]
        sqb = [vb(a, 256) for a in a_sq]
        a_t = [alloc(512) for _ in range(4)]
        tf = [vf(a, 512) for a in a_t]
        sqf = [tf[2], tf[3]]
        a_tb = [alloc(256) for _ in range(4)]
        tb = [vb(a, 256) for a in a_tb]
        a_pw = alloc(1024)
        PWb = vf(a_pw, 1024).rearrange("p (j t n) -> p j t n", j=8, t=2)
        a_s1 = alloc(64)
        a_s2 = alloc(64)
        sc1 = vf(a_s1, 64).rearrange("p (t g) -> p t g", t=2)
        sc2 = vf(a_s2, 64).rearrange("p (t g) -> p t g", t=2)
        a_end = ptr[0]
        ptr[0] = a_main
        a_pb = alloc(2048)
        PB = vf(a_pb, 2048).rearrange("p (t g h) -> p t g h", t=4, g=32)
        sm = [vf(alloc(32), 32) for _ in range(24)]
        smi = vf(alloc(32), 32).bitcast(mybir.dt.int32)
        Pc = [vf(alloc(512), 512).rearrange("p (g h) -> p g h", g=32) for _ in range(4)]
        Wc = [vf(alloc(512), 512).rearrange("p (g h) -> p g h", g=32) for _ in range(4)]
        tmpc = [vf(alloc(512), 512).rearrange("p (g h) -> p g h", g=32) for _ in range(2)]
        Pexp = [vf(alloc(1024), 1024) for _ in range(2)]
        Cexp = [vf(alloc(1024), 1024) for _ in range(2)]
        Wexp = [vf(alloc(1024), 1024) for _ in range(2)]
        zin_sb = vb(alloc(8192), 8192).rearrange("p (q j t n) -> p q j t n", q=8, j=8, t=2)
        lc_sb = vb(alloc(12288), 12288).rearrange("p (q n) -> p q n", q=8)
        dg_sb = vb(alloc(KW * 128), KW * 128).rearrange("p (c k n) -> p c k n", c=2, k=KW)
        WD = vf(alloc(8 * KW), 8 * KW)
        PWs = vf(alloc(1024), 1024).rearrange("p (j t n) -> p j t n", j=8, t=2)
        pwm = [vf(alloc(32), 32) for _ in range(4)]
        assert ptr[0] <= ARENA_W, ptr[0]

        psb = [es.enter_context(nc.psum_tensor("ps%d" % i, [128, 512], F32))[:] for i in range(8)]
        pool_i = [0]

        def nextbank(lo=0, hi=4):
            i = lo + pool_i[0] % (hi - lo)
            pool_i[0] += 1
            return i

        def TT(eng, out, a, b, op, R, W):
            S.op(eng, lambda e: e.tensor_tensor(out=out, in0=a, in1=b, op=op), R, W)

        def TS(eng, out, a, s1, s2, op0, op1, R, W):
            if s2 is None:
                S.op(eng, lambda e: e.tensor_scalar(out=out, in0=a, scalar1=s1, scalar2=None, op0=op0), R, W)
            else:
                S.op(eng, lambda e: e.tensor_scalar(out=out, in0=a, scalar1=s1, scalar2=s2, op0=op0, op1=op1), R, W)

        def STT(out, a, sc, b, op0, op1, R, W):
            S.op("dve", lambda e: e.scalar_tensor_tensor(out=out, in0=a, scalar=sc, in1=b, op0=op0, op1=op1), R, W)

        def ACT(out, a, func, R, W, bias=None, scale=None):
            kw = {}
            if bias is not None:
                kw["bias"] = bias
            if scale is not None:
                kw["scale"] = scale
            S.op("act", lambda e: e.activation(out=out, in_=a, func=func, **kw), R, W)

        def CP(eng, out, a, R, W):
            if eng == "act":
                S.op("act", lambda e: e.copy(out=out, in_=a), R, W)
            else:
                S.op(eng, lambda e: e.tensor_copy(out=out, in_=a), R, W)

        def MM(out, lhsT, rhs, start, stop, R, W, sig=None, tp=None):
            kw = {}
            if tp is not None:
                kw["tile_position"] = tp
            S.op("pe", lambda e: e.matmul(out, lhsT, rhs, start=start, stop=stop, **kw), R, W,
                 sig=True if sig is None else sig)

        S.dma("sp", lambda e: e.dma_start(out=CST, in_=cst[:, :]), [], ["cst"])
        S.dma("sp", lambda e: e.dma_start(out=PL, in_=pl[0:L].rearrange("l p n -> p l n")), [], ["pl"])
        S.op("dve", lambda e: e.memset(negpi, -math.pi), [], ["negpi"])
        S.op("dve", lambda e: e.memset(epsr, D * EPS_RMS), ["negpi"], ["negpi"])
        S.op("dve", lambda e: e.memset(epsl, EPS_LN), ["negpi"], ["negpi"])
        CP("dve", ones_b, ones_f, ["cst"], ["ones_b"])
        S.op("dve", lambda e: e.memset(vf(a_st, L * 64), 0.0), [], ["ST%d" % l for l in range(L)])
        S.op("dve", lambda e: e.memset(vf(a_hi, L * 8 * HIST), 0.0), [], ["HI%d" % l for l in range(L)])
        for l in range(L):
            TS("dve", GS[:, l * 32:l * 32 + 32], PL[:, l, 0:32], math.sqrt(D), None, ALU.mult, None, ["pl"], ["gs"])
        TS("dve", GS[:, L * 32:L * 32 + 16], gfin, math.sqrt(D), None, ALU.mult, None, ["cst"], ["gs"])

        def wpieces(key):
            _, K_, N_ = wsrc[key]
            cw = 1792 if N_ == INC else min(N_, 2048)
            return [(r, c, cw) for r in range(K_ // 128) for c in range(N_ // cw)]

        def cast_items(l):
            items = []
            for key in ("in", "co", "gl", "so", "o", "f1", "f2"):
                src = wsrc[key][0]
                for (r, c, cw) in wpieces(key):
                    items.append((lambda e, key=key, src=src, l=l, r=r, c=c, cw=cw: e.dma_start(
                        out=wb[key][l, r * 128:(r + 1) * 128, c * cw:(c + 1) * cw],
                        in_=src[l, r * 128:(r + 1) * 128, c * cw:(c + 1) * cw]), ("wb", key, l, r, c)))
            return items

        pending_cast = []
        for fn_, res_ in cast_items(0):
            S.dma("pool", fn_, [], [res_])

        def pace_cast(n, extra=()):
            for _ in range(min(n, len(pending_cast))):
                fn_, res_ = pending_cast.pop(0)
                S.dma("pool", fn_, [], [res_], extra=extra)

        cnt = [0]

        def cmul(o_re, o_im, a_re, a_im, b_re, b_im, t0, t1, R, W, eng="dve"):
            big = (t0 is tmpc[0])
            r0, r1 = ("tmpc0", "tmpc1") if big else ("s_q0", "s_q1")
            TT(eng, t0, a_re, b_re, ALU.mult, R, [r0])
            TT(eng, t1, a_im, b_im, ALU.mult, R, [r1])
            TT(eng, o_re, t0, t1, ALU.subtract, [r0, r1], W[0:1])
            TT(eng, t0, a_re, b_im, ALU.mult, R, [r0])
            TT(eng, t1, a_im, b_re, ALU.mult, R, [r1])
            TT(eng, o_im, t0, t1, ALU.add, [r0, r1], W[1:2])

        def bc(t):
            return t.unsqueeze(2).to_broadcast([128, 32, 16])

        S.tag = "prep"
        for l in range(L if kprep else 0):
            are, aim, ldt = PL[:, l, O_ARE:O_ARE + 32], PL[:, l, O_AIM:O_AIM + 32], PL[:, l, O_LDT:O_LDT + 32]
            S.dma("sp", lambda e, l=l: e.dma_start(out=vf(a_pb, 2048), in_=pb[l]), [], ["PB"])
            dt_, xr, xi, mag, sa, ca, sn, cs, lr, li, den, rden, nr, kr, ki, q0, q1, q2, q3 = sm[0:19]
            ACT(dt_, ldt, AF.Exp, ["pl"], ["s_dt"])
            TT("dve", xr, are, dt_, ALU.mult, ["pl", "s_dt"], ["s_xr"])
            TT("dve", xi, aim, dt_, ALU.mult, ["pl", "s_dt"], ["s_xi"])
            ACT(mag, xr, AF.Exp, ["s_xr"], ["s_mag"])
            for (dst, off, nm) in ((sn, 0.0, "s_sn"), (cs, 0.25, "s_cs")):
                TS("dve", sa, xi, 1.0 / (2.0 * math.pi), off, ALU.mult, ALU.add, ["s_xi"], ["s_sa"])
                CP("dve", smi, sa, ["s_sa"], ["s_smi"])
                CP("dve", ca, smi, ["s_smi"], ["s_ca"])
                TT("dve", sa, sa, ca, ALU.subtract, ["s_sa", "s_ca"], ["s_sa"])
                ACT(ca, sa, AF.Sin, ["s_sa"], ["s_ca"], scale=math.pi)
                ACT(q3, sa, AF.Sin, ["s_sa"], ["s_q3"], scale=math.pi / 2.0)
                TT("dve", q3, q3, q3, ALU.mult, ["s_q3"], ["s_q3"])
                TS("dve", q3, q3, -4.0, 2.0, ALU.mult, ALU.add, ["s_q3"], ["s_q3"])
                TT("dve", dst, ca, q3, ALU.mult, ["s_ca", "s_q3"], [nm])
            TT("dve", lr, mag, cs, ALU.mult, ["s_mag", "s_cs"], ["s_lr"])
            TT("dve", li, mag, sn, ALU.mult, ["s_mag", "s_sn"], ["s_li"])
            TT("dve", q0, are, are, ALU.mult, ["pl"], ["s_q0"])
            TT("dve", q1, aim, aim, ALU.mult, ["pl"], ["s_q1"])
            TT("dve", den, q0, q1, ALU.add, ["s_q0", "s_q1"], ["s_den"])
            S.op("dve", lambda e: e.reciprocal(out=rden, in_=den), ["s_den"], ["s_rden"])
            TS("dve", nr, lr, -1.0, None, ALU.add, None, ["s_lr"], ["s_nr"])
            TT("dve", q0, nr, are, ALU.mult, ["s_nr", "pl"], ["s_q0"])
            TT("dve", q1, li, aim, ALU.mult, ["s_li", "pl"], ["s_q1"])
            TT("dve", q2, q0, q1, ALU.add, ["s_q0", "s_q1"], ["s_q2"])
            TT("dve", kr, q2, rden, ALU.mult, ["s_q2", "s_rden"], ["s_kr"])
            TT("dve", q0, li, are, ALU.mult, ["s_li", "pl"], ["s_q0"])
            TT("dve", q1, nr, aim, ALU.mult, ["s_nr", "pl"], ["s_q1"])
            TT("dve", q2, q0, q1, ALU.subtract, ["s_q0", "s_q1"], ["s_q2"])
            TT("dve", ki, q2, rden, ALU.mult, ["s_q2", "s_rden"], ["s_ki"])
            m_re, m_im, n_re, n_im = sm[19:23]
            cmul(m_re, m_im, lr, li, lr, li, q0, q1, ["s_lr", "s_li"], ["s_mre", "s_mim"])
            cmul(n_re, n_im, m_re, m_im, m_re, m_im, q0, q1, ["s_mre", "s_mim"], ["s_nre", "s_nim"])
            cmul(m_re, m_im, n_re, n_im, n_re, n_im, q0, q1, ["s_nre", "s_nim"], ["s_mre", "s_mim"])
            mr = "MU%d" % l
            CP("dve", MU[:, l, 0, 0:32], m_re, ["s_mre"], [mr + "a"])
            CP("dve", MU[:, l, 0, 32:64], m_re, ["s_mre"], [mr + "b"])
            TS("dve", MU[:, l, 1, 0:32], m_im, -1.0, None, ALU.mult, None, ["s_mim"], [mr + "c"])
            CP("dve", MU[:, l, 1, 32:64], m_im, ["s_mim"], [mr + "d"])
            pcur = (m_re, m_im, "s_mre", "s_mim")
            for j_ in range(8):
                CP("dve", PWs[:, j_, 0, 0:32], pcur[0], [pcur[2]], ["PWs"])
                CP("dve", PWs[:, j_, 0, 32:64], pcur[0], [pcur[2]], ["PWs"])
                TS("dve", PWs[:, j_, 1, 0:32], pcur[1], -1.0, None, ALU.mult, None, [pcur[3]], ["PWs"])
                CP("dve", PWs[:, j_, 1, 32:64], pcur[1], [pcur[3]], ["PWs"])
                if j_ < 7:
                    nx_ = (pwm[0], pwm[1], "pwm0", "pwm1") if j_ % 2 == 0 else (pwm[2], pwm[3], "pwm2", "pwm3")
                    cmul(nx_[0], nx_[1], pcur[0], pcur[1], m_re, m_im, q0, q1, [pcur[2], pcur[3], "s_mre", "s_mim"], [nx_[2], nx_[3]])
                    pcur = nx_
            S.dma("sp", lambda e, l=l: e.dma_start(out=pwtab[l], in_=PWs.rearrange("p j t n -> p (j t n)")), ["PWs"], ["pwtab%d" % l])
            Bre, Bim, Cre, Cim = PB[:, 0], PB[:, 1], PB[:, 2], PB[:, 3]
            cmul(Pc[0], Pc[1], bc(kr), bc(ki), Bre, Bim, tmpc[0], tmpc[1], ["s_kr", "s_ki", "PB"], ["P0r", "P0i"])
            for t_ in range(2):
                S.op("dve", lambda e, t_=t_: e.memset(Cexp[t_], 0.0), [], ["Cexp%d" % t_])
                S.op("dve", lambda e, t_=t_: e.memset(Pexp[t_], 0.0), [], ["Pexp%d" % t_])
                S.op("dve", lambda e, t_=t_: e.memset(Wexp[t_], 0.0), [], ["Wexp%d" % t_])

            def expand(dst, src, R, W, scale=None, eng="dve"):
                dv = dst.rearrange("p (g e h) -> p g e h", g=32, e=2)
                for e_ in range(2):
                    o = dv[64 * e_:64 * e_ + 64, :, e_, :]
                    i = src[64 * e_:64 * e_ + 64, :, :]
                    if scale is None:
                        CP(eng, o, i, R, W)
                    else:
                        TS(eng, o, i, scale, None, ALU.mult, None, R, W)

            expand(Cexp[0], Cre, ["PB"], ["Cexp0"])
            expand(Cexp[1], Cim, ["PB"], ["Cexp1"], scale=-1.0)
            cur = 0
            for k in range(8):
                Pr, Pi = Pc[2 * cur], Pc[2 * cur + 1]
                rn = ["P%dr" % cur, "P%di" % cur]
                expand(Pexp[0], Pr, [rn[0]], ["Pexp0"])
                expand(Pexp[1], Pi, [rn[1]], ["Pexp1"])
                for qh in range(2):
                    bk = nextbank()
                    for qq in range(4):
                        q = qh * 4 + qq
                        o = psb[bk][:, qq * 128:(qq + 1) * 128]
                        MM(o, Pexp[0][:, q * 128:(q + 1) * 128], Cexp[0][:, q * 128:(q + 1) * 128], True, False,
                           ["Pexp0", "Cexp0"], ["ps%d" % bk], sig=False)
                        MM(o, Pexp[1][:, q * 128:(q + 1) * 128], Cexp[1][:, q * 128:(q + 1) * 128], False, True,
                           ["Pexp1", "Cexp1"], ["ps%d" % bk], sig=True)
                    o4 = lc_sb[:, qh * 4:qh * 4 + 4, k * 128:(k + 1) * 128]
                    TT("dve", o4, psb[bk].rearrange("p (q n) -> p q n", q=4),
                       bmask.rearrange("p (q n) -> p q n", q=4), ALU.mult, ["ps%d" % bk, "cst"], ["lc_sb"])
                    for t_ in range(2):
                        bk2 = nextbank()
                        for qq in range(4):
                            q = qh * 4 + qq
                            S.op("pe", lambda e, bk2=bk2, qq=qq, q=q, t_=t_: e.transpose(
                                psb[bk2][:, qq * 128:(qq + 1) * 128], Pexp[t_][:, q * 128:(q + 1) * 128], ident),
                                ["Pexp%d" % t_, "cst"], ["ps%d" % bk2], sig=(qq == 3))
                        CP("act", zin_sb[:, qh * 4:qh * 4 + 4, 7 - k, t_, :],
                           psb[bk2].rearrange("p (q n) -> p q n", q=4), ["ps%d" % bk2], ["zin_sb"])
                if k < 7:
                    nx = 1 - cur
                    cmul(Pc[2 * nx], Pc[2 * nx + 1], bc(lr), bc(li), Pr, Pi, tmpc[0], tmpc[1],
                         ["s_lr", "s_li"] + rn, ["P%dr" % nx, "P%di" % nx])
                    cur = nx
            for q in range(8):
                STT(lc_sb[:, q, 0:128], ident, PL[:, l, O_DSK + q:O_DSK + q + 1], lc_sb[:, q, 0:128], ALU.mult, ALU.add,
                    ["cst", "pl", "lc_sb"], ["lc_sb"])
            wcur = 0
            srcr, srci = Cre, Cim
            srcn = ["PB", "PB"]
            for k in range(1, 9):
                nr_, ni_ = Wc[2 * wcur], Wc[2 * wcur + 1]
                wn = ["W%dr" % wcur, "W%di" % wcur]
                cmul(nr_, ni_, bc(lr), bc(li), srcr, srci, tmpc[0], tmpc[1], ["s_lr", "s_li"] + srcn, wn)
                expand(Wexp[0], nr_, [wn[0]], ["Wexp0"])
                expand(Wexp[1], ni_, [wn[1]], ["Wexp1"], scale=-1.0)
                for t_ in range(2):
                    o = lc_sb[:, :, 1024 + (k - 1) * 256 + t_ * 128:1024 + (k - 1) * 256 + t_ * 128 + 128]
                    CP("act", o, Wexp[t_].rearrange("p (q n) -> p q n", q=8), ["Wexp%d" % t_], ["lc_sb"])
                srcr, srci, srcn = nr_, ni_, wn
                wcur = 1 - wcur
            S.dma("sp", lambda e, l=l: e.dma_start(out=WD, in_=wd[l]), [], ["WD"])
            wdw_l = WD.rearrange("p (c k) -> p c k", c=8)
            for s4 in range(4):
                for c2 in range(2):
                    TT("dve", dg_sb[:, c2, :, :], ident.unsqueeze(1).to_broadcast([128, KW, 128]),
                       wdw_l[:, 2 * s4 + c2, :].unsqueeze(2).to_broadcast([128, KW, 128]), ALU.mult,
                       ["cst", "WD"], ["dg_sb"])
                S.dma("sp", lambda e, l=l, s4=s4: e.dma_start(out=dgm[l, :, s4, :], in_=dg_sb.rearrange("p c k n -> p (c k n)")),
                      ["dg_sb"], ["dgm%d" % l])
            S.dma("sp", lambda e, l=l: e.dma_start(out=zinm[l], in_=zin_sb.rearrange("p q j t n -> p q (j t n)")),
                  ["zin_sb"], ["zinm%d" % l])
            S.dma("sp", lambda e, l=l: e.dma_start(out=lcm[l], in_=lc_sb), ["lc_sb"], ["lcm%d" % l])

        alltok = []
        for e_ in ("pe", "act", "dve", "pool"):
            n = S.nsig[e_]
            if n > 0:
                alltok.append(((e_, (n - 1) // EPOCH), (n - 1) % EPOCH + 1))
        for q_ in ("sp", "pool"):
            i = S.dma_i[q_]
            for s_ in range(min(i, NSLOT)):
                uses = (i - s_ + NSLOT - 1) // NSLOT
                alltok.append((("d" + q_, s_), 16 * uses))
        for e_ in ("pe", "act", "dve", "sp"):
            S.final_wait(e_, alltok)
        for i_ in range(2):
            S.op("dve", lambda e, i_=i_: e.memset(ubm[i_], 0.0), [], ["ubm0"] + ["ubm_%d_%d" % (i_, r) for r in range(4)])

        def wres(key, l, r0, r1, c0, c1):
            _, K_, N_ = wsrc[key]
            cw = 1792 if N_ == INC else min(N_, 2048)
            return [("wb", key, l, r, c) for r in range(r0 // 128, (r1 + 127) // 128)
                    for c in range(c0 // cw, (c1 - 1) // cw + 1)]

        slab_i = [0]

        def load_slab(key, l, r0, nk, c0, ncol):
            i = slab_i[0] % 3
            slab_i[0] += 1
            if pending_cast:
                pace_cast(5, extra=list(S.readers.get("wbuf%d" % i, {}).items()))
            v = wbufs[i][:, 0:nk * ncol].rearrange("p (k n) -> p k n", k=nk)
            srcap = wb[key][l, r0:r0 + nk * 128, c0:c0 + ncol].rearrange("(k p) n -> p k n", p=128)
            S.dma("sp", lambda e: e.dma_start(out=v, in_=srcap), wres(key, l, r0, r0 + nk * 128, c0, c0 + ncol),
                  ["wbuf%d" % i])
            return v, "wbuf%d" % i

        def rmsnorm_to_h(goff):
            bk = nextbank()
            for c in range(16):
                s = sqb[c % 2]
                ACT(s, xres[:, c, :], AF.Square, [("x", c)], ["sqb%d" % (c % 2)])
                MM(psb[bk], ones_b, s, c == 0, c == 15, ["sqb%d" % (c % 2), "ones_b"], ["ps%d" % bk])
            ACT(tf[0], psb[bk], AF.Sqrt, ["ps%d" % bk, "negpi"], ["tf0"], bias=epsr, scale=1.0)
            S.op("dve", lambda e: e.reciprocal(out=tf[0], in_=tf[0]), ["tf0"], ["tf0"])
            return goff

        def dump(ap_src, R):
            pass

        last_out_tok = []
        for ti in range(n_tiles):
            t0 = ti * NT
            S.dma("sp", lambda e, t0=t0: e.dma_start(out=xres, in_=xT[:, t0:t0 + NT].rearrange("(c p) n -> p c n", p=128)),
                  [], [("x", c) for c in range(16)])
            for l in range(L):
                g0 = l * 32
                if ti == 0:
                    pace_cast(len(pending_cast))
                    if l + 1 < L:
                        pending_cast.extend(cast_items(l + 1))
                S.mute = kcut < 1
                S.tag = "ph1"
                rmsnorm_to_h(g0)
                for c in range(16):
                    STT(hbuf[:, c, :], xres[:, c, :], GS[:, g0 + c:g0 + c + 1], tf[0], ALU.mult, ALU.mult,
                        [("x", c), "gs", "tf0"], [("h", c)])
                HR = [("h", c) for c in range(16)]
                S.mute = kcut < 2
                S.tag = "ph2"
                for s in (2, 3, 0, 1):
                    wv, wr = load_slab("in", l, 0, 16, s * 512, 512)
                    bks = [nextbank() for _ in range(4)]
                    if s == 2:
                        for kc in range(16):
                            for mi in range(4):
                                MM(psb[bks[mi]], wv[:, kc, mi * 128:(mi + 1) * 128], hbuf[:, kc, :], kc == 0, kc == 15,
                                   [wr, ("h", kc)], ["ps%d" % bks[mi]], sig=(kc == 15))
                    for mi in range(4):
                        bk = bks[mi]
                        for kc in range(16 if s != 2 else 0):
                            MM(psb[bk], wv[:, kc, mi * 128:(mi + 1) * 128], hbuf[:, kc, :], kc == 0, kc == 15,
                               [wr, ("h", kc)], ["ps%d" % bk], sig=(kc == 15))
                        cc = (s % 2) * 4 + mi
                        if s >= 2:
                            ACT(vconv[:, cc, :], psb[bk], AF.Sigmoid, ["ps%d" % bk], [("v", cc)])
                        else:
                            TT("dve", uc[:, cc, HIST:HIST + NT], psb[bk], vconv[:, cc, :], ALU.mult,
                               ["ps%d" % bk, ("v", cc)], [("R", "uc", cc)])
                S.mute = kcut < 3
                S.tag = "ph3"
                for s in (4, 5):
                    wv, wr = load_slab("in", l, 0, 16, s * 512, 512)
                    for mi in range(4):
                        bk = nextbank()
                        for kc in range(16):
                            MM(psb[bk], wv[:, kc, mi * 128:(mi + 1) * 128], hbuf[:, kc, :], kc == 0, kc == 15,
                               [wr, ("h", kc)], ["ps%d" % bk], sig=(kc == 15))
                        q = (s - 4) * 4 + mi
                        CP("act", ubuf[:, q, :].rearrange("p (j c) -> p c j", j=8), psb[bk].rearrange("p (c j) -> p c j", j=8), ["ps%d" % bk], [("u", q)])
                S.mute = kcut < 4
                S.tag = "ph4"
                CP("dve", uc[:, :, 0:HIST], HI[:, l, :, :], ["HI%d" % l], [("R", "uc", cc) for cc in range(8)])
                for s4 in range(4):
                    i = slab_i[0] % 3
                    slab_i[0] += 1
                    dgv = wbufs[i][:, 0:2 * KW * 128].rearrange("p (c k n) -> p c k n", c=2, k=KW)
                    S.dma("sp", lambda e, l=l, s4=s4, i=i: e.dma_start(out=wbufs[i][:, 0:2 * KW * 128], in_=dgm[l, :, s4, :]),
                          ["dgm%d" % l], ["wbuf%d" % i])
                    for c2 in range(2):
                        cc = 2 * s4 + c2
                        bk = nextbank()
                        for k in range(KW):
                            MM(psb[bk], dgv[:, c2, k, :], uc[:, cc, k:k + NT], k == 0, k == KW - 1,
                               ["wbuf%d" % i, ("R", "uc", cc)], ["ps%d" % bk], sig=(k == KW - 1))
                        ACT(convo[:, cc, :], psb[bk], AF.Identity, ["ps%d" % bk, "pl"], [("R", "cv", cc)],
                            bias=PL[:, l, O_BDW + cc:O_BDW + cc + 1], scale=1.0)
                CP("dve", HI[:, l, :, :], uc[:, :, NT:NT + HIST], [("R", "uc", cc) for cc in range(8)], ["HI%d" % l])
                S.mute = kcut < 5
                S.tag = "ph5"
                bm, bs = nextbank(), nextbank()
                for cc in range(8):
                    MM(psb[bm], ones_f, convo[:, cc, :], cc == 0, cc == 7, [("R", "cv", cc), "cst"], ["ps%d" % bm])
                for cc in range(8):
                    s = sqf[cc % 2]
                    ACT(s, convo[:, cc, :], AF.Square, [("R", "cv", cc)], ["tf%d" % (2 + cc % 2)])
                    MM(psb[bs], ones_f, s, cc == 0, cc == 7, ["tf%d" % (2 + cc % 2), "cst"], ["ps%d" % bs])
                TS("dve", tf[1], psb[bm], 1.0 / CWID, None, ALU.mult, None, ["ps%d" % bm], ["tf1"])
                TT("dve", tf[2], tf[1], tf[1], ALU.mult, ["tf1"], ["tf2"])
                STT(tf[3], psb[bs], 1.0 / CWID, tf[2], ALU.mult, ALU.subtract, ["ps%d" % bs, "tf2"], ["tf3"])
                ACT(tf[2], tf[3], AF.Sqrt, ["tf3", "negpi"], ["tf2"], bias=epsl, scale=1.0)
                S.op("dve", lambda e: e.reciprocal(out=tf[2], in_=tf[2]), ["tf2"], ["tf2"])
                STT(tf[3], tf[1], -1.0, tf[2], ALU.mult, ALU.mult, ["tf1", "tf2"], ["tf3"])
                for cc in range(8):
                    TT("dve", convo[:, cc, :], convo[:, cc, :], tf[2], ALU.mult, [("R", "cv", cc), "tf2"], [("R", "cv", cc)])
                    TT("dve", convo[:, cc, :], convo[:, cc, :], tf[3], ALU.add, [("R", "cv", cc), "tf3"], [("R", "cv", cc)])
                    ACT(vconv[:, cc, :], convo[:, cc, :], AF.Silu, [("R", "cv", cc), "pl"], [("v", cc)],
                        bias=PL[:, l, O_LNB + cc:O_LNB + cc + 1], scale=PL[:, l, O_LNG + cc:O_LNG + cc + 1])
                S.mute = kcut < 6
                S.tag = "ph6"
                RZ = [("R", "uc", c) for c in range(8)] + [("R", "cv", c) for c in range(8)]
                for q in range(8):
                    zb = zibuf[q % 2]
                    S.dma("sp", lambda e, l=l, q=q, zb=zb: e.dma_start(out=zb.rearrange("p j t n -> p (j t n)"), in_=zinm[l, :, q, :]),
                          ["zinm%d" % l], ["zib%d" % (q % 2)])
                    uq = ubuf[:, q, :].rearrange("p (j c) -> p j c", j=8)
                    um = ubm[q % 2]
                    for r in range(4):
                        CP("act", um[32 * r:32 * r + 32, :, r, :], uq[32 * r:32 * r + 32, :, :], [("u", q), "ubm0"], ["ubm_%d_%d" % (q % 2, r)])
                    bk = 4 + (q % 2)
                    zv = psb[bk].rearrange("p (t r c) -> p t r c", t=2, r=4)
                    for t_ in range(2):
                        for j in range(8):
                            MM(psb[bk][:, t_ * 256:(t_ + 1) * 256], zb[:, j, t_, :], um[:, j, :, :].rearrange("p r c -> p (r c)"),
                               j == 0, j == 7, ["zib%d" % (q % 2)] + ["ubm_%d_%d" % (q % 2, r) for r in range(4)], ["ps%d" % bk],
                               sig=(j == 7))
                    for t_ in range(2):
                        first = (q == 0 and t_ == 0)
                        CP("act" if t_ == 0 else "dve", Z[:, t_, 4 * q:4 * q + 4, 1:65], zv[:, t_, :, :],
                           ["ps%d" % bk] + (RZ if first else []), ["Z"] + (RZ if first else []))
                S.mute = kcut < 7
                S.tag = "ph7"
                S.dma("sp", lambda e, l=l: e.dma_start(out=vf(a_pw, 1024), in_=pwtab[l]), ["pwtab%d" % l], ["PWb"])
                CP("dve", Z[:, :, :, 0], ST[:, l, :].rearrange("p (t g) -> p t g", t=2), ["ST%d" % l, "Z"], ["Z"])

                def cstep(dst, src, src_sw, Aj, Bj, extraR=()):
                    TT("dve", sc1, Aj, src, ALU.mult, ["Z", "PWb"], ["sc1"])
                    TT("dve", sc2, Bj, src_sw, ALU.mult, ["Z", "PWb"], ["sc2"])
                    TT("dve", dst, dst, sc1, ALU.add, ["Z", "sc1"], ["Z"])
                    TT("dve", dst, dst, sc2, ALU.add, ["Z", "sc2"], ["Z"])

                def AB(j_):
                    return (PWb[:, j_, 0, :].rearrange("p (t g) -> p t g", t=2), PWb[:, j_, 1, :].rearrange("p (t g) -> p t g", t=2))
                A1, B1 = AB(0)
                A8, B8 = AB(7)
                cstep(Z[:, :, :, 1], Z[:, :, :, 0], Z[:, ::-1, :, 0], A1, B1)
                tA = tf[2].rearrange("p (t g k) -> p t g k", t=2, g=32)
                tB = tf[3].rearrange("p (t g k) -> p t g k", t=2, g=32)
                A1b = A1.unsqueeze(3).to_broadcast([128, 2, 32, 8])
                B1b = B1.unsqueeze(3).to_broadcast([128, 2, 32, 8])
                for j_ in range(1, 8):
                    prev = Z[:, :, :, j_:j_ + 57:8]
                    prev_sw = Z[:, ::-1, :, j_:j_ + 57:8]
                    cur_ = Z[:, :, :, j_ + 1:j_ + 58:8]
                    TT("dve", tA, A1b, prev, ALU.mult, ["Z", "PWb"], ["tf2"])
                    TT("dve", tB, B1b, prev_sw, ALU.mult, ["Z", "PWb"], ["tf3"])
                    TT("dve", cur_, cur_, tA, ALU.add, ["Z", "tf2"], ["Z"])
                    TT("dve", cur_, cur_, tB, ALU.add, ["Z", "tf3"], ["Z"])
                for k_ in range(1, 8):
                    cstep(Z[:, :, :, 8 * k_ + 8], Z[:, :, :, 8 * k_], Z[:, ::-1, :, 8 * k_], A8, B8)
                big = vf(a_t[0], 2048)
                for to in range(2):
                    zt = Z[:, to, :, 9:65].rearrange("p g (k j) -> p g k j", j=8)[:, :, :, 0:7]
                    Tsame = Z[:, to, :, 8:64:8].unsqueeze(3).to_broadcast([128, 32, 7, 7])
                    Tother = Z[:, 1 - to, :, 8:64:8].unsqueeze(3).to_broadcast([128, 32, 7, 7])
                    Aj = PWb[:, 0:7, 0, to * 32:to * 32 + 32].rearrange("p j g -> p g j").unsqueeze(2).to_broadcast([128, 32, 7, 7])
                    Bj = PWb[:, 0:7, 1, to * 32:to * 32 + 32].rearrange("p j g -> p g j").unsqueeze(2).to_broadcast([128, 32, 7, 7])
                    t1 = big[:, 0:1568].rearrange("p (g k j) -> p g k j", g=32, k=7)
                    TFN = ["tf0", "tf1", "tf2", "tf3"]
                    TT("dve", t1, Aj, Tsame, ALU.mult, ["Z", "PWb"] + TFN, TFN)
                    TT("dve", zt, zt, t1, ALU.add, ["Z"] + TFN, ["Z"])
                    TT("dve", t1, Bj, Tother, ALU.mult, ["Z", "PWb"] + TFN, TFN)
                    TT("dve", zt, zt, t1, ALU.add, ["Z"] + TFN, ["Z"])
                CP("dve", ST[:, l, :].rearrange("p (t g) -> p t g", t=2), Z[:, :, :, 64], ["Z"], ["ST%d" % l])
                CP("act", zbf, Z[:, :, :, 0:64], ["Z"], ["zbf"])
                S.mute = kcut < 8
                S.tag = "ph8"
                for q in range(8):
                    lb = lcb[q % 2]
                    S.dma("sp", lambda e, l=l, q=q, lb=lb: e.dma_start(out=lb, in_=lcm[l, :, q, :]), ["lcm%d" % l], ["lcb%d" % (q % 2)])
                    bk = nextbank()
                    yp = psb[bk]
                    ypv = yp.rearrange("p (j c) -> p j c", j=8)
                    MM(yp, lb[:, 0:128], ubuf[:, q, :], True, False, ["lcb%d" % (q % 2), ("u", q)], ["ps%d" % bk], sig=False)
                    for k in range(1, 8):
                        MM(yp[:, k * 64:512], lb[:, k * 128:(k + 1) * 128], ubuf[:, q, 0:(8 - k) * 64], False, False,
                           ["lcb%d" % (q % 2), ("u", q)], ["ps%d" % bk], sig=False)
                    for r in range(4):
                        for j in range(8):
                            for t_ in range(2):
                                o = 1024 + j * 256 + t_ * 128 + r * 32
                                last = (r == 3 and j == 7 and t_ == 1)
                                MM(ypv[32 * r:32 * r + 32, j, :], lb[:, o:o + 32], zbf[:, t_, 4 * q + r, :], False, (j == 7 and t_ == 1),
                                   ["lcb%d" % (q % 2), "zbf"], ["ps%d" % bk], sig=last, tp=(0, 32 * r))
                    ACT(tf[0], yp, AF.Square, ["ps%d" % bk], ["tf0"])
                    TS("dve", tf[0], tf[0], 0.044715, 1.0, ALU.mult, ALU.add, ["tf0"], ["tf0"])
                    TT("dve", tf[0], tf[0], yp, ALU.mult, ["tf0", "ps%d" % bk], ["tf0"])
                    ACT(tf[1], tf[0], AF.Sigmoid, ["tf0"], ["tf1"], scale=2.0 * math.sqrt(2.0 / math.pi))
                    TT("dve", yg[:, q, :].rearrange("p (c j) -> p c j", j=8), tf[1].rearrange("p (j c) -> p c j", j=8), yp.rearrange("p (j c) -> p c j", j=8), ALU.mult, ["tf1", "ps%d" % bk], [("yg", q)])
                S.mute = kcut < 9
                S.tag = "ph9"
                wv, wr = load_slab("gl", l, 0, 8, 0, 1024)
                for mt in range(8):
                    bk = nextbank()
                    for kc in range(8):
                        MM(psb[bk], wv[:, kc, mt * 128:(mt + 1) * 128], yg[:, kc, :], kc == 0, kc == 7,
                           [wr, ("yg", kc)], ["ps%d" % bk], sig=(kc == 7))
                    tbi = mt % 3
                    ACT(tb[tbi], psb[bk], AF.Sigmoid, ["ps%d" % bk], ["tb%d" % tbi])
                    TT("dve", ubuf[:, mt, :], yg[:, mt, :], tb[tbi], ALU.mult, [("yg", mt), "tb%d" % tbi], [("u", mt)])
                S.mute = kcut < 10
                S.tag = "ph10"
                RY = [("yg", q) for q in range(8)] + ["Z", "zbf"]
                for grp in range(4):
                    gv, gr = load_slab("in", l, 0, 16, 3072 + grp * 512, 512)
                    wv, wr = load_slab("co", l, 0, 8, grp * 512, 512)
                    for mi in range(4):
                        bkg = nextbank()
                        for kc in range(16):
                            MM(psb[bkg], gv[:, kc, mi * 128:(mi + 1) * 128], hbuf[:, kc, :], kc == 0, kc == 15,
                               [gr, ("h", kc)], ["ps%d" % bkg], sig=(kc == 15))
                        ACT(tb[mi], psb[bkg], AF.Sigmoid, ["ps%d" % bkg], ["tb%d" % mi])
                    for mi in range(4):
                        mt = grp * 4 + mi
                        bk = nextbank()
                        for kc in range(8):
                            MM(psb[bk], wv[:, kc, mi * 128:(mi + 1) * 128], vconv[:, kc, :], kc == 0, kc == 7,
                               [wr, ("v", kc)], ["ps%d" % bk], sig=(kc == 7))
                        extra = RY + RZ if mt == 0 else []
                        TT("dve", mbuf[:, mt, :], psb[bk], tb[mi], ALU.mult, ["ps%d" % bk, "tb%d" % mi] + extra, [("m", mt)] + extra)
                S.mute = kcut < 11
                S.tag = "ph11"
                for grp in range(4):
                    gv, gr = load_slab("in", l, 0, 16, 5120 + grp * 512, 512)
                    wv, wr = load_slab("so", l, 0, 8, grp * 512, 512)
                    for mi in range(4):
                        bkg = nextbank()
                        for kc in range(16):
                            MM(psb[bkg], gv[:, kc, mi * 128:(mi + 1) * 128], hbuf[:, kc, :], kc == 0, kc == 15,
                               [gr, ("h", kc)], ["ps%d" % bkg], sig=(kc == 15))
                        ACT(tb[mi], psb[bkg], AF.Sigmoid, ["ps%d" % bkg], ["tb%d" % mi])
                    for mi in range(4):
                        mt = grp * 4 + mi
                        bk = nextbank()
                        for kc in range(8):
                            MM(psb[bk], wv[:, kc, mi * 128:(mi + 1) * 128], ubuf[:, kc, :], kc == 0, kc == 7,
                               [wr, ("u", kc)], ["ps%d" % bk], sig=(kc == 7))
                        TT("dve", tf[mt % 2], psb[bk], tb[mi], ALU.mult, ["ps%d" % bk, "tb%d" % mi], ["tf%d" % (mt % 2)])
                        TT("dve", mbuf[:, mt, :], mbuf[:, mt, :], tf[mt % 2], ALU.add, [("m", mt), "tf%d" % (mt % 2)], [("m", mt)])
                S.mute = kcut < 12
                S.tag = "ph12"
                for s in range(4):
                    wv, wr = load_slab("o", l, 0, 16, s * 512, 512)
                    for mi in range(4):
                        mt = s * 4 + mi
                        bk = nextbank()
                        for kc in range(16):
                            MM(psb[bk], wv[:, kc, mi * 128:(mi + 1) * 128], mbuf[:, kc, :], kc == 0, kc == 15,
                               [wr, ("m", kc)], ["ps%d" % bk], sig=(kc == 15))
                        TT("dve", xres[:, mt, :], xres[:, mt, :], psb[bk], ALU.add, [("x", mt), "ps%d" % bk], [("x", mt)])
                S.mute = kcut < 13
                S.tag = "ph13"
                rmsnorm_to_h(g0 + 16)
                for c in range(16):
                    STT(hbuf[:, c, :], xres[:, c, :], GS[:, g0 + 16 + c:g0 + 17 + c], tf[0], ALU.mult, ALU.mult,
                        [("x", c), "gs", "tf0"], [("h", c)])
                RM = [("m", c) for c in range(16)]
                for half in range(2):
                    for s in range(8):
                        wv, wr = load_slab("f1", l, 0, 16, half * 4096 + s * 512, 512)
                        bks = [nextbank() for _ in range(4)]
                        kco = (half == 0 and s == 0)
                        if kco:
                            for kc in range(16):
                                for mi in range(4):
                                    MM(psb[bks[mi]], wv[:, kc, mi * 128:(mi + 1) * 128], hbuf[:, kc, :], kc == 0, kc == 15,
                                       [wr, ("h", kc)], ["ps%d" % bks[mi]], sig=(kc == 15))
                        for mi in range(4):
                            bk = bks[mi]
                            for kc in range(0 if kco else 16):
                                MM(psb[bk], wv[:, kc, mi * 128:(mi + 1) * 128], hbuf[:, kc, :], kc == 0, kc == 15,
                                   [wr, ("h", kc)], ["ps%d" % bk], sig=(kc == 15))
                            ai = s * 4 + mi
                            tbi = ai % 3
                            ACT(tb[tbi], psb[bk], AF.Relu, ["ps%d" % bk], ["tb%d" % tbi])
                            extra = RM if (ai == 0) else []
                            TT("dve", abuf[:, ai, :], tb[tbi], tb[tbi], ALU.mult, ["tb%d" % tbi] + extra, [("a", ai)] + extra)
                    for cg in range(4):
                        bset = 4 if cg % 2 == 0 else 0
                        for kq in range(2):
                            wv, wr = load_slab("f2", l, half * 4096 + kq * 2048, 16, cg * 512, 512)
                            for mi in range(4):
                                bk = bset + mi
                                for kc in range(16):
                                    st = (kq == 0 and kc == 0)
                                    sp_ = (kq == 1 and kc == 15)
                                    MM(psb[bk], wv[:, kc, mi * 128:(mi + 1) * 128], abuf[:, kq * 16 + kc, :], st, sp_,
                                       [wr, ("a", kq * 16 + kc)], ["ps%d" % bk], sig=(kc == 15))
                        for mi in range(4):
                            mt = cg * 4 + mi
                            bk = bset + mi
                            TT("dve", xres[:, mt, :], xres[:, mt, :], psb[bk], ALU.add, [("x", mt), "ps%d" % bk], [("x", mt)])
                RA = [("a", i) for i in range(32)]
                S.op("dve", lambda e: e.memset(sc1, 0.0), RA + ["sc1"], RA + RZ + ["sc1"])
            S.mute = False
            S.tag = "final"
            rmsnorm_to_h(L * 32)
            for c in range(16):
                STT(ostage[:, c, :], xres[:, c, :], GS[:, L * 32 + c:L * 32 + c + 1], tf[0], ALU.mult, ALU.mult,
                    [("x", c), "gs", "tf0"] + (RZ if c == 0 else []), [("os", c)] + (RZ if c == 0 else []))
            S.dma("sp", lambda e, t0=t0: e.dma_start(out=outT[:, t0:t0 + NT].rearrange("(c p) n -> p c n", p=128), in_=ostage),
                  [("os", c) for c in range(16)], ["outdma"] + RZ)
            last_out_tok.append(S.lastw["outdma"])
        S.final_wait("sp", last_out_tok)

        sems = {}
        for sk in sorted(S.semkeys, key=str):
            sems[sk] = es.enter_context(nc.semaphore("s_%s_%d" % (sk[0], sk[1])))
        with nc.Block() as block:
            @block.tensor
            def _(e):
                S.replay("pe", e, sems)

            @block.scalar
            def _(e):
                S.replay("act", e, sems)

            @block.vector
            def _(e):
                S.replay("dve", e, sems)

            @block.gpsimd
            def _(e):
                S.replay("pool", e, sems)

            @block.sync
            def _(e):
                S.replay("sp", e, sems)
    nc._sched = S
    return nc


def host_pack(inp):
    f = np.float32
    L = DEPTH
    pl = np.zeros((L, 128, NPL), f)
    pbk = np.zeros((L, 128, 4, 32, 16), f)
    wdl = np.zeros((L, 128, 8 * KW), f)

    def col(v, n):
        return np.ascontiguousarray(v.reshape(n, 128).T)

    for l in range(L):
        pl[l, :, O_GMIX:O_GMIX + 16] = col(inp["norm_mix"][l], 16)
        pl[l, :, O_GMLP:O_GMLP + 16] = col(inp["norm_mlp"][l], 16)
        pl[l, :, O_BDW:O_BDW + 8] = col(inp["b_dw"][l], 8)
        pl[l, :, O_LNG:O_LNG + 8] = col(inp["ln_g"][l], 8)
        pl[l, :, O_LNB:O_LNB + 8] = col(inp["ln_b"][l], 8)
        wdl[l] = inp["w_dw"][l].T.reshape(8, 128, KW).transpose(1, 0, 2).reshape(128, 248)
        pl[l, :, O_DSK:O_DSK + 8] = col(inp["d_skip"][l], 8)

        def gp(a):
            return a.reshape(32, 2, 64).transpose(1, 2, 0).reshape(128, 32)
        pl[l, :, O_ARE:O_ARE + 32] = gp(inp["a_re"][l])
        pl[l, :, O_AIM:O_AIM + 32] = gp(inp["a_im"][l])
        pl[l, :, O_LDT:O_LDT + 32] = gp(np.repeat(inp["log_dt"][l][:, None], 64, axis=1))
        for i, nm in enumerate(("b_re", "b_im")):
            pbk[l, :, i] = inp[nm][l].reshape(32, 2, 64, 16).transpose(1, 2, 0, 3).reshape(128, 32, 16)
        for i, nm in enumerate(("c_re", "c_im")):
            pbk[l, :, 2 + i] = inp[nm][l].reshape(32, 2, 16, 64).transpose(1, 3, 0, 2).reshape(128, 32, 16)
    cst = np.zeros((128, 784), f)
    cst[:, 0:128] = np.eye(128, dtype=f)
    cst[:, 128:256] = 1.0
    blk = np.kron(np.eye(8, dtype=f), np.ones((16, 16), f))
    cst[:, 256:768] = np.tile(blk, (1, 4))
    cst[:, 768:784] = col(inp["norm_final"], 16)
    return pl, pbk.reshape(L, 128, 2048), cst, wdl


def make_shared(inp):
    pl, pbk, cst, wdl = host_pack(inp)
    return {"w_in": inp["w_in"], "w_conv_out": inp["w_conv_out"], "w_glu": inp["w_glu"], "w_ssm_out": inp["w_ssm_out"],
            "w_out": inp["w_out"], "w_ff1": inp["w_ff1"], "w_ff2": inp["w_ff2"], "pl": pl, "pb": pbk, "cst": cst, "wd": wdl}


_NC_CACHE = {}


def kernel(**inputs):
    inp = {k: np.asarray(v) for k, v in inputs.items()}
    x = inp["x"]
    B = x.shape[0]
    shared = make_shared(inp)
    if "nc" not in _NC_CACHE:
        _NC_CACHE["nc"] = build_nc()
    nc = _NC_CACHE["nc"]
    in_maps = []
    for b in range(B):
        m = dict(shared)
        m["xT"] = np.ascontiguousarray(x[b].T)
        in_maps.append(m)
    res = run_bass_kernel_spmd(nc, in_maps, core_ids=list(range(B)))
    out = np.stack([np.ascontiguousarray(r["outT"].T) for r in res.results], axis=0)
    return out.astype(np.float32)
```

```python
import math
import numpy as np
import concourse.bass as bass
import concourse.mybir as mybir
from concourse.bass_utils import run_bass_kernel_spmd

F32 = mybir.dt.float32
BF16 = mybir.dt.bfloat16
AF = mybir.ActivationFunctionType
ALU = mybir.AluOpType

D = 2048
SEQ = 4096
DEPTH = 4
CWID = 1024
KW = 31
HIST = KW - 1
NT = 512
DFF = 8192
INC = 7168
EPS_RMS = 1e-6
EPS_LN = 1e-5
NPL = 160
EPOCH = 2000
NSLOT = 8

O_GMIX, O_GMLP, O_BDW, O_LNG, O_LNB, O_DSK, O_ARE, O_AIM, O_LDT = 0, 16, 32, 40, 48, 56, 64, 96, 128


class Sched:
    def __init__(self):
        self.streams = {e: [] for e in ("pe", "act", "dve", "pool", "sp")}
        self.nsig = {e: 0 for e in ("pe", "act", "dve", "pool")}
        self.lastw = {}
        self.readers = {}
        self.known = {e: {} for e in self.streams}
        self.dma_i = {"sp": 0, "pool": 0}
        self.semkeys = set()
        self.mute = False
        self.tag = "init"
        self.pe_tags = []

    def _need(self, reads, writes):
        need = {}

        def add(t):
            if t is None:
                return
            k, v = t
            if need.get(k, 0) < v:
                need[k] = v
        for r in reads:
            add(self.lastw.get(r))
        for w in writes:
            add(self.lastw.get(w))
            for t in self.readers.get(w, {}).items():
                add(t)
        return need

    def _waits(self, eng, need):
        waits = []
        kn = self.known[eng]
        for sk, v in need.items():
            if eng == "pe" and sk[0] == "pe":
                continue
            if sk[0] in self.nsig:
                if any(k2[0] == sk[0] and k2[1] > sk[1] for k2 in kn):
                    continue
            if kn.get(sk, 0) >= v:
                continue
            kn[sk] = v
            waits.append((sk, v))
        return waits

    def _commit(self, tok, reads, writes):
        for r in reads:
            d = self.readers.setdefault(r, {})
            if d.get(tok[0], 0) < tok[1]:
                d[tok[0]] = tok[1]
        for w in writes:
            self.lastw[w] = tok
            self.readers[w] = {}

    def op(self, eng, fn, reads=(), writes=(), sig=True):
        if self.mute:
            return
        if eng == "pe":
            self.pe_tags.append(self.tag)
        need = self._need(reads, writes)
        waits = self._waits(eng, need)
        n = self.nsig[eng]
        sk = (eng, n // EPOCH)
        tok = (sk, n % EPOCH + 1)
        self.semkeys.add(sk)
        self.streams[eng].append((waits, fn, (sk, 1) if sig else None))
        if sig:
            self.nsig[eng] = n + 1
        self._commit(tok, reads, writes)

    def dma(self, q, fn, reads=(), writes=(), extra=()):
        if self.mute:
            return
        i = self.dma_i[q]
        self.dma_i[q] = i + 1
        slot, use = i % NSLOT, i // NSLOT
        sk = ("d" + q, slot)
        self.semkeys.add(sk)
        need = self._need(reads, writes)
        for k_, v_ in extra:
            if need.get(k_, 0) < v_:
                need[k_] = v_
        if use > 0:
            if need.get(sk, 0) < 16 * use:
                need[sk] = 16 * use
        waits = self._waits(q, need)
        self.streams[q].append((waits, fn, (sk, 16)))
        self._commit((sk, 16 * (use + 1)), reads, writes)

    def final_wait(self, eng, toks):
        need = {}
        for k, v in toks:
            if need.get(k, 0) < v:
                need[k] = v
        waits = self._waits(eng, need)
        self.streams[eng].append((waits, None, None))

    def replay(self, eng, handle, sems):
        for waits, fn, inc in self.streams[eng]:
            for sk, v in waits:
                handle.wait_ge(sems[sk], v)
            if fn is None:
                continue
            ins = fn(handle)
            if inc is not None:
                ins.then_inc(sems[inc[0]], inc[1])


def build_nc(n_layers=DEPTH, n_tiles=SEQ // NT, dbg=False, kcut=99, kprep=True):
    nc = bass.Bass("TRN2", target_bir_lowering=False)
    S = Sched()
    L = n_layers
    T = n_tiles * NT

    def din(name, shape, dt=F32):
        return nc.dram_tensor(name, shape, dt, kind="ExternalInput").ap()

    def dscr(name, shape, dt=BF16):
        return nc.dram_tensor(name, shape, dt, kind="Internal").ap()

    xT = din("xT", [D, T])
    w_in = din("w_in", [DEPTH, D, INC])
    w_co = din("w_conv_out", [DEPTH, CWID, D])
    w_gl = din("w_glu", [DEPTH, CWID, CWID])
    w_so = din("w_ssm_out", [DEPTH, CWID, D])
    w_o = din("w_out", [DEPTH, D, D])
    w_f1 = din("w_ff1", [DEPTH, D, DFF])
    w_f2 = din("w_ff2", [DEPTH, DFF, D])
    pl = din("pl", [DEPTH, 128, NPL])
    pb = din("pb", [DEPTH, 128, 4 * 512])
    wd = din("wd", [DEPTH, 128, 8 * KW])
    cst = din("cst", [128, 128 + 128 + 512 + 16])
    outT = nc.dram_tensor("outT", [D, T], F32, kind="ExternalOutput").ap()
    dbg_out = None
    if dbg:
        dbg_out = nc.dram_tensor("dbg", [128, 16 * NT], F32, kind="ExternalOutput").ap()

    wsrc = {"in": (w_in, D, INC), "co": (w_co, CWID, D), "gl": (w_gl, CWID, CWID), "so": (w_so, CWID, D),
            "o": (w_o, D, D), "f1": (w_f1, D, DFF), "f2": (w_f2, DFF, D)}
    wb = {k: dscr("wb_" + k, [L, v[1], v[2]]) for k, v in wsrc.items()}
    zinm = dscr("zinm", [L, 128, 8, 2048])
    lcm = dscr("lcm", [L, 128, 8, 3072])
    dgm = dscr("dgm", [L, 128, 4, 2 * KW * 128])
    pwtab = nc.dram_tensor("pwtab", [L, 128, 1024], F32, kind="Internal").ap()

    import contextlib
    es = contextlib.ExitStack()
    with es:
        ARENA_W = 52600
        arena = es.enter_context(nc.sbuf_tensor("arena", [128, ARENA_W], F32))
        ptr = [0]

        def alloc(words):
            a = ptr[0]
            ptr[0] += words
            assert ptr[0] <= ARENA_W, ptr[0]
            return a

        def vf(a, words):
            return arena[:, a:a + words]

        def vb(a, words):
            return arena[:, a:a + words].bitcast(BF16)

        a_pl = alloc(L * NPL)
        PL = vf(a_pl, L * NPL).rearrange("p (l n) -> p l n", l=L)
        a_c = alloc(128 + 128 + 512 + 16)
        CST = vf(a_c, 784)
        ident = CST[:, 0:128]
        ones_f = CST[:, 128:256]
        bmask = CST[:, 256:768]
        gfin = CST[:, 768:784]
        a_ob = alloc(64)
        ones_b = vb(a_ob, 64)
        a_gs = alloc(L * 32 + 16)
        GS = vf(a_gs, L * 32 + 16)
        a_mu = alloc(L * 128)
        MU = vf(a_mu, L * 128).rearrange("p (l t n) -> p l t n", l=L, t=2)
        a_st = alloc(L * 64)
        ST = vf(a_st, L * 64).rearrange("p (l n) -> p l n", l=L)
        a_hi = alloc(L * 8 * HIST)
        HI = vf(a_hi, L * 8 * HIST).rearrange("p (l c n) -> p l c n", l=L, c=8)
        a_np = alloc(3)
        negpi = vf(a_np, 1)
        epsr = vf(a_np + 1, 1)
        epsl = vf(a_np + 2, 1)
        a_main = ptr[0]
        a_x = alloc(16 * NT)
        xres = vf(a_x, 16 * NT).rearrange("p (c n) -> p c n", c=16)
        a_wb = [alloc(4096) for _ in range(3)]
        wbufs = [vb(a, 4096) for a in a_wb]
        a_h = alloc(4096)
        hbuf = vb(a_h, 4096).rearrange("p (c n) -> p c n", c=16)
        a_R = alloc(8432)
        uc = vb(a_R, 4 * (NT + HIST)).rearrange("p (c n) -> p c n", c=8)
        convo = vf(a_R + 4 * (NT + HIST), 4096).rearrange("p (c n) -> p c n", c=8)
        Z = vf(a_R, 4160).rearrange("p (t g c) -> p t g c", t=2, g=32)
        zbf = vb(a_R + 4160, 2048).rearrange("p (t g c) -> p t g c", t=2, g=32)
        yg = vb(a_R + 4160 + 2048, 2048).rearrange("p (c n) -> p c n", c=8)
        mbuf = vb(a_R, 4096).rearrange("p (c n) -> p c n", c=16)
        abuf = vb(a_R, 8192).rearrange("p (c n) -> p c n", c=32)
        ostage = vf(a_R, 8192).rearrange("p (c n) -> p c n", c=16)
        a_v = alloc(2048)
        vconv = vb(a_v, 2048).rearrange("p (c n) -> p c n", c=8)
        a_u = alloc(2048)
        ubuf = vb(a_u, 2048).rearrange("p (c n) -> p c n", c=8)
        a_zi = [alloc(1024) for _ in range(2)]
        zibuf = [vb(a, 1024).rearrange("p (j t n) -> p j t n", j=8, t=2) for a in a_zi]
        a_ubm = [alloc(1024) for _ in range(2)]
        ubm = [vb(a, 1024).rearrange("p (j r c) -> p j r c", j=8, r=4) for a in a_ubm]
        a_lc = [alloc(1536) for _ in range(2)]
        lcb = [vb(a, 1536) for a in a_lc]
        a_sq = [alloc(256) for _ in range(2)]
        sqb = [vb(a, 256) for a in a_sq]
        a_t = [alloc(512) for _ in range(4)]
        tf = [vf(a, 512) for a in a_t]
        sqf = [tf[2], tf[3]]
        a_tb = [alloc(256) for _ in range(4)]
        tb = [vb(a, 256) for a in a_tb]
        a_pw = alloc(1024)
        PWb = vf(a_pw, 1024).rearrange("p (j t n) -> p j t n", j=8, t=2)
        a_s1 = alloc(64)
        a_s2 = alloc(64)
        sc1 = vf(a_s1, 64).rearrange("p (t g) -> p t g", t=2)
        sc2 = vf(a_s2, 64).rearrange("p (t g) -> p t g", t=2)
        a_end = ptr[0]
        ptr[0] = a_main
        a_pb = alloc(2048)
        PB = vf(a_pb, 2048).rearrange("p (t g h) -> p t g h", t=4, g=32)
        sm = [vf(alloc(32), 32) for _ in range(24)]
        smi = vf(alloc(32), 32).bitcast(mybir.dt.int32)
        Pc = [vf(alloc(512), 512).rearrange("p (g h) -> p g h", g=32) for _ in range(4)]
        Wc = [vf(alloc(512), 512).rearrange("p (g h) -> p g h", g=32) for _ in range(4)]
        tmpc = [vf(alloc(512), 512).rearrange("p (g h) -> p g h", g=32) for _ in range(2)]
        Pexp = [vf(alloc(1024), 1024) for _ in range(2)]
        Cexp = [vf(alloc(1024), 1024) for _ in range(2)]
        Wexp = [vf(alloc(1024), 1024) for _ in range(2)]
        zin_sb = vb(alloc(8192), 8192).rearrange("p (q j t n) -> p q j t n", q=8, j=8, t=2)
        lc_sb = vb(alloc(12288), 12288).rearrange("p (q n) -> p q n", q=8)
        dg_sb = vb(alloc(KW * 128), KW * 128).rearrange("p (c k n) -> p c k n", c=2, k=KW)
        WD = vf(alloc(8 * KW), 8 * KW)
        PWs = vf(alloc(1024), 1024).rearrange("p (j t n) -> p j t n", j=8, t=2)
        pwm = [vf(alloc(32), 32) for _ in range(4)]
        assert ptr[0] <= ARENA_W, ptr[0]

        psbig = es.enter_context(nc.psum_tensor("psbig", [128, 4096], F32))[:]
        psb = [psbig[:, i * 512:(i + 1) * 512] for i in range(8)]
        pool_i = [0]

        def nextbank(lo=0, hi=4):
            i = lo + pool_i[0] % (hi - lo)
            pool_i[0] += 1
            return i

        def TT(eng, out, a, b, op, R, W):
            S.op(eng, lambda e: e.tensor_tensor(out=out, in0=a, in1=b, op=op), R, W)

        def TS(eng, out, a, s1, s2, op0, op1, R, W):
            if s2 is None:
                S.op(eng, lambda e: e.tensor_scalar(out=out, in0=a, scalar1=s1, scalar2=None, op0=op0), R, W)
            else:
                S.op(eng, lambda e: e.tensor_scalar(out=out, in0=a, scalar1=s1, scalar2=s2, op0=op0, op1=op1), R, W)

        def STT(out, a, sc, b, op0, op1, R, W):
            S.op("dve", lambda e: e.scalar_tensor_tensor(out=out, in0=a, scalar=sc, in1=b, op0=op0, op1=op1), R, W)

        def ACT(out, a, func, R, W, bias=None, scale=None):
            kw = {}
            if bias is not None:
                kw["bias"] = bias
            if scale is not None:
                kw["scale"] = scale
            S.op("act", lambda e: e.activation(out=out, in_=a, func=func, **kw), R, W)

        def CP(eng, out, a, R, W):
            if eng == "act":
                S.op("act", lambda e: e.copy(out=out, in_=a), R, W)
            else:
                S.op(eng, lambda e: e.tensor_copy(out=out, in_=a), R, W)

        def MM(out, lhsT, rhs, start, stop, R, W, sig=None, tp=None):
            kw = {}
            if tp is not None:
                kw["tile_position"] = tp
            S.op("pe", lambda e: e.matmul(out, lhsT, rhs, start=start, stop=stop, **kw), R, W,
                 sig=True if sig is None else sig)

        S.dma("sp", lambda e: e.dma_start(out=CST, in_=cst[:, :]), [], ["cst"])
        S.dma("sp", lambda e: e.dma_start(out=PL, in_=pl[0:L].rearrange("l p n -> p l n")), [], ["pl"])
        S.op("dve", lambda e: e.memset(negpi, -math.pi), [], ["negpi"])
        S.op("dve", lambda e: e.memset(epsr, D * EPS_RMS), ["negpi"], ["negpi"])
        S.op("dve", lambda e: e.memset(epsl, EPS_LN), ["negpi"], ["negpi"])
        CP("dve", ones_b, ones_f, ["cst"], ["ones_b"])
        S.op("dve", lambda e: e.memset(vf(a_st, L * 64), 0.0), [], ["ST%d" % l for l in range(L)])
        S.op("dve", lambda e: e.memset(vf(a_hi, L * 8 * HIST), 0.0), [], ["HI%d" % l for l in range(L)])
        for l in range(L):
            TS("dve", GS[:, l * 32:l * 32 + 32], PL[:, l, 0:32], math.sqrt(D), None, ALU.mult, None, ["pl"], ["gs"])
        TS("dve", GS[:, L * 32:L * 32 + 16], gfin, math.sqrt(D), None, ALU.mult, None, ["cst"], ["gs"])

        def wpieces(key):
            _, K_, N_ = wsrc[key]
            cw = 1792 if N_ == INC else min(N_, 2048)
            return [(r, c, cw) for r in range(K_ // 128) for c in range(N_ // cw)]

        def cast_items(l):
            items = []
            for key in ("in", "co", "gl", "so", "o", "f1", "f2"):
                src = wsrc[key][0]
                for (r, c, cw) in wpieces(key):
                    items.append((lambda e, key=key, src=src, l=l, r=r, c=c, cw=cw: e.dma_start(
                        out=wb[key][l, r * 128:(r + 1) * 128, c * cw:(c + 1) * cw],
                        in_=src[l, r * 128:(r + 1) * 128, c * cw:(c + 1) * cw]), ("wb", key, l, r, c)))
            return items

        pending_cast = []
        for fn_, res_ in cast_items(0):
            S.dma("pool", fn_, [], [res_])

        def pace_cast(n, extra=()):
            for _ in range(min(n, len(pending_cast))):
                fn_, res_ = pending_cast.pop(0)
                S.dma("pool", fn_, [], [res_], extra=extra)

        cnt = [0]

        def cmul(o_re, o_im, a_re, a_im, b_re, b_im, t0, t1, R, W, eng="dve"):
            big = (t0 is tmpc[0])
            r0, r1 = ("tmpc0", "tmpc1") if big else ("s_q0", "s_q1")
            TT(eng, t0, a_re, b_re, ALU.mult, R, [r0])
            TT(eng, t1, a_im, b_im, ALU.mult, R, [r1])
            TT(eng, o_re, t0, t1, ALU.subtract, [r0, r1], W[0:1])
            TT(eng, t0, a_re, b_im, ALU.mult, R, [r0])
            TT(eng, t1, a_im, b_re, ALU.mult, R, [r1])
            TT(eng, o_im, t0, t1, ALU.add, [r0, r1], W[1:2])

        def bc(t):
            return t.unsqueeze(2).to_broadcast([128, 32, 16])

        S.tag = "prep"
        for l in range(L if kprep else 0):
            are, aim, ldt = PL[:, l, O_ARE:O_ARE + 32], PL[:, l, O_AIM:O_AIM + 32], PL[:, l, O_LDT:O_LDT + 32]
            S.dma("sp", lambda e, l=l: e.dma_start(out=vf(a_pb, 2048), in_=pb[l]), [], ["PB"])
            dt_, xr, xi, mag, sa, ca, sn, cs, lr, li, den, rden, nr, kr, ki, q0, q1, q2, q3 = sm[0:19]
            ACT(dt_, ldt, AF.Exp, ["pl"], ["s_dt"])
            TT("dve", xr, are, dt_, ALU.mult, ["pl", "s_dt"], ["s_xr"])
            TT("dve", xi, aim, dt_, ALU.mult, ["pl", "s_dt"], ["s_xi"])
            ACT(mag, xr, AF.Exp, ["s_xr"], ["s_mag"])
            for (dst, off, nm) in ((sn, 0.0, "s_sn"), (cs, 0.25, "s_cs")):
                TS("dve", sa, xi, 1.0 / (2.0 * math.pi), off, ALU.mult, ALU.add, ["s_xi"], ["s_sa"])
                CP("dve", smi, sa, ["s_sa"], ["s_smi"])
                CP("dve", ca, smi, ["s_smi"], ["s_ca"])
                TT("dve", sa, sa, ca, ALU.subtract, ["s_sa", "s_ca"], ["s_sa"])
                ACT(ca, sa, AF.Sin, ["s_sa"], ["s_ca"], scale=math.pi)
                ACT(q3, sa, AF.Sin, ["s_sa"], ["s_q3"], scale=math.pi / 2.0)
                TT("dve", q3, q3, q3, ALU.mult, ["s_q3"], ["s_q3"])
                TS("dve", q3, q3, -4.0, 2.0, ALU.mult, ALU.add, ["s_q3"], ["s_q3"])
                TT("dve", dst, ca, q3, ALU.mult, ["s_ca", "s_q3"], [nm])
            TT("dve", lr, mag, cs, ALU.mult, ["s_mag", "s_cs"], ["s_lr"])
            TT("dve", li, mag, sn, ALU.mult, ["s_mag", "s_sn"], ["s_li"])
            TT("dve", q0, are, are, ALU.mult, ["pl"], ["s_q0"])
            TT("dve", q1, aim, aim, ALU.mult, ["pl"], ["s_q1"])
            TT("dve", den, q0, q1, ALU.add, ["s_q0", "s_q1"], ["s_den"])
            S.op("dve", lambda e: e.reciprocal(out=rden, in_=den), ["s_den"], ["s_rden"])
            TS("dve", nr, lr, -1.0, None, ALU.add, None, ["s_lr"], ["s_nr"])
            TT("dve", q0, nr, are, ALU.mult, ["s_nr", "pl"], ["s_q0"])
            TT("dve", q1, li, aim, ALU.mult, ["s_li", "pl"], ["s_q1"])
            TT("dve", q2, q0, q1, ALU.add, ["s_q0", "s_q1"], ["s_q2"])
            TT("dve", kr, q2, rden, ALU.mult, ["s_q2", "s_rden"], ["s_kr"])
            TT("dve", q0, li, are, ALU.mult, ["s_li", "pl"], ["s_q0"])
            TT("dve", q1, nr, aim, ALU.mult, ["s_nr", "pl"], ["s_q1"])
            TT("dve", q2, q0, q1, ALU.subtract, ["s_q0", "s_q1"], ["s_q2"])
            TT("dve", ki, q2, rden, ALU.mult, ["s_q2", "s_rden"], ["s_ki"])
            m_re, m_im, n_re, n_im = sm[19:23]
            cmul(m_re, m_im, lr, li, lr, li, q0, q1, ["s_lr", "s_li"], ["s_mre", "s_mim"])
            cmul(n_re, n_im, m_re, m_im, m_re, m_im, q0, q1, ["s_mre", "s_mim"], ["s_nre", "s_nim"])
            cmul(m_re, m_im, n_re, n_im, n_re, n_im, q0, q1, ["s_nre", "s_nim"], ["s_mre", "s_mim"])
            mr = "MU%d" % l
            CP("dve", MU[:, l, 0, 0:32], m_re, ["s_mre"], [mr + "a"])
            CP("dve", MU[:, l, 0, 32:64], m_re, ["s_mre"], [mr + "b"])
            TS("dve", MU[:, l, 1, 0:32], m_im, -1.0, None, ALU.mult, None, ["s_mim"], [mr + "c"])
            CP("dve", MU[:, l, 1, 32:64], m_im, ["s_mim"], [mr + "d"])
            pcur = (m_re, m_im, "s_mre", "s_mim")
            for j_ in range(8):
                CP("dve", PWs[:, j_, 0, 0:32], pcur[0], [pcur[2]], ["PWs"])
                CP("dve", PWs[:, j_, 0, 32:64], pcur[0], [pcur[2]], ["PWs"])
                TS("dve", PWs[:, j_, 1, 0:32], pcur[1], -1.0, None, ALU.mult, None, [pcur[3]], ["PWs"])
                CP("dve", PWs[:, j_, 1, 32:64], pcur[1], [pcur[3]], ["PWs"])
                if j_ < 7:
                    nx_ = (pwm[0], pwm[1], "pwm0", "pwm1") if j_ % 2 == 0 else (pwm[2], pwm[3], "pwm2", "pwm3")
                    cmul(nx_[0], nx_[1], pcur[0], pcur[1], m_re, m_im, q0, q1, [pcur[2], pcur[3], "s_mre", "s_mim"], [nx_[2], nx_[3]])
                    pcur = nx_
            S.dma("sp", lambda e, l=l: e.dma_start(out=pwtab[l], in_=PWs.rearrange("p j t n -> p (j t n)")), ["PWs"], ["pwtab%d" % l])
            Bre, Bim, Cre, Cim = PB[:, 0], PB[:, 1], PB[:, 2], PB[:, 3]
            cmul(Pc[0], Pc[1], bc(kr), bc(ki), Bre, Bim, tmpc[0], tmpc[1], ["s_kr", "s_ki", "PB"], ["P0r", "P0i"])
            for t_ in range(2):
                S.op("dve", lambda e, t_=t_: e.memset(Cexp[t_], 0.0), [], ["Cexp%d" % t_])
                S.op("dve", lambda e, t_=t_: e.memset(Pexp[t_], 0.0), [], ["Pexp%d" % t_])
                S.op("dve", lambda e, t_=t_: e.memset(Wexp[t_], 0.0), [], ["Wexp%d" % t_])

            def expand(dst, src, R, W, scale=None, eng="dve"):
                dv = dst.rearrange("p (g e h) -> p g e h", g=32, e=2)
                for e_ in range(2):
                    o = dv[64 * e_:64 * e_ + 64, :, e_, :]
                    i = src[64 * e_:64 * e_ + 64, :, :]
                    if scale is None:
                        CP(eng, o, i, R, W)
                    else:
                        TS(eng, o, i, scale, None, ALU.mult, None, R, W)

            expand(Cexp[0], Cre, ["PB"], ["Cexp0"])
            expand(Cexp[1], Cim, ["PB"], ["Cexp1"], scale=-1.0)
            cur = 0
            for k in range(8):
                Pr, Pi = Pc[2 * cur], Pc[2 * cur + 1]
                rn = ["P%dr" % cur, "P%di" % cur]
                expand(Pexp[0], Pr, [rn[0]], ["Pexp0"])
                expand(Pexp[1], Pi, [rn[1]], ["Pexp1"])
                for qh in range(2):
                    bk = nextbank()
                    for qq in range(4):
                        q = qh * 4 + qq
                        o = psb[bk][:, qq * 128:(qq + 1) * 128]
                        MM(o, Pexp[0][:, q * 128:(q + 1) * 128], Cexp[0][:, q * 128:(q + 1) * 128], True, False,
                           ["Pexp0", "Cexp0"], ["ps%d" % bk], sig=False)
                        MM(o, Pexp[1][:, q * 128:(q + 1) * 128], Cexp[1][:, q * 128:(q + 1) * 128], False, True,
                           ["Pexp1", "Cexp1"], ["ps%d" % bk], sig=True)
                    o4 = lc_sb[:, qh * 4:qh * 4 + 4, k * 128:(k + 1) * 128]
                    TT("dve", o4, psb[bk].rearrange("p (q n) -> p q n", q=4),
                       bmask.rearrange("p (q n) -> p q n", q=4), ALU.mult, ["ps%d" % bk, "cst"], ["lc_sb"])
                    for t_ in range(2):
                        bk2 = nextbank()
                        for qq in range(4):
                            q = qh * 4 + qq
                            S.op("pe", lambda e, bk2=bk2, qq=qq, q=q, t_=t_: e.transpose(
                                psb[bk2][:, qq * 128:(qq + 1) * 128], Pexp[t_][:, q * 128:(q + 1) * 128], ident),
                                ["Pexp%d" % t_, "cst"], ["ps%d" % bk2], sig=(qq == 3))
                        CP("act", zin_sb[:, qh * 4:qh * 4 + 4, 7 - k, t_, :],
                           psb[bk2].rearrange("p (q n) -> p q n", q=4), ["ps%d" % bk2], ["zin_sb"])
                if k < 7:
                    nx = 1 - cur
                    cmul(Pc[2 * nx], Pc[2 * nx + 1], bc(lr), bc(li), Pr, Pi, tmpc[0], tmpc[1],
                         ["s_lr", "s_li"] + rn, ["P%dr" % nx, "P%di" % nx])
                    cur = nx
            for q in range(8):
                STT(lc_sb[:, q, 0:128], ident, PL[:, l, O_DSK + q:O_DSK + q + 1], lc_sb[:, q, 0:128], ALU.mult, ALU.add,
                    ["cst", "pl", "lc_sb"], ["lc_sb"])
            wcur = 0
            srcr, srci = Cre, Cim
            srcn = ["PB", "PB"]
            for k in range(1, 9):
                nr_, ni_ = Wc[2 * wcur], Wc[2 * wcur + 1]
                wn = ["W%dr" % wcur, "W%di" % wcur]
                cmul(nr_, ni_, bc(lr), bc(li), srcr, srci, tmpc[0], tmpc[1], ["s_lr", "s_li"] + srcn, wn)
                expand(Wexp[0], nr_, [wn[0]], ["Wexp0"])
                expand(Wexp[1], ni_, [wn[1]], ["Wexp1"], scale=-1.0)
                for t_ in range(2):
                    o = lc_sb[:, :, 1024 + (k - 1) * 256 + t_ * 128:1024 + (k - 1) * 256 + t_ * 128 + 128]
                    CP("act", o, Wexp[t_].rearrange("p (q n) -> p q n", q=8), ["Wexp%d" % t_], ["lc_sb"])
                srcr, srci, srcn = nr_, ni_, wn
                wcur = 1 - wcur
            S.dma("sp", lambda e, l=l: e.dma_start(out=WD, in_=wd[l]), [], ["WD"])
            wdw_l = WD.rearrange("p (c k) -> p c k", c=8)
            for s4 in range(4):
                for c2 in range(2):
                    TT("dve", dg_sb[:, c2, :, :], ident.unsqueeze(1).to_broadcast([128, KW, 128]),
                       wdw_l[:, 2 * s4 + c2, :].unsqueeze(2).to_broadcast([128, KW, 128]), ALU.mult,
                       ["cst", "WD"], ["dg_sb"])
                S.dma("sp", lambda e, l=l, s4=s4: e.dma_start(out=dgm[l, :, s4, :], in_=dg_sb.rearrange("p c k n -> p (c k n)")),
                      ["dg_sb"], ["dgm%d" % l])
            S.dma("sp", lambda e, l=l: e.dma_start(out=zinm[l], in_=zin_sb.rearrange("p q j t n -> p q (j t n)")),
                  ["zin_sb"], ["zinm%d" % l])
            S.dma("sp", lambda e, l=l: e.dma_start(out=lcm[l], in_=lc_sb), ["lc_sb"], ["lcm%d" % l])

        alltok = []
        for e_ in ("pe", "act", "dve", "pool"):
            n = S.nsig[e_]
            if n > 0:
                alltok.append(((e_, (n - 1) // EPOCH), (n - 1) % EPOCH + 1))
        for q_ in ("sp", "pool"):
            i = S.dma_i[q_]
            for s_ in range(min(i, NSLOT)):
                uses = (i - s_ + NSLOT - 1) // NSLOT
                alltok.append((("d" + q_, s_), 16 * uses))
        for e_ in ("pe", "act", "dve", "sp"):
            S.final_wait(e_, alltok)
        for i_ in range(2):
            S.op("dve", lambda e, i_=i_: e.memset(ubm[i_], 0.0), [], ["ubm0"] + ["ubm_%d_%d" % (i_, r) for r in range(4)])

        def wres(key, l, r0, r1, c0, c1):
            _, K_, N_ = wsrc[key]
            cw = 1792 if N_ == INC else min(N_, 2048)
            return [("wb", key, l, r, c) for r in range(r0 // 128, (r1 + 127) // 128)
                    for c in range(c0 // cw, (c1 - 1) // cw + 1)]

        slab_i = [0]

        def load_slab(key, l, r0, nk, c0, ncol):
            i = slab_i[0] % 3
            slab_i[0] += 1
            if pending_cast:
                pace_cast(5, extra=list(S.readers.get("wbuf%d" % i, {}).items()))
            v = wbufs[i][:, 0:nk * ncol].rearrange("p (k n) -> p k n", k=nk)
            srcap = wb[key][l, r0:r0 + nk * 128, c0:c0 + ncol].rearrange("(k p) n -> p k n", p=128)
            S.dma("sp", lambda e: e.dma_start(out=v, in_=srcap), wres(key, l, r0, r0 + nk * 128, c0, c0 + ncol),
                  ["wbuf%d" % i])
            return v, "wbuf%d" % i

        def rmsnorm_to_h(goff):
            bk = nextbank()
            for c in range(16):
                s = sqb[c % 2]
                ACT(s, xres[:, c, :], AF.Square, [("x", c)], ["sqb%d" % (c % 2)])
                MM(psb[bk], ones_b, s, c == 0, c == 15, ["sqb%d" % (c % 2), "ones_b"], ["ps%d" % bk])
            ACT(tf[0], psb[bk], AF.Sqrt, ["ps%d" % bk, "negpi"], ["tf0"], bias=epsr, scale=1.0)
            S.op("dve", lambda e: e.reciprocal(out=tf[0], in_=tf[0]), ["tf0"], ["tf0"])
            return goff

        def dump(ap_src, R):
            pass

        last_out_tok = []
        for ti in range(n_tiles):
            t0 = ti * NT
            S.dma("sp", lambda e, t0=t0: e.dma_start(out=xres, in_=xT[:, t0:t0 + NT].rearrange("(c p) n -> p c n", p=128)),
                  [], [("x", c) for c in range(16)])
            for l in range(L):
                g0 = l * 32
                if ti == 0:
                    pace_cast(len(pending_cast))
                    if l + 1 < L:
                        pending_cast.extend(cast_items(l + 1))
                S.mute = kcut < 1
                S.tag = "ph1"
                rmsnorm_to_h(g0)
                for c in range(16):
                    STT(hbuf[:, c, :], xres[:, c, :], GS[:, g0 + c:g0 + c + 1], tf[0], ALU.mult, ALU.mult,
                        [("x", c), "gs", "tf0"], [("h", c)])
                HR = [("h", c) for c in range(16)]
                S.mute = kcut < 2
                S.tag = "ph2"
                for s in (2, 3, 0, 1):
                    wv, wr = load_slab("in", l, 0, 16, s * 512, 512)
                    bks = [nextbank() for _ in range(4)]
                    if s == 2:
                        for kc in range(16):
                            for mi in range(4):
                                MM(psb[bks[mi]], wv[:, kc, mi * 128:(mi + 1) * 128], hbuf[:, kc, :], kc == 0, kc == 15,
                                   [wr, ("h", kc)], ["ps%d" % bks[mi]], sig=(kc == 15))
                    for mi in range(4):
                        bk = bks[mi]
                        for kc in range(16 if s != 2 else 0):
                            MM(psb[bk], wv[:, kc, mi * 128:(mi + 1) * 128], hbuf[:, kc, :], kc == 0, kc == 15,
                               [wr, ("h", kc)], ["ps%d" % bk], sig=(kc == 15))
                        cc = (s % 2) * 4 + mi
                        if s >= 2:
                            ACT(vconv[:, cc, :], psb[bk], AF.Sigmoid, ["ps%d" % bk], [("v", cc)])
                        else:
                            TT("dve", uc[:, cc, HIST:HIST + NT], psb[bk], vconv[:, cc, :], ALU.mult,
                               ["ps%d" % bk, ("v", cc)], [("R", "uc", cc)])
                S.mute = kcut < 3
                S.tag = "ph3"
                for s in (4, 5):
                    wv, wr = load_slab("in", l, 0, 16, s * 512, 512)
                    for mi in range(4):
                        bk = nextbank()
                        for kc in range(16):
                            MM(psb[bk], wv[:, kc, mi * 128:(mi + 1) * 128], hbuf[:, kc, :], kc == 0, kc == 15,
                               [wr, ("h", kc)], ["ps%d" % bk], sig=(kc == 15))
                        q = (s - 4) * 4 + mi
                        CP("act", ubuf[:, q, :].rearrange("p (j c) -> p c j", j=8), psb[bk].rearrange("p (c j) -> p c j", j=8), ["ps%d" % bk], [("u", q)])
                S.mute = kcut < 4
                S.tag = "ph4"
                CP("dve", uc[:, :, 0:HIST], HI[:, l, :, :], ["HI%d" % l], [("R", "uc", cc) for cc in range(8)])
                for s4 in range(4):
                    i = slab_i[0] % 3
                    slab_i[0] += 1
                    dgv = wbufs[i][:, 0:2 * KW * 128].rearrange("p (c k n) -> p c k n", c=2, k=KW)
                    S.dma("sp", lambda e, l=l, s4=s4, i=i: e.dma_start(out=wbufs[i][:, 0:2 * KW * 128], in_=dgm[l, :, s4, :]),
                          ["dgm%d" % l], ["wbuf%d" % i])
                    for c2 in range(2):
                        cc = 2 * s4 + c2
                        bk = nextbank()
                        for k in range(KW):
                            MM(psb[bk], dgv[:, c2, k, :], uc[:, cc, k:k + NT], k == 0, k == KW - 1,
                               ["wbuf%d" % i, ("R", "uc", cc)], ["ps%d" % bk], sig=(k == KW - 1))
                        ACT(convo[:, cc, :], psb[bk], AF.Identity, ["ps%d" % bk, "pl"], [("R", "cv", cc)],
                            bias=PL[:, l, O_BDW + cc:O_BDW + cc + 1], scale=1.0)
                CP("dve", HI[:, l, :, :], uc[:, :, NT:NT + HIST], [("R", "uc", cc) for cc in range(8)], ["HI%d" % l])
                S.mute = kcut < 5
                S.tag = "ph5"
                bm, bs = nextbank(), nextbank()
                for cc in range(8):
                    MM(psb[bm], ones_f, convo[:, cc, :], cc == 0, cc == 7, [("R", "cv", cc), "cst"], ["ps%d" % bm])
                for cc in range(8):
                    s = sqf[cc % 2]
                    ACT(s, convo[:, cc, :], AF.Square, [("R", "cv", cc)], ["tf%d" % (2 + cc % 2)])
                    MM(psb[bs], ones_f, s, cc == 0, cc == 7, ["tf%d" % (2 + cc % 2), "cst"], ["ps%d" % bs])
                TS("dve", tf[1], psb[bm], 1.0 / CWID, None, ALU.mult, None, ["ps%d" % bm], ["tf1"])
                TT("dve", tf[2], tf[1], tf[1], ALU.mult, ["tf1"], ["tf2"])
                STT(tf[3], psb[bs], 1.0 / CWID, tf[2], ALU.mult, ALU.subtract, ["ps%d" % bs, "tf2"], ["tf3"])
                ACT(tf[2], tf[3], AF.Sqrt, ["tf3", "negpi"], ["tf2"], bias=epsl, scale=1.0)
                S.op("dve", lambda e: e.reciprocal(out=tf[2], in_=tf[2]), ["tf2"], ["tf2"])
                STT(tf[3], tf[1], -1.0, tf[2], ALU.mult, ALU.mult, ["tf1", "tf2"], ["tf3"])
                for cc in range(8):
                    TT("dve", convo[:, cc, :], convo[:, cc, :], tf[2], ALU.mult, [("R", "cv", cc), "tf2"], [("R", "cv", cc)])
                    TT("dve", convo[:, cc, :], convo[:, cc, :], tf[3], ALU.add, [("R", "cv", cc), "tf3"], [("R", "cv", cc)])
                    ACT(vconv[:, cc, :], convo[:, cc, :], AF.Silu, [("R", "cv", cc), "pl"], [("v", cc)],
                        bias=PL[:, l, O_LNB + cc:O_LNB + cc + 1], scale=PL[:, l, O_LNG + cc:O_LNG + cc + 1])
                S.mute = kcut < 6
                S.tag = "ph6"
                RZ = [("R", "uc", c) for c in range(8)] + [("R", "cv", c) for c in range(8)]
                for q in range(8):
                    zb = zibuf[q % 2]
                    S.dma("sp", lambda e, l=l, q=q, zb=zb: e.dma_start(out=zb.rearrange("p j t n -> p (j t n)"), in_=zinm[l, :, q, :]),
                          ["zinm%d" % l], ["zib%d" % (q % 2)])
                    uq = ubuf[:, q, :].rearrange("p (j c) -> p j c", j=8)
                    um = ubm[q % 2]
                    for r in range(4):
                        CP("act", um[32 * r:32 * r + 32, :, r, :], uq[32 * r:32 * r + 32, :, :], [("u", q), "ubm0"], ["ubm_%d_%d" % (q % 2, r)])
                    bk = 4 + (q % 2)
                    zv = psb[bk].rearrange("p (t r c) -> p t r c", t=2, r=4)
                    for t_ in range(2):
                        for j in range(8):
                            MM(psb[bk][:, t_ * 256:(t_ + 1) * 256], zb[:, j, t_, :], um[:, j, :, :].rearrange("p r c -> p (r c)"),
                               j == 0, j == 7, ["zib%d" % (q % 2)] + ["ubm_%d_%d" % (q % 2, r) for r in range(4)], ["ps%d" % bk],
                               sig=(j == 7))
                    for t_ in range(2):
                        first = (q == 0 and t_ == 0)
                        CP("act" if t_ == 0 else "dve", Z[:, t_, 4 * q:4 * q + 4, 1:65], zv[:, t_, :, :],
                           ["ps%d" % bk] + (RZ if first else []), ["Z"] + (RZ if first else []))
                S.mute = kcut < 7
                S.tag = "ph7"
                S.dma("sp", lambda e, l=l: e.dma_start(out=vf(a_pw, 1024), in_=pwtab[l]), ["pwtab%d" % l], ["PWb"])
                CP("dve", Z[:, :, :, 0], ST[:, l, :].rearrange("p (t g) -> p t g", t=2), ["ST%d" % l, "Z"], ["Z"])

                def cstep(dst, src, src_sw, Aj, Bj, extraR=()):
                    TT("dve", sc1, Aj, src, ALU.mult, ["Z", "PWb"], ["sc1"])
                    TT("dve", sc2, Bj, src_sw, ALU.mult, ["Z", "PWb"], ["sc2"])
                    TT("dve", dst, dst, sc1, ALU.add, ["Z", "sc1"], ["Z"])
                    TT("dve", dst, dst, sc2, ALU.add, ["Z", "sc2"], ["Z"])

                def AB(j_):
                    return (PWb[:, j_, 0, :].rearrange("p (t g) -> p t g", t=2), PWb[:, j_, 1, :].rearrange("p (t g) -> p t g", t=2))
                A1, B1 = AB(0)
                A8, B8 = AB(7)
                cstep(Z[:, :, :, 1], Z[:, :, :, 0], Z[:, ::-1, :, 0], A1, B1)
                tA = tf[2].rearrange("p (t g k) -> p t g k", t=2, g=32)
                tB = tf[3].rearrange("p (t g k) -> p t g k", t=2, g=32)
                A1b = A1.unsqueeze(3).to_broadcast([128, 2, 32, 8])
                B1b = B1.unsqueeze(3).to_broadcast([128, 2, 32, 8])
                for j_ in range(1, 8):
                    prev = Z[:, :, :, j_:j_ + 57:8]
                    prev_sw = Z[:, ::-1, :, j_:j_ + 57:8]
                    cur_ = Z[:, :, :, j_ + 1:j_ + 58:8]
                    TT("dve", tA, A1b, prev, ALU.mult, ["Z", "PWb"], ["tf2"])
                    TT("dve", tB, B1b, prev_sw, ALU.mult, ["Z", "PWb"], ["tf3"])
                    TT("dve", cur_, cur_, tA, ALU.add, ["Z", "tf2"], ["Z"])
                    TT("dve", cur_, cur_, tB, ALU.add, ["Z", "tf3"], ["Z"])
                for k_ in range(1, 8):
                    cstep(Z[:, :, :, 8 * k_ + 8], Z[:, :, :, 8 * k_], Z[:, ::-1, :, 8 * k_], A8, B8)
                big = vf(a_t[0], 2048)
                for to in range(2):
                    zt = Z[:, to, :, 9:65].rearrange("p g (k j) -> p g k j", j=8)[:, :, :, 0:7]
                    Tsame = Z[:, to, :, 8:64:8].unsqueeze(3).to_broadcast([128, 32, 7, 7])
                    Tother = Z[:, 1 - to, :, 8:64:8].unsqueeze(3).to_broadcast([128, 32, 7, 7])
                    Aj = PWb[:, 0:7, 0, to * 32:to * 32 + 32].rearrange("p j g -> p g j").unsqueeze(2).to_broadcast([128, 32, 7, 7])
                    Bj = PWb[:, 0:7, 1, to * 32:to * 32 + 32].rearrange("p j g -> p g j").unsqueeze(2).to_broadcast([128, 32, 7, 7])
                    t1 = big[:, 0:1568].rearrange("p (g k j) -> p g k j", g=32, k=7)
                    TFN = ["tf0", "tf1", "tf2", "tf3"]
                    TT("dve", t1, Aj, Tsame, ALU.mult, ["Z", "PWb"] + TFN, TFN)
                    TT("dve", zt, zt, t1, ALU.add, ["Z"] + TFN, ["Z"])
                    TT("dve", t1, Bj, Tother, ALU.mult, ["Z", "PWb"] + TFN, TFN)
                    TT("dve", zt, zt, t1, ALU.add, ["Z"] + TFN, ["Z"])
                CP("dve", ST[:, l, :].rearrange("p (t g) -> p t g", t=2), Z[:, :, :, 64], ["Z"], ["ST%d" % l])
                CP("act", zbf, Z[:, :, :, 0:64], ["Z"], ["zbf"])
                S.mute = kcut < 8
                S.tag = "ph8"
                for q in range(8):
                    lb = lcb[q % 2]
                    S.dma("sp", lambda e, l=l, q=q, lb=lb: e.dma_start(out=lb, in_=lcm[l, :, q, :]), ["lcm%d" % l], ["lcb%d" % (q % 2)])
                    bk = nextbank()
                    yp = psb[bk]
                    ypv = yp.rearrange("p (j c) -> p j c", j=8)
                    MM(yp, lb[:, 0:128], ubuf[:, q, :], True, False, ["lcb%d" % (q % 2), ("u", q)], ["ps%d" % bk], sig=False)
                    for k in range(1, 8):
                        MM(yp[:, k * 64:512], lb[:, k * 128:(k + 1) * 128], ubuf[:, q, 0:(8 - k) * 64], False, k == 7,
                           ["lcb%d" % (q % 2), ("u", q)], ["ps%d" % bk], sig=(k == 7))
                    PC = psbig[:, 2048:4096].rearrange("p (j r c) -> p j r c", j=8, r=4)
                    for j in range(8):
                        for t_ in range(2):
                            o = 1024 + j * 256 + t_ * 128
                            MM(psbig[:, 2048 + j * 256:2048 + (j + 1) * 256], lb[:, o:o + 128],
                               zbf[:, t_, 4 * q:4 * q + 4, :].rearrange("p g c -> p (g c)"), t_ == 0, t_ == 1,
                               ["lcb%d" % (q % 2), "zbf"], ["ps%d" % (4 + j // 2)], sig=(t_ == 1))
                    yc = tf[2 + q % 2]
                    ycn = "tf%d" % (2 + q % 2)
                    ycv = yc.rearrange("p (j c) -> p j c", j=8)
                    for r in range(4):
                        CP("act", ycv[32 * r:32 * r + 32, :, :], PC[32 * r:32 * r + 32, :, r, :],
                           ["ps4", "ps5", "ps6", "ps7"], [ycn + "_%d" % r, ycn] if r == 0 else [ycn + "_%d" % r])
                    TT("dve", yc, yp, yc, ALU.add, ["ps%d" % bk, ycn] + [ycn + "_%d" % r for r in range(4)], [ycn])
                    ACT(tf[0], yc, AF.Square, [ycn], ["tf0"])
                    TS("dve", tf[0], tf[0], 0.044715, 1.0, ALU.mult, ALU.add, ["tf0"], ["tf0"])
                    TT("dve", tf[0], tf[0], yc, ALU.mult, ["tf0", ycn], ["tf0"])
                    ACT(tf[1], tf[0], AF.Sigmoid, ["tf0"], ["tf1"], scale=2.0 * math.sqrt(2.0 / math.pi))
                    TT("dve", yg[:, q, :].rearrange("p (c j) -> p c j", j=8), tf[1].rearrange("p (j c) -> p c j", j=8), yc.rearrange("p (j c) -> p c j", j=8), ALU.mult, ["tf1", ycn], [("yg", q)])
                S.mute = kcut < 9
                S.tag = "ph9"
                wv, wr = load_slab("gl", l, 0, 8, 0, 1024)
                for mt in range(8):
                    bk = nextbank()
                    for kc in range(8):
                        MM(psb[bk], wv[:, kc, mt * 128:(mt + 1) * 128], yg[:, kc, :], kc == 0, kc == 7,
                           [wr, ("yg", kc)], ["ps%d" % bk], sig=(kc == 7))
                    tbi = mt % 3
                    ACT(tb[tbi], psb[bk], AF.Sigmoid, ["ps%d" % bk], ["tb%d" % tbi])
                    TT("dve", ubuf[:, mt, :], yg[:, mt, :], tb[tbi], ALU.mult, [("yg", mt), "tb%d" % tbi], [("u", mt)])
                S.mute = kcut < 10
                S.tag = "ph10"
                RY = [("yg", q) for q in range(8)] + ["Z", "zbf"]
                for grp in range(4):
                    gv, gr = load_slab("in", l, 0, 16, 3072 + grp * 512, 512)
                    wv, wr = load_slab("co", l, 0, 8, grp * 512, 512)
                    for mi in range(4):
                        bkg = nextbank()
                        for kc in range(16):
                            MM(psb[bkg], gv[:, kc, mi * 128:(mi + 1) * 128], hbuf[:, kc, :], kc == 0, kc == 15,
                               [gr, ("h", kc)], ["ps%d" % bkg], sig=(kc == 15))
                        ACT(tb[mi], psb[bkg], AF.Sigmoid, ["ps%d" % bkg], ["tb%d" % mi])
                    for mi in range(4):
                        mt = grp * 4 + mi
                        bk = nextbank()
                        for kc in range(8):
                            MM(psb[bk], wv[:, kc, mi * 128:(mi + 1) * 128], vconv[:, kc, :], kc == 0, kc == 7,
                               [wr, ("v", kc)], ["ps%d" % bk], sig=(kc == 7))
                        extra = RY + RZ if mt == 0 else []
                        TT("dve", mbuf[:, mt, :], psb[bk], tb[mi], ALU.mult, ["ps%d" % bk, "tb%d" % mi] + extra, [("m", mt)] + extra)
                S.mute = kcut < 11
                S.tag = "ph11"
                for grp in range(4):
                    gv, gr = load_slab("in", l, 0, 16, 5120 + grp * 512, 512)
                    wv, wr = load_slab("so", l, 0, 8, grp * 512, 512)
                    for mi in range(4):
                        bkg = nextbank()
                        for kc in range(16):
                            MM(psb[bkg], gv[:, kc, mi * 128:(mi + 1) * 128], hbuf[:, kc, :], kc == 0, kc == 15,
                               [gr, ("h", kc)], ["ps%d" % bkg], sig=(kc == 15))
                        ACT(tb[mi], psb[bkg], AF.Sigmoid, ["ps%d" % bkg], ["tb%d" % mi])
                    for mi in range(4):
                        mt = grp * 4 + mi
                        bk = nextbank()
                        for kc in range(8):
                            MM(psb[bk], wv[:, kc, mi * 128:(mi + 1) * 128], ubuf[:, kc, :], kc == 0, kc == 7,
                               [wr, ("u", kc)], ["ps%d" % bk], sig=(kc == 7))
                        TT("dve", tf[mt % 2], psb[bk], tb[mi], ALU.mult, ["ps%d" % bk, "tb%d" % mi], ["tf%d" % (mt % 2)])
                        TT("dve", mbuf[:, mt, :], mbuf[:, mt, :], tf[mt % 2], ALU.add, [("m", mt), "tf%d" % (mt % 2)], [("m", mt)])
                S.mute = kcut < 12
                S.tag = "ph12"
                for s in range(4):
                    wv, wr = load_slab("o", l, 0, 16, s * 512, 512)
                    for mi in range(4):
                        mt = s * 4 + mi
                        bk = nextbank()
                        for kc in range(16):
                            MM(psb[bk], wv[:, kc, mi * 128:(mi + 1) * 128], mbuf[:, kc, :], kc == 0, kc == 15,
                               [wr, ("m", kc)], ["ps%d" % bk], sig=(kc == 15))
                        TT("dve", xres[:, mt, :], xres[:, mt, :], psb[bk], ALU.add, [("x", mt), "ps%d" % bk], [("x", mt)])
                S.mute = kcut < 13
                S.tag = "ph13"
                rmsnorm_to_h(g0 + 16)
                for c in range(16):
                    STT(hbuf[:, c, :], xres[:, c, :], GS[:, g0 + 16 + c:g0 + 17 + c], tf[0], ALU.mult, ALU.mult,
                        [("x", c), "gs", "tf0"], [("h", c)])
                RM = [("m", c) for c in range(16)]
                for half in range(2):
                    for s in range(8):
                        wv, wr = load_slab("f1", l, 0, 16, half * 4096 + s * 512, 512)
                        bks = [nextbank() for _ in range(4)]
                        kco = (half == 0 and s == 0)
                        if kco:
                            for kc in range(16):
                                for mi in range(4):
                                    MM(psb[bks[mi]], wv[:, kc, mi * 128:(mi + 1) * 128], hbuf[:, kc, :], kc == 0, kc == 15,
                                       [wr, ("h", kc)], ["ps%d" % bks[mi]], sig=(kc == 15))
                        for mi in range(4):
                            bk = bks[mi]
                            for kc in range(0 if kco else 16):
                                MM(psb[bk], wv[:, kc, mi * 128:(mi + 1) * 128], hbuf[:, kc, :], kc == 0, kc == 15,
                                   [wr, ("h", kc)], ["ps%d" % bk], sig=(kc == 15))
                            ai = s * 4 + mi
                            tbi = ai % 3
                            ACT(tb[tbi], psb[bk], AF.Relu, ["ps%d" % bk], ["tb%d" % tbi])
                            extra = RM if (ai == 0) else []
                            TT("dve", abuf[:, ai, :], tb[tbi], tb[tbi], ALU.mult, ["tb%d" % tbi] + extra, [("a", ai)] + extra)
                    for cg in range(4):
                        bset = 4 if cg % 2 == 0 else 0
                        for kq in range(2):
                            wv, wr = load_slab("f2", l, half * 4096 + kq * 2048, 16, cg * 512, 512)
                            for mi in range(4):
                                bk = bset + mi
                                for kc in range(16):
                                    st = (kq == 0 and kc == 0)
                                    sp_ = (kq == 1 and kc == 15)
                                    MM(psb[bk], wv[:, kc, mi * 128:(mi + 1) * 128], abuf[:, kq * 16 + kc, :], st, sp_,
                                       [wr, ("a", kq * 16 + kc)], ["ps%d" % bk], sig=(kc == 15))
                        for mi in range(4):
                            mt = cg * 4 + mi
                            bk = bset + mi
                            TT("dve", xres[:, mt, :], xres[:, mt, :], psb[bk], ALU.add, [("x", mt), "ps%d" % bk], [("x", mt)])
                RA = [("a", i) for i in range(32)]
                S.op("dve", lambda e: e.memset(sc1, 0.0), RA + ["sc1"], RA + RZ + ["sc1"])
            S.mute = False
            S.tag = "final"
            rmsnorm_to_h(L * 32)
            for c in range(16):
                STT(ostage[:, c, :], xres[:, c, :], GS[:, L * 32 + c:L * 32 + c + 1], tf[0], ALU.mult, ALU.mult,
                    [("x", c), "gs", "tf0"] + (RZ if c == 0 else []), [("os", c)] + (RZ if c == 0 else []))
            S.dma("sp", lambda e, t0=t0: e.dma_start(out=outT[:, t0:t0 + NT].rearrange("(c p) n -> p c n", p=128), in_=ostage),
                  [("os", c) for c in range(16)], ["outdma"] + RZ)
            last_out_tok.append(S.lastw["outdma"])
        S.final_wait("sp", last_out_tok)

        sems = {}
        for sk in sorted(S.semkeys, key=str):
            sems[sk] = es.enter_context(nc.semaphore("s_%s_%d" % (sk[0], sk[1])))
        with nc.Block() as block:
            @block.tensor
            def _(e):
                S.replay("pe", e, sems)

            @block.scalar
            def _(e):
                S.replay("act", e, sems)

            @block.vector
            def _(e):
                S.replay("dve", e, sems)

            @block.gpsimd
            def _(e):
                S.replay("pool", e, sems)

            @block.sync
            def _(e):
                S.replay("sp", e, sems)
    nc._sched = S
    return nc


def host_pack(inp):
    f = np.float32
    L = DEPTH
    pl = np.zeros((L, 128, NPL), f)
    pbk = np.zeros((L, 128, 4, 32, 16), f)
    wdl = np.zeros((L, 128, 8 * KW), f)

    def col(v, n):
        return np.ascontiguousarray(v.reshape(n, 128).T)

    for l in range(L):
        pl[l, :, O_GMIX:O_GMIX + 16] = col(inp["norm_mix"][l], 16)
        pl[l, :, O_GMLP:O_GMLP + 16] = col(inp["norm_mlp"][l], 16)
        pl[l, :, O_BDW:O_BDW + 8] = col(inp["b_dw"][l], 8)
        pl[l, :, O_LNG:O_LNG + 8] = col(inp["ln_g"][l], 8)
        pl[l, :, O_LNB:O_LNB + 8] = col(inp["ln_b"][l], 8)
        wdl[l] = inp["w_dw"][l].T.reshape(8, 128, KW).transpose(1, 0, 2).reshape(128, 248)
        pl[l, :, O_DSK:O_DSK + 8] = col(inp["d_skip"][l], 8)

        def gp(a):
            return a.reshape(32, 2, 64).transpose(1, 2, 0).reshape(128, 32)
        pl[l, :, O_ARE:O_ARE + 32] = gp(inp["a_re"][l])
        pl[l, :, O_AIM:O_AIM + 32] = gp(inp["a_im"][l])
        pl[l, :, O_LDT:O_LDT + 32] = gp(np.repeat(inp["log_dt"][l][:, None], 64, axis=1))
        for i, nm in enumerate(("b_re", "b_im")):
            pbk[l, :, i] = inp[nm][l].reshape(32, 2, 64, 16).transpose(1, 2, 0, 3).reshape(128, 32, 16)
        for i, nm in enumerate(("c_re", "c_im")):
            pbk[l, :, 2 + i] = inp[nm][l].reshape(32, 2, 16, 64).transpose(1, 3, 0, 2).reshape(128, 32, 16)
    cst = np.zeros((128, 784), f)
    cst[:, 0:128] = np.eye(128, dtype=f)
    cst[:, 128:256] = 1.0
    blk = np.kron(np.eye(8, dtype=f), np.ones((16, 16), f))
    cst[:, 256:768] = np.tile(blk, (1, 4))
    cst[:, 768:784] = col(inp["norm_final"], 16)
    return pl, pbk.reshape(L, 128, 2048), cst, wdl


def make_shared(inp):
    pl, pbk, cst, wdl = host_pack(inp)
    return {"w_in": inp["w_in"], "w_conv_out": inp["w_conv_out"], "w_glu": inp["w_glu"], "w_ssm_out": inp["w_ssm_out"],
            "w_out": inp["w_out"], "w_ff1": inp["w_ff1"], "w_ff2": inp["w_ff2"], "pl": pl, "pb": pbk, "cst": cst, "wd": wdl}


_NC_CACHE = {}


def kernel(**inputs):
    inp = {k: np.asarray(v) for k, v in inputs.items()}
    x = inp["x"]
    B = x.shape[0]
    shared = make_shared(inp)
    if "nc" not in _NC_CACHE:
        _NC_CACHE["nc"] = build_nc()
    nc = _NC_CACHE["nc"]
    in_maps = []
    for b in range(B):
        m = dict(shared)
        m["xT"] = np.ascontiguousarray(x[b].T)
        in_maps.append(m)
    res = run_bass_kernel_spmd(nc, in_maps, core_ids=list(range(B)))
    out = np.stack([np.ascontiguousarray(r["outT"].T) for r in res.results], axis=0)
    return out.astype(np.float32)
```
